# Optimizing a Trainium2 kernel written in Bass

```python
import jax, jax.numpy as jnp
from jax import lax
import numpy as np

D_MODEL = 1024
BATCH = 16
SEQ = 4096
DEPTH = 1

PLE_DIM = 256
D_MIX = D_MODEL
FOX_HEADS = 8
FOX_HEAD_DIM = 64
FOX_WIDTH = FOX_HEADS * FOX_HEAD_DIM
MLSTM_HEADS = 4
MLSTM_HEAD_DIM = 128
MLSTM_WIDTH = MLSTM_HEADS * MLSTM_HEAD_DIM
CONV_WIDTH = 4
D_FF = 4 * D_MODEL
Q_BLOCK = 128
CHUNK = 128
EPS = 1e-6

IN_SIZES = (FOX_WIDTH, FOX_WIDTH, FOX_WIDTH, FOX_HEADS,
            MLSTM_WIDTH, MLSTM_WIDTH, MLSTM_WIDTH, MLSTM_HEADS, MLSTM_HEADS, MLSTM_WIDTH)
IN_COLS = int(sum(IN_SIZES))
IN_SPLITS = tuple(int(s) for s in np.cumsum(IN_SIZES)[:-1])

kernel_name = "hymba_fox_mlstm_sqrelu_ple"


def rmsnorm(x, g):
    xf = x.astype(jnp.float32)
    y = xf * lax.rsqrt(jnp.mean(xf * xf, axis=-1, keepdims=True) + EPS)
    return (y * g.astype(jnp.float32)).astype(x.dtype)


def head_rmsnorm(h, g):
    B, S, H, d = h.shape
    hf = h.astype(jnp.float32)
    y = hf * lax.rsqrt(jnp.mean(hf * hf, axis=-1, keepdims=True) + EPS)
    return y.reshape(B, S, H * d) * g.astype(jnp.float32)


def causal_conv(u, w):
    K = w.shape[0]
    S = u.shape[1]
    up = jnp.pad(u, ((0, 0), (K - 1, 0), (0, 0)))
    out = up[:, 0:S] * w[0]
    for j in range(1, K):
        out = out + up[:, j:j + S] * w[j]
    return out


def fox_attention(q, k, v, f_pre):
    B, S, H, d = q.shape
    nb = S // Q_BLOCK
    log_f = jax.nn.log_sigmoid(f_pre.astype(jnp.float32))
    c = jnp.cumsum(log_f, axis=1).transpose(0, 2, 1)
    kh = k.transpose(0, 2, 1, 3)
    vh = v.transpose(0, 2, 1, 3)
    qb = q.reshape(B, nb, Q_BLOCK, H, d).transpose(1, 0, 3, 2, 4)
    cb = c.reshape(B, H, nb, Q_BLOCK).transpose(2, 0, 1, 3)
    kpos = jnp.arange(S)
    scale = d ** -0.5

    def block(args):
        qi, ci, i = args
        s = jnp.einsum('bhqd,bhkd->bhqk', qi, kh).astype(jnp.float32) * scale
        s = s + ci[..., :, None] - c[:, :, None, :]
        qpos = i * Q_BLOCK + jnp.arange(Q_BLOCK)
        s = jnp.where(kpos[None, :] <= qpos[:, None], s, -jnp.inf)
        pr = jax.nn.softmax(s, axis=-1).astype(vh.dtype)
        return jnp.einsum('bhqk,bhkd->bhqd', pr, vh)

    o = lax.map(block, (qb, cb, jnp.arange(nb)))
    return o.transpose(1, 0, 3, 2, 4).reshape(B, S, H, d)


def mlstm_chunkwise(q, k, v, i_pre, f_pre):
    B, S, H, d = q.shape
    L = CHUNK
    nc = S // L
    f32 = jnp.float32

    def to_chunks(a):
        a = a.reshape((B, nc, L, H) + a.shape[3:])
        return jnp.moveaxis(a, (1, 3), (0, 2))

    qc = to_chunks(q.astype(f32))
    kc = to_chunks(k.astype(f32) * (d ** -0.5))
    vc = to_chunks(v.astype(f32))
    ic = to_chunks(i_pre.astype(f32))
    fc = to_chunks(jax.nn.log_sigmoid(f_pre.astype(f32)))
    causal = jnp.tril(jnp.ones((L, L), dtype=bool))

    def step(carry, xs):
        C, n, m = carry
        qi, ki, vi, ii, fi = xs
        b = jnp.cumsum(fi, axis=-1)
        Dm = b[..., :, None] - b[..., None, :] + ii[..., None, :]
        Dm = jnp.where(causal, Dm, -jnp.inf)
        inter = b + m[..., None]
        mt = jnp.maximum(inter, jnp.max(Dm, axis=-1))
        w_inter = jnp.exp(inter - mt)
        w_intra = jnp.exp(Dm - mt[..., None])
        sqk = jnp.einsum('bhtd,bhsd->bhts', qi, ki) * w_intra
        num = (w_inter[..., None] * jnp.einsum('bhtd,bhde->bhte', qi, C)
               + jnp.einsum('bhts,bhse->bhte', sqk, vi))
        den = w_inter * jnp.einsum('bhtd,bhd->bht', qi, n) + jnp.sum(sqk, axis=-1)
        h = num / jnp.maximum(jnp.abs(den), jnp.exp(-mt))[..., None]
        bL = b[..., -1]
        g = bL[..., None] - b + ii
        m_new = jnp.maximum(bL + m, jnp.max(g, axis=-1))
        decay = jnp.exp(bL + m - m_new)
        wk = jnp.exp(g - m_new[..., None])[..., None] * ki
        C_new = decay[..., None, None] * C + jnp.einsum('bhsd,bhse->bhde', wk, vi)
        n_new = decay[..., None] * n + jnp.sum(wk, axis=2)
        return (C_new, n_new, m_new), h

    init = (jnp.zeros((B, H, d, d), f32), jnp.zeros((B, H, d), f32), jnp.zeros((B, H), f32))
    _, hs = lax.scan(step, init, (qc, kc, vc, ic, fc))
    hs = jnp.moveaxis(hs, (0, 2), (1, 3)).reshape(B, S, H, d)
    return hs


def setup_inputs(seed: int = 0) -> dict:
    key = jax.random.key(seed)
    ks = jax.random.split(key, 20)
    f32 = jnp.float32
    nrm = lambda k, shape, s: jax.random.normal(k, shape, f32) * s
    gain = lambda k, shape: 1.0 + 0.02 * jax.random.normal(k, shape, f32)
    return {
        "x": jax.random.normal(ks[0], (BATCH, SEQ, D_MODEL), f32),
        "p": jax.random.normal(ks[1], (DEPTH, BATCH, SEQ, PLE_DIM), f32),
        "w_in": nrm(ks[2], (DEPTH, D_MODEL, IN_COLS), D_MODEL ** -0.5),
        "b_fox_f": 3.0 + 0.5 * jax.random.normal(ks[3], (DEPTH, FOX_HEADS), f32),
        "b_mlstm_i": nrm(ks[4], (DEPTH, MLSTM_HEADS), 0.1),
        "b_mlstm_f": jnp.linspace(3.0, 6.0, MLSTM_HEADS, dtype=f32)[None, :]
                     + 0.1 * jax.random.normal(ks[5], (DEPTH, MLSTM_HEADS), f32),
        "w_conv": nrm(ks[6], (DEPTH, CONV_WIDTH, 2 * MLSTM_WIDTH), CONV_WIDTH ** -0.5),
        "g_mix": gain(ks[7], (DEPTH, D_MODEL)),
        "g_fox_out": gain(ks[8], (DEPTH, FOX_WIDTH)),
        "g_mlstm_out": gain(ks[9], (DEPTH, MLSTM_WIDTH)),
        "w_out": nrm(ks[10], (DEPTH, D_MIX, D_MODEL), D_MIX ** -0.5),
        "g_mlp": gain(ks[11], (DEPTH, D_MODEL)),
        "w_up": nrm(ks[12], (DEPTH, D_MODEL, D_FF), D_MODEL ** -0.5),
        "w_down": nrm(ks[13], (DEPTH, D_FF, D_MODEL), D_FF ** -0.5),
        "w_ple": nrm(ks[14], (DEPTH, PLE_DIM, D_MODEL), PLE_DIM ** -0.5),
        "g_ple": gain(ks[15], (DEPTH, D_MODEL)),
        "w_ple_gate": nrm(ks[16], (DEPTH, D_MODEL, D_MODEL), D_MODEL ** -0.5),
        "g_final": gain(ks[17], (D_MODEL,)),
    }


def reference(x, p, w_in, b_fox_f, b_mlstm_i, b_mlstm_f, w_conv, g_mix, g_fox_out,
              g_mlstm_out, w_out, g_mlp, w_up, w_down, w_ple, g_ple, w_ple_gate, g_final):
    B, S, _ = x.shape
    for i in range(DEPTH):
        h = rmsnorm(x, g_mix[i])
        z = h @ w_in[i]
        (fq, fk, fv, ff, mq, mk, mv, mi, mf, mo) = jnp.split(z, IN_SPLITS, axis=-1)

        fox = fox_attention(fq.reshape(B, S, FOX_HEADS, FOX_HEAD_DIM),
                            fk.reshape(B, S, FOX_HEADS, FOX_HEAD_DIM),
                            fv.reshape(B, S, FOX_HEADS, FOX_HEAD_DIM),
                            ff + b_fox_f[i])
        fox_out = head_rmsnorm(fox, g_fox_out[i])

        qk = jax.nn.silu(causal_conv(jnp.concatenate([mq, mk], axis=-1), w_conv[i]))
        mq_c, mk_c = jnp.split(qk, 2, axis=-1)
        ml = mlstm_chunkwise(mq_c.reshape(B, S, MLSTM_HEADS, MLSTM_HEAD_DIM),
                             mk_c.reshape(B, S, MLSTM_HEADS, MLSTM_HEAD_DIM),
                             mv.reshape(B, S, MLSTM_HEADS, MLSTM_HEAD_DIM),
                             mi + b_mlstm_i[i], mf + b_mlstm_f[i])
        ml_out = head_rmsnorm(ml, g_mlstm_out[i]) * jax.nn.sigmoid(mo.astype(jnp.float32))

        mix = jnp.concatenate([fox_out, ml_out], axis=-1).astype(x.dtype)
        x = x + mix @ w_out[i]

        hm = rmsnorm(x, g_mlp[i])
        x = x + jnp.square(jax.nn.relu(hm @ w_up[i])) @ w_down[i]

        gate = jax.nn.sigmoid(rmsnorm(x, g_ple[i]) @ w_ple_gate[i])
        x = x + gate * (p[i] @ w_ple[i])
    return rmsnorm(x, g_final)
```

```python
import heapq
import os
import sys
import types
import numpy as np
import concourse.bass as bass
import concourse.mybir as mybir
from concourse.bass_utils import run_bass_kernel_spmd
from contextlib import ExitStack

F32 = mybir.dt.float32
BF16 = mybir.dt.bfloat16
ALU = mybir.AluOpType
AF = mybir.ActivationFunctionType
AX = mybir.AxisListType

D = 1024
T = 512
NSUB = 4
FH = 8
MH = 4
INC = 3600
DFF = 4096
PLE = 256
EPS = 1e-6
NSLAB = 2
NPT = 3


class Res:
    __slots__ = ("name", "w", "r")

    def __init__(self, name):
        self.name = name
        self.w = None
        self.r = []


def _freeze(fn):
    if fn.__closure__ is None:
        return fn
    cells = tuple(types.CellType(c.cell_contents) for c in fn.__closure__)
    return types.FunctionType(fn.__code__, fn.__globals__, fn.__name__, fn.__defaults__, cells)


class _Probe:
    def __getattr__(self, name):
        def f(*a, **kw):
            out = kw.get("out", a[0] if a else None)
            return out, kw
        return f


def _free_elems(ap):
    n = 1
    for d in ap.shape[1:]:
        n *= int(d)
    return n


_DT_SIZE = {}


class KB:
    ENG = ("pe", "act", "dve", "pool", "sp")

    def __init__(self, nc, es):
        self.nc = nc
        self.es = es
        self.eng = {"pe": nc.tensor, "act": nc.scalar, "dve": nc.vector, "pool": nc.gpsimd, "sp": nc.sync}
        self.sem = {e: es.enter_context(nc.semaphore("s_" + e)) for e in self.ENG}
        self.ops = []
        self.dsem = {}

    def _deps(self, r, w):
        d = set()
        for x in r:
            if x.w is not None:
                d.add(x.w)
        for x in w:
            if x.w is not None:
                d.add(x.w)
            d.update(x.r)
        return d

    def _mark(self, oid, r, w):
        for x in r:
            x.r.append(oid)
        for x in w:
            x.w = oid
            x.r = []

    def _add(self, **kw):
        oid = len(self.ops)
        kw["id"] = oid
        kw["line"] = sys._getframe(2).f_lineno
        self.ops.append(kw)
        return oid

    def op(self, e, fn, r=(), w=(), f=None):
        if f is None:
            out, kw = fn(_Probe())
            f = _free_elems(out)
        base = {"act": 0.20, "dve": 0.12, "pool": 0.25}.get(e, 0.1)
        rate = {"act": 1400.0, "dve": 960.0, "pool": 500.0}.get(e, 1000.0)
        oid = self._add(eng=e, kind="op", fns=[_freeze(fn)], deps=self._deps(r, w), dur=base + f / rate, lat=0.0)
        self._mark(oid, r, w)
        return oid

    def pe(self, fns, r=(), w=(), n=None):
        dur = 0.0
        for fn in fns:
            out, kw = fn(_Probe())
            ni = _free_elems(out)
            passes = 4.0 if ("rhs" in kw and kw["rhs"].dtype == F32) else 1.0
            dur += 0.03 + passes * ni / 2200.0
        oid = self._add(eng="pe", kind="op", fns=[_freeze(fn) for fn in fns], deps=self._deps(r, w), dur=dur, lat=0.0)
        self._mark(oid, r, w)
        return oid

    def dma(self, q, stream, out, in_, r=(), w=(), nbytes=None):
        if stream not in self.dsem:
            self.dsem[stream] = self.es.enter_context(self.nc.semaphore("d_" + stream))
        if nbytes is None:
            nel = 1
            for d in out.shape:
                nel *= int(d)
            nbytes = nel * (2 if out.dtype == BF16 else 4)
        fn = lambda e, out=out, in_=in_: e.dma_start(out=out, in_=in_)
        oid = self._add(eng=q, kind="dma", fns=[fn], deps=self._deps(r, w), dur=(1.0 if q == "pool" else 0.06), lat=2.0 + nbytes / 300000.0, stream=stream)
        self._mark(oid, r, w)
        return oid

    def alias(self, old, new):
        ids = set()
        for o in old:
            if o.w is not None:
                ids.add(o.w)
            ids.update(o.r)
        for n in new:
            n.r = list(set(n.r) | ids)

    def finalize(self, final_wait_streams=()):
        ops = self.ops
        n = len(ops)
        succ = [[] for _ in range(n)]
        ndep = [0] * n
        for o in ops:
            o["deps"].discard(o["id"])
            ndep[o["id"]] = len(o["deps"])
            for d in o["deps"]:
                succ[d].append(o["id"])
        fin = [0.0] * n
        start = [0.0] * n
        efree = {e: 0.0 for e in self.ENG}
        ready_t = [0.0] * n
        heap = []
        for o in ops:
            if ndep[o["id"]] == 0:
                heapq.heappush(heap, (0.0, o["id"]))
        order = []
        while heap:
            est, oid = heapq.heappop(heap)
            o = ops[oid]
            real = max(ready_t[oid], efree[o["eng"]])
            if real > est + 1e-9:
                heapq.heappush(heap, (real, oid))
                continue
            start[oid] = real
            efree[o["eng"]] = real + o["dur"]
            fin[oid] = real + o["dur"] + o["lat"]
            order.append(oid)
            for s_ in succ[oid]:
                lat = 0.04 if ops[s_]["eng"] == o["eng"] else 0.18
                ready_t[s_] = max(ready_t[s_], fin[oid] + lat)
                ndep[s_] -= 1
                if ndep[s_] == 0:
                    heapq.heappush(heap, (max(ready_t[s_], efree[ops[s_]["eng"]]), s_))
        assert len(order) == n, (len(order), n)
        self.sim_time = max(fin) if n else 0.0
        if os.environ.get("KB_ANALYZE"):
            for eng_name in ("pe", "act", "dve"):
                prev_end = 0.0
                busy = 0.0
                blame = {}
                for oid in order:
                    o = ops[oid]
                    if o["eng"] != eng_name:
                        continue
                    gap = start[oid] - prev_end
                    if gap > 0.05 and o["deps"]:
                        d = max(o["deps"], key=lambda d_: fin[d_])
                        key = (ops[d]["eng"], ops[d]["line"], o["line"])
                        blame[key] = blame.get(key, 0.0) + gap
                    busy += o["dur"]
                    prev_end = start[oid] + o["dur"]
                print("ANALYZE %s busy %.0f us of %.0f (%.0f%%)" % (eng_name, busy, self.sim_time, 100 * busy / self.sim_time))
                for key, g in sorted(blame.items(), key=lambda kv: -kv[1])[:14]:
                    print("   idle %.0f us waiting for %s op@line %d (consumer line %d)" % (g, key[0], key[1], key[2]))
        cnt = {e: 0 for e in self.ENG}
        dcnt = {}
        tok = [None] * n
        seen = {e: {} for e in self.ENG}
        streams = {e: [] for e in self.ENG}
        nwait = 0
        for oid in order:
            o = ops[oid]
            e = o["eng"]
            need = {}
            for d in o["deps"]:
                key, val = tok[d]
                if key == "pe" and e == "pe" and ops[d]["kind"] == "op":
                    continue
                if need.get(key, 0) < val:
                    need[key] = val
            for key, val in need.items():
                if seen[e].get(key, 0) >= val:
                    continue
                semh = self.sem[key] if key in self.sem else self.dsem[key[4:]]
                self.eng[e].wait_ge(semh, val)
                seen[e][key] = val
                streams[e].append(("wait", key, val))
                nwait += 1
            ins = None
            for fn in o["fns"]:
                ins = fn(self.eng[e])
            if o["kind"] == "dma":
                st = o["stream"]
                dcnt[st] = dcnt.get(st, 0) + 1
                ins.then_inc(self.dsem[st], 16)
                tok[oid] = ("dma:" + st, 16 * dcnt[st])
                streams[e].append(("inc", "dma:" + st, 16))
            else:
                cnt[e] += 1
                ins.then_inc(self.sem[e], 1)
                tok[oid] = (e, cnt[e])
                streams[e].append(("inc", e, 1))
        for st in final_wait_streams:
            if st in dcnt:
                self.eng["pool"].wait_ge(self.dsem[st], 16 * dcnt[st])
                streams["pool"].append(("wait", "dma:" + st, 16 * dcnt[st]))
        semv = {}
        pos = {e: 0 for e in self.ENG}
        progress = True
        while progress:
            progress = False
            for e in self.ENG:
                st = streams[e]
                while pos[e] < len(st):
                    kind, key, val = st[pos[e]]
                    if kind == "wait":
                        if semv.get(key, 0) < val:
                            break
                    else:
                        semv[key] = semv.get(key, 0) + val
                    pos[e] += 1
                    progress = True
        for e in self.ENG:
            assert pos[e] == len(streams[e]), ("DEADLOCK in emitted program", e, pos[e], len(streams[e]), streams[e][pos[e]])
        self.cnt = cnt
        self.nwait = nwait
        self.nins = sum(len(o["fns"]) for o in ops)


def build_nc(NSEQ, S, dbg=None):
    nc = bass.Bass("TRN2", target_bir_lowering=False)
    NT = S // T
    NKT = S // 128
    di = lambda n, sh: nc.dram_tensor(n, sh, F32, kind="ExternalInput").ap()
    x_d = di("x", [NSEQ, S, D])
    p_d = di("p", [NSEQ, S, PLE])
    w_in_d = di("w_in", [D, INC])
    w_out_d = di("w_out", [D, D])
    w_up_d = di("w_up", [D, DFF])
    w_down_d = di("w_down", [DFF, D])
    w_ple_d = di("w_ple", [PLE, D])
    w_pg_d = di("w_pg", [D, D])
    wgate_d = di("wgate", [D, 16])
    gcol_d = di("gcol", [128, 32])
    gfox_d = di("gfox", [64, 8])
    gfin_d = di("gfin", [D])
    gbias_d = di("gbias", [16])
    wconv_d = di("wconv", [128, 8, 4])
    gmixrow_d = di("gmixrow", [1, D])
    wc3row_d = di("wc3row", [1, D])
    y_d = nc.dram_tensor("y", [NSEQ, S, D], F32, kind="ExternalOutput").ap()
    dbg_d = {}
    if dbg:
        for n, sh in dbg.items():
            dbg_d[n] = nc.dram_tensor("dbg_" + n, sh, F32, kind="ExternalOutput").ap()
    wb = lambda n, sh: nc.dram_tensor(n, sh, BF16, kind="Internal").ap()
    wb_in = wb("wb_in", [D, INC])
    wb_out = wb("wb_out", [D, D])
    wb_up = wb("wb_up", [D, DFF])
    wb_down = wb("wb_down", [DFF, D])
    wb_ple = wb("wb_ple", [PLE, D])
    wb_pg = wb("wb_pg", [D, D])

    es = ExitStack()
    with es:
        es.enter_context(nc.allow_low_precision("bf16 matmul operands / activations are intended (bf16-reference regime)"))
        k = KB(nc, es)
        SB = lambda n, sh, dt: es.enter_context(nc.sbuf_tensor("sb_" + n, sh, dt))
        Kc = SB("Kc", [128, FH, S], BF16)
        Vcf = SB("Vc", [128, NKT * FH * 65 + 64], BF16)
        Vc = Vcf[:, 0:NKT * FH * 65].rearrange("p (k h e) -> p k h e", k=NKT, h=FH)
        Cst = SB("Cst", [128, MH, 129], F32)
        Csb = SB("Csb", [128, MH, 129], BF16)
        BufA = SB("BufA", [128, 8192], BF16)
        hT = SB("hT", [128, 8, T], BF16)
        mixT = SB("mixT", [128, 8, T], BF16)
        slabs = [SB(f"slab{i}", [128, 8, 512], BF16) for i in range(NSLAB)]
        BufB = SB("BufB", [128, 8192], BF16)
        U2 = SB("U2", [128, 2560], F32)
        U3 = SB("U3", [128, 2080], F32)
        sgo = SB("sgo", [128, MH, T], BF16)
        pleW = SB("pleW", [128, 2, D], BF16)
        ginv = SB("ginv", [128, 4], F32)
        hml = SB("hml", [128, 3, NSUB, FH], BF16)
        ident_f = SB("ident_f", [128, 128], F32)
        ident_b = SB("ident_b", [128, 128], BF16)
        tri_f = SB("tri_f", [128, 128], F32)
        ones_f = SB("ones_f", [128, 128], F32)
        maskT = SB("maskT", [128, 128], BF16)
        Amat = SB("Amat", [128, 64], BF16)
        gfin = SB("gfin", [128, D], F32)
        gcol = SB("gcol", [128, 32], F32)
        gfox = SB("gfox", [64, 8], F32)
        gb = SB("gb", [128, 16], F32)
        wcv = SB("wcv", [128, 8, 4], F32)
        Wg = SB("Wg", [128, 8, 16], BF16)
        halo = SB("halo", [128, 8, 3], F32)
        tot = SB("tot", [1, 16], F32)
        Mt = SB("Mt", [4, 8], F32)
        DG = SB("DG", [4, 32], F32)
        A4 = SB("A4", [4, 4], F32)
        wdec = SB("wdec", [4, 4], F32)
        ms = SB("ms", [128, 8], F32)
        ms2 = SB("ms2", [128, 8], F32)
        h0col = SB("h0col", [128, 8], F32)
        s00 = SB("s00", [1, 4], F32)
        rstd2 = SB("rstd2", [128, 4], F32)
        sigb = SB("sigb", [128, 2, 512], F32)
        rstd = SB("rstd", [128, 8], F32)
        epsc = SB("epsc", [128, 1], F32)
        lnsc = SB("lnsc", [128, 1], F32)
        zg = SB("zg", [128, NSUB, 16], F32)
        e1 = SB("e1", [128, NSUB, 16], F32)
        lsp = SB("lsp", [128, NSUB, 16], F32)
        cpos = SB("cpos", [128, NSUB, 16], F32)
        r1t = SB("r1t", [128, 64], F32)
        alpha = SB("alpha", [128, NSUB, 4], F32)
        mbc = SB("mbc", [128, 32], F32)
        d1 = SB("d1", [128, NSUB, 4], F32)
        es_t = SB("es_t", [128, NSUB, 4], F32)
        clampv = SB("clampv", [128, NSUB, 4], F32)
        sm = SB("sm", [128, 2, 32], F32)
        ps = [es.enter_context(nc.psum_tensor(f"ps{i}", [128, 512], F32)) for i in range(8)]
        psr = [Res(f"ps{i}") for i in range(8)]
        psb = [ps[i][:].bitcast(BF16) for i in range(8)]

        def buf_views(X):
            return dict(
                R=X[:].bitcast(F32).rearrange("p (s d) -> p s d", s=NSUB),
                aT=X[:].rearrange("p (b c t) -> p b c t", b=2, c=8),
                QT=X[:, 0:4096].rearrange("p (h t) -> p h t", h=FH),
                qcT=X[:, 4096:6144].rearrange("p (h t) -> p h t", h=MH),
                kcT=X[:, 6144:8192].rearrange("p (h t) -> p h t", h=MH),
                rR=[Res(f"R{s_}") for s_ in range(NSUB)],
                rUa=[Res(f"U1a{h}") for h in range(FH)],
                rUb=[Res(f"U1b{h}") for h in range(8)],
            )

        bufs = [buf_views(BufA), buf_views(BufB)]
        R = bufs[0]["R"]
        r_R = bufs[0]["rR"]
        aT, QT, qcT, kcT = bufs[1]["aT"], bufs[1]["QT"], bufs[1]["qcT"], bufs[1]["kcT"]
        r_U1a, r_U1b = bufs[1]["rUa"], bufs[1]["rUb"]
        U2b = U2[:].bitcast(BF16)
        U3b = U3[:].bitcast(BF16)
        Pt = [U2b[:, i * 512:(i + 1) * 512] for i in range(NPT)]
        sqx = U2b[:, 1536:2048]
        lnr = U2[:, 1024:1536]
        cv_acc = [U2[:, 1536 + i * 512:1536 + (i + 1) * 512] for i in range(2)]
        junkF = U2b[:, 4096:5120]
        p32 = U2[:, 0:1024].rearrange("p (s d) -> p s d", s=NSUB)
        pbf = U2b[:, 2048:3072].rearrange("p (s d) -> p s d", s=NSUB)
        pT = U2b[:, 3072:4096].rearrange("p (c t) -> p c t", c=2)
        ktm = U3b[:, 0:2048].rearrange("p (s h d) -> p s h d", s=NSUB, h=MH)
        vx = U3b[:, 2048:2048 + NSUB * MH * 129].rearrange("p (s h d) -> p s h d", s=NSUB, h=MH)
        hn = U3b[:, 0:2048].rearrange("p (b d) -> p b d", b=2)
        EK = U3b[:, 2048:3072].rearrange("p (s h e) -> p s h e", s=NSUB, h=FH)
        EQ = U3b[:, 3072:4096].rearrange("p (s h e) -> p s h e", s=NSUB, h=FH)
        sig = [sigb[:, i, :] for i in range(2)]
        prod = [U3[:, 1024 + i * 512:1024 + (i + 1) * 512] for i in range(2)]
        Ssb = SB("Ssb", [128, 2, MH, 128], BF16)
        ybf = SB("ybf", [128, 2, MH, 128], BF16)
        print("sbuf bytes remaining:", nc.sbuf_bytes_remaining)

        r_Kc = [Res(f"Kc{h}") for h in range(FH)]
        r_Vc = Res("Vc")
        r_C = [Res(f"C{h}") for h in range(MH)]
        r_Csb = [Res(f"Csb{h}") for h in range(MH)]
        r_hT = [Res(f"hT{s}") for s in range(NSUB)]
        r_mix = [Res(f"mix{c}") for c in range(8)]
        r_slab = [Res(f"slab{i}") for i in range(NSLAB)]
        r_U2 = Res("U2")
        r_cvacc = [Res(f"cvacc{i}") for i in range(2)]
        r_Pt = [Res(f"Pt{i}") for i in range(NPT)]
        r_sqx = Res("sqx")
        r_lnr = Res("lnr")
        r_p = Res("pbufs")
        r_ktm = [Res(f"ktm{s}") for s in range(NSUB)]
        r_vx = [Res(f"vx{s}") for s in range(NSUB)]
        r_hn = [Res("hn0"), Res("hn1")]
        r_sig = [Res("sig0"), Res("sig1")]
        r_prod = [Res("prod0"), Res("prod1")]
        r_sgo = [Res(f"sgo{h}") for h in range(MH)]
        r_EK = Res("EK")
        r_EQ = Res("EQ")
        r_pleW = Res("pleW")
        r_s00 = Res("s00")
        r_h0 = Res("h0col")
        V_E = [r_EK, r_EQ]
        r_const = Res("const")
        r_halo = [Res(f"halo{c}") for c in range(8)]
        r_tot = Res("tot")
        r_Mt = Res("Mt")
        r_g = Res("gates")
        r_ms = Res("ms")
        r_msS = [Res(f"ms{i}") for i in range(NSUB)]
        r_ms2 = [Res(f"ms2_{i}") for i in range(NSUB)]
        r_sm = Res("sm")
        r_smj = [Res("smj0"), Res("smj1")]
        r_Ssb = [Res("Ssb0"), Res("Ssb1")]
        r_ybf = [Res("ybf0"), Res("ybf1")]
        r_w = {n: Res("w_" + n) for n in ["in", "out", "up", "down", "ple", "pg"]}
        V_att = r_Pt + [r_sqx, r_lnr]
        V_conv = r_cvacc
        V_ple2 = [r_p]
        V_hn = r_hn
        V_ml = r_ktm + r_vx
        V_ple3 = r_sig + r_prod

        bank_rr = {"mm": [0, [0, 1, 2]], "st": [0, [3, 4, 5]], "pv": [0, [6, 7]], "all": [0, [0, 1, 2, 3, 4, 5, 6, 7]]}

        def bank(cls):
            st = bank_rr[cls]
            b = st[1][st[0] % len(st[1])]
            st[0] += 1
            return b

        k.dma("pool", "xin", R[:], x_d[0, 0:T, :].rearrange("(j p) d -> p j d", p=128), w=r_R)
        k.dma("pool", "setup2", Wg[:], wgate_d.rearrange("(c p) n -> p c n", p=128), w=[r_const])
        k.dma("sp", "setup", gcol[:], gcol_d, w=[r_const])
        k.dma("sp", "setup", gfox[:], gfox_d, w=[r_const])
        k.dma("sp", "setup", gfin[:], gfin_d.partition_broadcast(128), w=[r_const])
        k.dma("sp", "setup", gb[:], gbias_d.partition_broadcast(128), w=[r_const])
        k.dma("sp", "setup", wcv[:], wconv_d, w=[r_const])
        k.op("dve", lambda e: e.memset(ident_f[:], 1.0), w=[r_const])
        k.op("pool", lambda e: e.affine_select(out=ident_f[:], in_=ident_f[:], pattern=[[-1, 128]], compare_op=ALU.is_equal,
                                               fill=0.0, base=0, channel_multiplier=1), r=[r_const], w=[r_const])
        k.op("dve", lambda e: e.memset(tri_f[:], 1.0), w=[r_const])
        k.op("pool", lambda e: e.affine_select(out=tri_f[:], in_=tri_f[:], pattern=[[1, 128]], compare_op=ALU.is_ge,
                                               fill=0.0, base=0, channel_multiplier=-1), r=[r_const], w=[r_const])
        k.op("dve", lambda e: e.memset(ones_f[:], 1.0), w=[r_const])
        k.op("dve", lambda e: e.tensor_copy(out=ident_b[:], in_=ident_f[:]), r=[r_const], w=[r_const])
        k.op("dve", lambda e: e.tensor_copy(out=maskT[:], in_=tri_f[:]), r=[r_const], w=[r_const])
        k.op("dve", lambda e: e.memset(Amat[0:64, :], 1.0 / 64), w=[r_const])
        k.op("dve", lambda e: e.memset(Amat[64:65, :], EPS), w=[r_const])
        k.op("dve", lambda e: e.memset(epsc[:], EPS), w=[r_const])
        k.op("dve", lambda e: e.memset(lnsc[:], float(-0.5 * np.log(128.0))), w=[r_const])
        k.op("dve", lambda e: e.reciprocal(out=ginv[:], in_=gcol[:, 28:32]), r=[r_const], w=[r_const])
        k.op("dve", lambda e: e.memset(Vcf[:], 0.0), w=[r_Vc])
        k.op("dve", lambda e: e.memset(Vc[:, :, :, 64:65], 1.0), r=[r_Vc], w=[r_Vc])

        slab_i = [0]

        r_wblk = {}

        def cast_blk(name, dst, src):
            r_wblk[name] = Res("wb_" + name)
            k.dma("pool", "wc_" + name, dst, src, w=[r_wblk[name]])

        for c0 in (0, 512, 1024, 1544, 2056, 2568, 3088):
            cast_blk(f"in{c0}", wb_in[:, c0:c0 + 512], w_in_d[:, c0:c0 + 512])
        for hf in range(2):
            cast_blk(f"out{hf}", wb_out[:, hf * 512:(hf + 1) * 512], w_out_d[:, hf * 512:(hf + 1) * 512])
        for qd in range(4):
            for sl in range(2):
                c0 = qd * 1024 + sl * 512
                cast_blk(f"up{c0}", wb_up[:, c0:c0 + 512], w_up_d[:, c0:c0 + 512])
            for hf in range(2):
                cast_blk(f"down{qd}_{hf}", wb_down[qd * 1024:(qd + 1) * 1024, hf * 512:(hf + 1) * 512], w_down_d[qd * 1024:(qd + 1) * 1024, hf * 512:(hf + 1) * 512])
        cast_blk("ple", wb_ple, w_ple_d)
        for hf in range(2):
            cast_blk(f"pg{hf}", wb_pg[:, hf * 512:(hf + 1) * 512], w_pg_d[:, hf * 512:(hf + 1) * 512])
        k.dma("sp", "setup3", pleW[:], wb_ple.rearrange("(c p) n -> p c n", p=128), r=[r_wblk["ple"]], w=[r_pleW])
        tile_specs = []
        for c0 in (0, 512, 1024, 1544, 2056, 2568, 3088):
            tile_specs.append((f"in{c0}", wb_in[:, c0:c0 + 512].rearrange("(c p) n -> p c n", p=128)))
        for hf in range(2):
            tile_specs.append((f"out{hf}", wb_out[:, hf * 512:(hf + 1) * 512].rearrange("(c p) n -> p c n", p=128)))
        def up_specs(qd):
            for sl in range(2):
                c0 = qd * 1024 + sl * 512
                tile_specs.append((f"up{c0}", wb_up[:, c0:c0 + 512].rearrange("(c p) n -> p c n", p=128)))

        def down_specs(qd):
            for hf in range(2):
                tile_specs.append((f"down{qd}_{hf}", wb_down[qd * 1024:(qd + 1) * 1024, hf * 512:(hf + 1) * 512].rearrange("(c p) n -> p c n", p=128)))

        up_specs(0)
        for qd in range(4):
            if qd + 1 < 4:
                up_specs(qd + 1)
            down_specs(qd)
        for hf in range(2):
            tile_specs.append((f"pg{hf}", wb_pg[:, hf * 512:(hf + 1) * 512].rearrange("(c p) n -> p c n", p=128)))
        slab_specs = tile_specs * (NSEQ * NT)
        slab_pos = [0]
        slab_ready = []
        slab_free = list(range(NSLAB))

        def slab_topup():
            while slab_free and slab_pos[0] < len(slab_specs):
                b = slab_free.pop(0)
                name, src = slab_specs[slab_pos[0]]
                slab_pos[0] += 1
                k.dma("sp", f"slab{b}", slabs[b][:], src, r=[r_wblk[name]], w=[r_slab[b]])
                slab_ready.append((b, name))

        def slab_acquire(name):
            slab_topup()
            b, nm = slab_ready.pop(0)
            assert nm == name, (nm, name)
            return b

        def slab_release(b):
            slab_free.append(b)
            slab_topup()

        def wslab(wap, c0, ncols=512):
            return wap[:, c0:c0 + ncols].rearrange("(c p) n -> p c n", p=128)

        dbg_done = set()

        def dump(name, ap, rlist):
            if name in dbg_d and name not in dbg_done:
                dbg_done.add(name)
                k.dma("pool", "dbg", dbg_d[name], ap, r=rlist)

        def norm_a(sub):
            b = sub % 2
            k.op("dve", lambda e: e.memset(ms[:, sub:sub + 1], 0.0), w=[r_msS[sub]])
            k.op("act", lambda e: e.activation(out=hn[:, b, :], in_=R[:, sub, :], func=AF.Square, scale=1.0 / 32, accum_out=ms[:, sub:sub + 1]),
                 r=[r_R[sub], r_msS[sub]], w=[r_hn[b], r_msS[sub]])
            k.op("act", lambda e: e.activation(out=ms[:, 4 + sub:5 + sub], in_=ms[:, sub:sub + 1], func=AF.Ln, bias=epsc[:]), r=[r_msS[sub], r_const], w=[r_msS[sub]])
            k.op("act", lambda e: e.activation(out=rstd[:, sub:sub + 1], in_=ms[:, 4 + sub:5 + sub], func=AF.Exp, scale=-0.5), r=[r_msS[sub]], w=[r_msS[sub]])
            k.op("dve", lambda e: e.tensor_scalar(out=hn[:, b, :], in0=R[:, sub, :], scalar1=rstd[:, sub:sub + 1], scalar2=None, op0=ALU.mult),
                 r=[r_R[sub], r_msS[sub]], w=[r_hn[b]])

        def norm_b(gidx, sub):
            b = sub % 2
            pb = bank("mm")
            k.pe([(lambda e, c=c: e.transpose(out=psb[pb][:, c * 128:(c + 1) * 128], in_=hn[:, b, c * 128:(c + 1) * 128], identity=ident_b[:]))
                  for c in range(8)], r=[r_hn[b], r_const], w=[psr[pb]])
            k.op("dve", lambda e: e.tensor_tensor(
                out=hT[:, :, sub * 128:(sub + 1) * 128], in0=psb[pb].rearrange("p (c t) -> p c t", c=8),
                in1=gcol[:, gidx * 8:(gidx + 1) * 8].unsqueeze(2).to_broadcast([128, 8, 128]), op=ALU.mult),
                 r=[psr[pb], r_const], w=[r_hT[sub]])

        def stage_then_norm(stage_fn, gidx, after_stage=None):
            stage_fn(0)
            stage_fn(1)
            norm_a(0)
            stage_fn(2)
            norm_a(1)
            norm_b(gidx, 0)
            stage_fn(3)
            if after_stage is not None:
                after_stage()
            norm_a(2)
            norm_b(gidx, 1)
            norm_a(3)
            norm_b(gidx, 2)
            norm_b(gidx, 3)


        def mm_feat(sb, c4, kchunks=8, rhs_fn=None):
            pb = bank("mm")
            k.pe([(lambda e, c=c, pb=pb: e.matmul(ps[pb][:], lhsT=slabs[sb][:, c, c4 * 128:(c4 + 1) * 128], rhs=hT[:, c, :],
                                                   start=(c == 0), stop=(c == kchunks - 1))) for c in range(kchunks)],
                 r=[r_slab[sb]] + r_hT, w=[psr[pb]])
            return pb

        def mm_tok(sb, sub, lhs, rl, kchunks=8, col0=0):
            pb = bank("mm")
            k.pe([(lambda e, c=c, pb=pb: e.matmul(ps[pb][:], lhsT=lhs[:, c, sub * 128:(sub + 1) * 128], rhs=slabs[sb][:, c, col0:col0 + 512],
                                                   start=(c == 0), stop=(c == kchunks - 1))) for c in range(kchunks)],
                 r=[r_slab[sb]] + rl, w=[psr[pb]])
            return pb

        def tok0_precise():
            rowA = U3[0:1, 0:1024]
            rowB = lnr[0:1, :]
            Wf = sigb[:].rearrange("p a (c n) -> p (a c) n", c=4)
            WS = V_hn + [r_lnr]
            for hf in range(2):
                k.dma("sp", "rows", rowB, gmixrow_d[0:1, hf * 512:(hf + 1) * 512], w=WS)
                k.op("dve", lambda e: e.scalar_tensor_tensor(out=rowA[:, hf * 512:(hf + 1) * 512], in0=R[0:1, 0, hf * 512:(hf + 1) * 512], scalar=rstd[0:1, 0:1], in1=rowB,
                                                             op0=ALU.mult, op1=ALU.mult), r=WS + [r_R[0], r_msS[0]], w=WS)
            hb_ = bank("mm")
            k.pe([(lambda e, c=c: e.matmul(ps[hb_][:, c:c + 1], lhsT=rowA[:, c * 128:(c + 1) * 128], rhs=ones_f[0:1, 0:1], start=True, stop=True)) for c in range(8)],
                 r=WS + [r_const], w=[psr[hb_]])
            k.op("dve", lambda e: e.tensor_copy(out=h0col[:], in_=ps[hb_][:, 0:8]), r=[psr[hb_]], w=[r_h0])
            zb = [bank("mm"), bank("mm")]
            for i in range(8):
                k.dma("sp", "wf", Wf, w_in_d[:, 1544 + i * 128:1544 + (i + 1) * 128].rearrange("(c p) n -> p c n", p=128), w=r_sig)
                k.pe([(lambda e, c=c: e.matmul(ps[zb[i // 4]][0:1, (i % 4) * 128:(i % 4 + 1) * 128], lhsT=h0col[:, c:c + 1], rhs=Wf[:, c, :], start=(c == 0), stop=(c == 7)))
                      for c in range(8)], r=r_sig + [r_h0], w=[psr[zb[i // 4]]])
            for hf in range(2):
                k.dma("sp", "rows", rowA[:, hf * 512:(hf + 1) * 512], wc3row_d[0:1, hf * 512:(hf + 1) * 512], w=WS)
                k.op("dve", lambda e: e.tensor_tensor(out=rowA[:, hf * 512:(hf + 1) * 512], in0=ps[zb[hf]][0:1, :], in1=rowA[:, hf * 512:(hf + 1) * 512], op=ALU.mult),
                     r=WS + [psr[zb[hf]]], w=WS)
                k.op("act", lambda e: e.activation(out=rowB, in_=rowA[:, hf * 512:(hf + 1) * 512], func=AF.Exp, scale=-1.0), r=WS, w=WS)
                k.op("act", lambda e: e.activation(out=rowB, in_=rowB, func=AF.Ln, bias=ones_f[0:1, 0:1]), r=WS + [r_const], w=WS)
                k.op("act", lambda e: e.activation(out=rowB, in_=rowB, func=AF.Exp, scale=-1.0), r=WS, w=WS)
                k.op("dve", lambda e: e.tensor_tensor(out=rowA[:, hf * 512:(hf + 1) * 512], in0=rowA[:, hf * 512:(hf + 1) * 512], in1=rowB, op=ALU.mult), r=WS, w=WS)
            k.op("dve", lambda e: e.tensor_tensor(out=rowB, in0=rowA[:, 0:512], in1=rowA[:, 512:1024], op=ALU.mult), r=WS, w=WS)
            k.op("dve", lambda e: e.tensor_reduce(out=s00[:], in_=rowB.rearrange("p (h d) -> p h d", h=MH), axis=AX.X, op=ALU.add), r=WS, w=[r_s00])

        for seq in range(NSEQ):
            for h in range(MH):
                k.op("pool", lambda e, h=h: e.memset(Cst[:, h, :], 0.0), w=[r_C[h]])
            k.op("pool", lambda e: e.memset(halo[:], 0.0), w=r_halo)
            k.op("pool", lambda e: e.memset(tot[:], 0.0), w=[r_tot])
            k.op("pool", lambda e: e.memset(Mt[:], 0.0), w=[r_Mt])
            for ti in range(NT):
                t0 = ti * T
                kt0 = t0 // 128
                nkt = kt0 + NSUB
                g_idx = seq * NT + ti
                bR, bU = bufs[g_idx % 2], bufs[(g_idx + 1) % 2]
                R, r_R = bR["R"], bR["rR"]
                aT, QT, qcT, kcT = bU["aT"], bU["QT"], bU["qcT"], bU["kcT"]
                r_U1a, r_U1b = bU["rUa"], bU["rUb"]
                k.alias(bU["rR"], r_U1a + r_U1b)
                k.alias(V_ple2, V_att + V_conv)
                k.alias(V_ml + V_hn, V_E)
                norm_a(0)
                norm_a(1)
                norm_b(0, 0)
                norm_a(2)
                norm_b(0, 1)
                norm_a(3)
                norm_b(0, 2)
                norm_b(0, 3)
                gbk = bank("mm")
                for sub in range(NSUB):
                    k.pe([(lambda e, c=c: e.matmul(ps[gbk][:, sub * 16:(sub + 1) * 16], lhsT=hT[:, c, sub * 128:(sub + 1) * 128], rhs=Wg[:, c, :],
                                                   start=(c == 0), stop=(c == 7))) for c in range(8)], r=[r_hT[sub], r_const], w=[psr[gbk]])
                k.op("dve", lambda e: e.tensor_tensor(out=zg[:], in0=ps[gbk][:, 0:64].rearrange("p (s g) -> p s g", s=NSUB),
                                                      in1=gb[:].unsqueeze(1).to_broadcast([128, NSUB, 16]), op=ALU.add), r=[psr[gbk], r_const], w=[r_g])
                k.op("act", lambda e: e.activation(out=e1[:], in_=zg[:], func=AF.Exp, scale=-1.0), r=[r_g], w=[r_g])
                k.op("act", lambda e: e.activation(out=lsp[:], in_=e1[:], func=AF.Ln, bias=ones_f[:, 0:1]), r=[r_g, r_const], w=[r_g])
                k.op("pool", lambda e: e.memset(EK[:], 0.0), w=[r_EK])
                k.op("pool", lambda e: e.memset(EQ[:], 0.0), w=[r_EQ])
                k.op("pool", lambda e: e.memset(EK[:, :, :, 0:3], 1.0), r=[r_EK], w=[r_EK])
                k.op("pool", lambda e: e.memset(EQ[:, :, :, 3:6], 1.0), r=[r_EQ], w=[r_EQ])
                sb = slab_acquire("in0")
                for j in range(4):
                    pb = mm_feat(sb, j)
                    k.op("act", lambda e: e.activation(out=QT[0:64, 2 * j, :], in_=ps[pb][0:64, :], func=AF.Copy, scale=0.125), r=[psr[pb]], w=[r_U1a[2 * j]])
                    k.op("dve", lambda e: e.tensor_scalar(out=QT[0:64, 2 * j + 1, :], in0=ps[pb][64:128, :], scalar1=0.125, scalar2=None, op0=ALU.mult),
                         r=[psr[pb]], w=[r_U1a[2 * j + 1]])
                slab_release(sb)
                cbk = bank("mm")
                for j in range(NSUB):
                    fns = [lambda e, j=j: e.matmul(ps[cbk][:, j * 16:(j + 1) * 16], lhsT=tri_f[:], rhs=lsp[:, j, :], start=True, stop=False)]
                    for i in range(j):
                        fns.append(lambda e, j=j, i=i: e.matmul(ps[cbk][:, j * 16:(j + 1) * 16], lhsT=ones_f[:], rhs=lsp[:, i, :], start=False, stop=False))
                    fns.append(lambda e, j=j: e.matmul(ps[cbk][:, j * 16:(j + 1) * 16], lhsT=ones_f[0:1, :], rhs=tot[0:1, :], start=False, stop=True))
                    k.pe(fns, r=[r_g, r_tot, r_const], w=[psr[cbk]])
                k.op("dve", lambda e: e.tensor_copy(out=cpos[:], in_=ps[cbk][:, 0:64].rearrange("p (s g) -> p s g", s=NSUB)), r=[psr[cbk]], w=[r_g])
                tbk = bank("mm")
                fns = [(lambda e, i=i: e.matmul(ps[tbk][0:1, 0:16], lhsT=ones_f[:, 0:1], rhs=lsp[:, i, :], start=(i == 0), stop=False)) for i in range(NSUB)]
                fns.append(lambda e: e.matmul(ps[tbk][0:1, 0:16], lhsT=ones_f[0:1, 0:1], rhs=tot[0:1, :], start=False, stop=True))
                k.pe(fns, r=[r_g, r_tot, r_const], w=[psr[tbk]])
                k.op("dve", lambda e: e.tensor_copy(out=tot[:], in_=ps[tbk][0:1, 0:16]), r=[psr[tbk]], w=[r_tot])
                sb = slab_acquire("in512")
                for j in range(4):
                    pb = mm_feat(sb, j)
                    k.op("act", lambda e: e.activation(out=Kc[0:64, 2 * j, t0:t0 + T], in_=ps[pb][0:64, :], func=AF.Copy), r=[psr[pb]], w=[r_Kc[2 * j]])
                    k.op("dve", lambda e: e.tensor_copy(out=Kc[0:64, 2 * j + 1, t0:t0 + T], in_=ps[pb][64:128, :]), r=[psr[pb]], w=[r_Kc[2 * j + 1]])
                slab_release(sb)
                cf = cpos[:, :, 0:8]
                r1v = r1t[:, 0:32].rearrange("p (s h) -> p s h", s=NSUB)
                r2v = r1t[:, 32:64].rearrange("p (s h) -> p s h", s=NSUB)
                k.op("dve", lambda e: e.tensor_copy(out=hml[:, 0], in_=cf), r=[r_g], w=[r_sm])
                k.op("dve", lambda e: e.tensor_tensor(out=r1v, in0=cf, in1=hml[:, 0], op=ALU.subtract), r=[r_g, r_sm], w=[r_sm])
                k.op("dve", lambda e: e.tensor_copy(out=hml[:, 1], in_=r1v), r=[r_sm], w=[r_sm])
                k.op("dve", lambda e: e.tensor_tensor(out=r2v, in0=r1v, in1=hml[:, 1], op=ALU.subtract), r=[r_sm], w=[r_sm])
                k.op("dve", lambda e: e.tensor_copy(out=hml[:, 2], in_=r2v), r=[r_sm], w=[r_sm])
                for i3 in range(3):
                    k.op("dve", lambda e: e.tensor_copy(out=EK[:, :, :, 3 + i3], in_=hml[:, i3]), r=[r_sm, r_EK], w=[r_EK])
                    k.op("dve", lambda e: e.tensor_scalar(out=EQ[:, :, :, i3], in0=hml[:, i3], scalar1=-1.0, scalar2=None, op0=ALU.mult), r=[r_sm, r_EQ], w=[r_EQ])
                k.op("dve", lambda e: e.tensor_tensor(out=alpha[:], in0=zg[:, :, 8:12], in1=cpos[:, :, 12:16], op=ALU.add), r=[r_g], w=[r_g])
                abk = bank("mm")
                k.pe([(lambda e, j=j: e.matmul(ps[abk][0:4, j * 128:(j + 1) * 128], lhsT=alpha[:, j, :], rhs=ident_f[:], start=True, stop=True)) for j in range(NSUB)],
                     r=[r_g, r_const], w=[psr[abk]])
                k.op("dve", lambda e: e.tensor_reduce(out=A4[:], in_=ps[abk][0:4, :].rearrange("p (s t) -> p s t", s=NSUB), axis=AX.X, op=ALU.max),
                     r=[psr[abk]], w=[r_Mt])
                for j in range(NSUB):
                    k.op("dve", lambda e: e.tensor_tensor(out=Mt[:, j + 1:j + 2], in0=Mt[:, j:j + 1], in1=A4[:, j:j + 1], op=ALU.max), r=[r_Mt], w=[r_Mt])
                k.op("dve", lambda e: e.tensor_tensor(out=wdec[:], in0=Mt[:, 0:4], in1=Mt[:, 1:5], op=ALU.subtract), r=[r_Mt], w=[r_Mt])
                k.op("act", lambda e: e.activation(out=wdec[:], in_=wdec[:], func=AF.Exp), r=[r_Mt], w=[r_Mt])
                k.op("dve", lambda e: e.tensor_tensor(out=DG[:, 0:16].rearrange("p (j h) -> p j h", j=NSUB), in0=Mt[:, 1:5].unsqueeze(2).to_broadcast([4, NSUB, 4]),
                                                      in1=ident_f[0:4, 0:4].unsqueeze(1).to_broadcast([4, NSUB, 4]), op=ALU.mult), r=[r_Mt, r_const], w=[r_Mt])
                k.op("dve", lambda e: e.tensor_tensor(out=DG[:, 16:32].rearrange("p (j h) -> p j h", j=NSUB), in0=wdec[:].unsqueeze(2).to_broadcast([4, NSUB, 4]),
                                                      in1=ident_f[0:4, 0:4].unsqueeze(1).to_broadcast([4, NSUB, 4]), op=ALU.mult), r=[r_Mt, r_const], w=[r_Mt])
                k.op("dve", lambda e: e.tensor_copy(out=Mt[:, 0:1], in_=Mt[:, 4:5]), r=[r_Mt], w=[r_Mt])
                bbk = bank("mm")
                k.pe([lambda e: e.matmul(ps[bbk][:, 0:32], lhsT=ones_f[0:4, :], rhs=DG[:], start=True, stop=True)], r=[r_Mt, r_const], w=[psr[bbk]])
                k.op("dve", lambda e: e.tensor_copy(out=mbc[:], in_=ps[bbk][:, 0:32]), r=[psr[bbk]], w=[r_g])
                mbM = mbc[:, 0:16].rearrange("p (j h) -> p j h", j=NSUB)
                k.op("dve", lambda e: e.tensor_tensor(out=d1[:], in0=alpha[:], in1=mbM, op=ALU.subtract), r=[r_g], w=[r_g])
                k.op("act", lambda e: e.activation(out=es_t[:], in_=d1[:], func=AF.Exp, bias=lnsc[:]), r=[r_g, r_const], w=[r_g])
                k.op("dve", lambda e: e.tensor_tensor(out=d1[:], in0=cpos[:, :, 12:16], in1=mbM, op=ALU.subtract), r=[r_g], w=[r_g])
                k.op("act", lambda e: e.activation(out=clampv[:], in_=d1[:], func=AF.Exp), r=[r_g], w=[r_g])
                sb = slab_acquire("in1024")
                for sub in range(NSUB):
                    pb = mm_tok(sb, sub, hT, [r_hT[sub]])
                    if sub % 2 == 0:
                        k.op("act", lambda e: e.activation(out=Vc[:, kt0 + sub, :, 0:64], in_=ps[pb][:].rearrange("p (h d) -> p h d", h=FH), func=AF.Copy), r=[psr[pb]], w=[r_Vc])
                    else:
                        k.op("dve", lambda e: e.tensor_copy(out=Vc[:, kt0 + sub, :, 0:64], in_=ps[pb][:].rearrange("p (h d) -> p h d", h=FH)), r=[psr[pb]], w=[r_Vc])
                slab_release(sb)
                exk = [bank("all") for _ in range(3)]
                exq = [bank("all") for _ in range(3)]
                hgroups = [(0, 3), (3, 3), (6, 2)]
                for EX, rEX, banks in ((EK, r_EK, exk), (EQ, r_EQ, exq)):
                    for gi, (h0, n) in enumerate(hgroups):
                        k.pe([(lambda e, sub=sub: e.matmul(ps[banks[gi]][0:n * 32, sub * 128:(sub + 1) * 128], lhsT=EX[:, sub, h0:h0 + n, :].rearrange("p h e -> p (h e)"),
                                                           rhs=ident_b[:], start=True, stop=True)) for sub in range(NSUB)], r=[rEX, r_const], w=[psr[banks[gi]]])
                for h in range(FH):
                    gi, row = h // 3, 32 * (h % 3)
                    k.op("dve", lambda e: e.tensor_copy(out=Kc[64:70, h, t0:t0 + T], in_=ps[exk[gi]][row:row + 6, :]), r=[psr[exk[gi]]], w=[r_Kc[h]])
                    k.op("act", lambda e: e.activation(out=QT[64:70, h, :], in_=ps[exq[gi]][row:row + 6, :], func=AF.Copy), r=[psr[exq[gi]]], w=[r_U1a[h]])

                if ti == 0:
                    tok0_precise()
                k.alias(V_hn, r_ktm)
                k.alias(V_E, r_vx)
                bg = []
                slab_of = {}

                def conv_a(which, col0, c4):
                    if c4 == 0:
                        slab_of[col0] = slab_acquire(f"in{col0}")
                    sbx = slab_of[col0]
                    ch = which * 4 + c4
                    pb = mm_feat(sbx, c4)
                    if c4 == 3:
                        slab_release(sbx)
                    cb = ch % 2
                    acc = cv_acc[cb]
                    k.op("dve", lambda e: e.tensor_scalar(out=acc, in0=ps[pb][:, 0:512], scalar1=wcv[:, ch, 3:4], scalar2=None, op0=ALU.mult),
                         r=[psr[pb], r_const], w=[r_cvacc[cb]])
                    for jj in (2, 1, 0):
                        sh = 3 - jj
                        k.op("dve", lambda e: e.scalar_tensor_tensor(out=acc[:, sh:512], in0=ps[pb][:, 0:512 - sh], scalar=wcv[:, ch, jj:jj + 1], in1=acc[:, sh:512],
                                                                     op0=ALU.mult, op1=ALU.add), r=[psr[pb], r_cvacc[cb], r_const], w=[r_cvacc[cb]])
                        k.op("dve", lambda e: e.scalar_tensor_tensor(out=acc[:, 0:sh], in0=halo[:, ch, 3 - sh:3], scalar=wcv[:, ch, jj:jj + 1], in1=acc[:, 0:sh],
                                                                     op0=ALU.mult, op1=ALU.add), r=[r_halo[ch], r_cvacc[cb], r_const], w=[r_cvacc[cb]])
                    k.op("dve", lambda e: e.tensor_copy(out=halo[:, ch, :], in_=ps[pb][:, 509:512]), r=[psr[pb]], w=[r_halo[ch]])

                def conv_b(which, dst, rr_off, c4):
                    ch = which * 4 + c4
                    cb = ch % 2
                    acc = cv_acc[cb]
                    etmp = Ssb[:, cb, :, :].rearrange("p h t -> p (h t)")
                    k.op("act", lambda e: e.activation(out=etmp, in_=acc, func=AF.Exp, scale=-1.0), r=[r_cvacc[cb]], w=[r_Ssb[cb]])
                    k.op("act", lambda e: e.activation(out=etmp, in_=etmp, func=AF.Ln, bias=ones_f[:, 0:1]), r=[r_Ssb[cb], r_const], w=[r_Ssb[cb]])
                    k.op("act", lambda e: e.activation(out=etmp, in_=etmp, func=AF.Exp, scale=-1.0), r=[r_Ssb[cb]], w=[r_Ssb[cb]])
                    k.op("dve", lambda e: e.tensor_tensor(out=dst[:, c4, :], in0=acc, in1=etmp, op=ALU.mult), r=[r_cvacc[cb], r_Ssb[cb]], w=[r_U1b[rr_off + c4]])

                def mv_unit(sub):
                    if sub == 0:
                        slab_of["mv"] = slab_acquire("in2568")
                    pb = mm_tok(slab_of["mv"], sub, hT, [r_hT[sub]])
                    if sub == NSUB - 1:
                        slab_release(slab_of["mv"])
                    k.op("pool", lambda e: e.memset(vx[:, sub, :, 128:129], 1.0), w=[r_vx[sub]])
                    k.op("dve", lambda e: e.tensor_copy(out=vx[:, sub, :, 0:128], in_=ps[pb][:].rearrange("p (h d) -> p h d", h=MH)), r=[psr[pb]], w=[r_vx[sub]])
                    k.op("dve", lambda e: e.tensor_tensor(out=vx[:, sub, :, :], in0=vx[:, sub, :, :], in1=es_t[:, sub, :].unsqueeze(2).to_broadcast([128, MH, 129]), op=ALU.mult),
                         r=[r_vx[sub], r_g], w=[r_vx[sub]])

                def mo_unit(c4):
                    if c4 == 0:
                        slab_of["mo"] = slab_acquire("in3088")
                    pb = mm_feat(slab_of["mo"], c4)
                    if c4 == 3:
                        slab_release(slab_of["mo"])
                    k.op("act", lambda e: e.activation(out=sgo[:, c4, :], in_=ps[pb][:], func=AF.Exp, scale=-1.0), r=[psr[pb]], w=[r_sgo[c4]])
                    k.op("act", lambda e: e.activation(out=sgo[:, c4, :], in_=sgo[:, c4, :], func=AF.Ln, bias=ones_f[:, 0:1]), r=[r_sgo[c4], r_const], w=[r_sgo[c4]])
                    k.op("act", lambda e: e.activation(out=sgo[:, c4, :], in_=sgo[:, c4, :], func=AF.Exp, scale=-1.0), r=[r_sgo[c4]], w=[r_sgo[c4]])
                    k.op("dve", lambda e: e.tensor_scalar(out=sgo[:, c4, :], in0=sgo[:, c4, :], scalar1=gcol[:, 28 + c4:29 + c4], scalar2=None, op0=ALU.mult),
                         r=[r_sgo[c4], r_const], w=[r_sgo[c4]])

                ua, ub_ = [], []
                for which, col0, dst, rr_off in ((0, 1544, qcT, 0), (1, 2056, kcT, 4)):
                    for c4 in range(4):
                        ua.append(lambda which=which, col0=col0, c4=c4: conv_a(which, col0, c4))
                        ub_.append(lambda which=which, dst=dst, rr_off=rr_off, c4=c4: conv_b(which, dst, rr_off, c4))
                bg.append(ua[0])
                for n in range(1, 8):
                    bg.append(ua[n])
                    bg.append(ub_[n - 1])
                bg.append(ub_[7])
                for sub in range(NSUB):
                    bg.append(lambda sub=sub: mv_unit(sub))
                post_units = [lambda c4=c4: mo_unit(c4) for c4 in range(4)]


                pairs = [(h, kt) for h in range(FH) for kt in range(nkt)]
                sbank = {}
                pvbank = {}

                def lo_of(kt):
                    return max(0, (kt - kt0) * 128)

                def emit_S(idx):
                    h, kt = pairs[idx]
                    b = bank("st")
                    sbank[idx] = b
                    lo = lo_of(kt)
                    k.pe([lambda e: e.matmul(ps[b][:, lo:T], lhsT=Kc[0:70, h, kt * 128:(kt + 1) * 128], rhs=QT[0:70, h, lo:T], start=True, stop=True)],
                         r=[r_Kc[h], r_U1a[h]], w=[psr[b]])

                def epilogue(h):
                    pvb = pvbank[h]
                    k.op("act", lambda e: e.activation(out=sqx[0:65, :], in_=ps[pvb][0:65, :], func=AF.Square), r=[psr[pvb]], w=[r_sqx])
                    rb = bank("mm")
                    k.pe([lambda e: e.matmul(ps[rb][0:64, :], lhsT=Amat[0:65, :], rhs=sqx[0:65, :], start=True, stop=True)], r=[r_sqx, r_const], w=[psr[rb]])
                    k.op("act", lambda e: e.activation(out=lnr[0:64, :], in_=ps[rb][0:64, :], func=AF.Ln), r=[psr[rb]], w=[r_lnr])
                    k.op("act", lambda e: e.activation(out=lnr[0:64, :], in_=lnr[0:64, :], func=AF.Exp, scale=-0.5), r=[r_lnr], w=[r_lnr])
                    po = (h % 2) * 64
                    k.op("dve", lambda e: e.scalar_tensor_tensor(out=mixT[po:po + 64, h // 2, :], in0=ps[pvb][0:64, :], scalar=gfox[0:64, h:h + 1], in1=lnr[0:64, :],
                                                                 op0=ALU.mult, op1=ALU.mult), r=[psr[pvb], r_lnr, r_const], w=[r_mix[h // 2]])

                def attn_range(p0, p1, with_bg):
                    for i_ in range(p0, min(p0 + 2, p1)):
                        emit_S(i_)
                    for idx in range(p0, p1):
                        h, kt = pairs[idx]
                        if idx + 2 < p1:
                            emit_S(idx + 2)
                        b = sbank[idx]
                        lo = lo_of(kt)
                        pi = idx % NPT
                        k.op("act", lambda e: e.activation(out=Pt[pi][:, lo:T], in_=ps[b][:, lo:T], func=AF.Exp), r=[psr[b]], w=[r_Pt[pi]])
                        if kt >= kt0:
                            k.op("dve", lambda e: e.tensor_tensor(out=Pt[pi][:, lo:lo + 128], in0=Pt[pi][:, lo:lo + 128], in1=maskT[:], op=ALU.mult),
                                 r=[r_Pt[pi], r_const], w=[r_Pt[pi]])
                        if kt == 0:
                            pvbank[h] = bank("pv")
                        pvb = pvbank[h]
                        vo = (kt * FH + h) * 65
                        k.pe([lambda e: e.matmul(ps[pvb][:, lo:T], lhsT=Vcf[:, vo:vo + 128], rhs=Pt[pi][:, lo:T], start=(kt == 0), stop=(kt == nkt - 1))],
                             r=[r_Pt[pi], r_Vc], w=[psr[pvb]])
                        if kt == nkt - 1:
                            epilogue(h)
                        if with_bg and bg and (idx + 1) % bg_stride == 0:
                            bg.pop(0)()

                p_mid = (FH // 2) * nkt
                bg_stride = max(1, p_mid // (len(bg) + 2))
                attn_range(0, p_mid, True)
                while bg:
                    bg.pop(0)()
                for u in post_units:
                    u()
                st1 = {}
                st2 = {}

                def ml_stage1(j):
                    js = slice(j * 128, (j + 1) * 128)
                    jb = j % 2
                    ktb = 0
                    k.pe([(lambda e, h=h: e.transpose(out=psb[ktb][:, h * 128:(h + 1) * 128], in_=kcT[:, h, js], identity=ident_b[:])) for h in range(MH)],
                         r=r_U1b[4:8] + [r_const], w=[psr[ktb]])
                    k.op("act", lambda e: e.activation(out=ktm[:, j, :, :], in_=psb[ktb][:, 0:512].rearrange("p (h d) -> p h d", h=MH), func=AF.Copy), r=[psr[ktb]], w=[r_ktm[j]])
                    stb = 1
                    k.pe([(lambda e, h=h: e.matmul(ps[stb][:, h * 128:(h + 1) * 128], lhsT=kcT[:, h, js], rhs=qcT[:, h, js], start=True, stop=True)) for h in range(MH)],
                         r=r_U1b, w=[psr[stb]])
                    k.op("dve", lambda e: e.tensor_tensor(out=Ssb[:, jb, :, :], in0=ps[stb][:].rearrange("p (h t) -> p h t", h=MH),
                                                          in1=maskT[:].unsqueeze(1).to_broadcast([128, MH, 128]), op=ALU.mult), r=[psr[stb], r_const], w=[r_Ssb[jb]])
                    if ti == 0 and j == 0:
                        k.op("dve", lambda e: e.tensor_copy(out=Ssb[0:1, jb, :, 0:1], in_=s00[:].unsqueeze(2)), r=[r_s00, r_Ssb[jb]], w=[r_Ssb[jb]])
                    ub = [2, 1]
                    k.pe([(lambda e, h=h: e.matmul(ps[2][:, h * 129:(h + 1) * 129], lhsT=ktm[:, j, h, :], rhs=vx[:, j, h, :], start=True, stop=True)) for h in range(3)],
                         r=[r_ktm[j], r_vx[j]], w=[psr[2]])
                    k.pe([lambda e: e.matmul(ps[1][:, 0:129], lhsT=ktm[:, j, 3, :], rhs=vx[:, j, 3, :], start=True, stop=True)], r=[r_ktm[j], r_vx[j]], w=[psr[1]])
                    st1[j] = ub

                def ml_stage2(j):
                    js = slice(j * 128, (j + 1) * 128)
                    jb = j % 2
                    ub = st1[j]
                    wbc = mbc[:, 16 + 4 * j:20 + 4 * j].unsqueeze(2).to_broadcast([128, MH, 129])
                    k.op("dve", lambda e: e.tensor_tensor(out=Cst[:], in0=Cst[:], in1=wbc, op=ALU.mult), r=r_C + [r_g], w=r_C)
                    k.op("act", lambda e: e.activation(out=Csb[:], in_=Cst[:], func=AF.Copy), r=r_C, w=r_Csb)
                    k.op("dve", lambda e: e.tensor_tensor(out=Cst[:, 0:3, :], in0=Cst[:, 0:3, :], in1=ps[2][:, 0:387].rearrange("p (h d) -> p h d", h=3), op=ALU.add),
                         r=r_C + [psr[2]], w=r_C)
                    k.op("dve", lambda e: e.tensor_tensor(out=Cst[:, 3, :], in0=Cst[:, 3, :], in1=ps[1][:, 0:129], op=ALU.add), r=r_C + [psr[1]], w=r_C)
                    nb = [0, 1]
                    for hb in range(2):
                        fns = []
                        for hh in range(2):
                            h = 2 * hb + hh
                            fns.append(lambda e, h=h, hh=hh: e.matmul(ps[nb[hb]][:, hh * 129:(hh + 1) * 129], lhsT=Ssb[:, jb, h, :], rhs=vx[:, j, h, :], start=True, stop=False))
                            fns.append(lambda e, h=h, hh=hh: e.matmul(ps[nb[hb]][:, hh * 129:(hh + 1) * 129], lhsT=qcT[:, h, js], rhs=Csb[:, h, :], start=False, stop=True))
                        k.pe(fns, r=[r_Ssb[jb], r_vx[j]] + r_U1b[0:4] + r_Csb, w=[psr[nb[hb]]])
                    st2[j] = nb

                def ml_stage3a(j):
                    jb = j % 2
                    nb = st2[j]
                    smj = sm[:, jb, :]
                    k.op("dve", lambda e: e.memset(smj[:, 0:4], 0.0), w=[r_smj[jb]])
                    for h in range(MH):
                        hb, hh = h // 2, h % 2
                        k.op("act", lambda e: e.activation(out=ybf[:, jb, h, :], in_=ps[nb[hb]][:, hh * 129:hh * 129 + 128], func=AF.Square, scale=float(128 ** -0.5),
                                                           accum_out=smj[:, h:h + 1]), r=[psr[nb[hb]], r_smj[jb]], w=[r_ybf[jb], r_smj[jb]])
                    for hb in range(2):
                        k.op("dve", lambda e: e.tensor_copy(out=smj[:, 4 + 2 * hb:6 + 2 * hb], in_=ps[nb[hb]][:, 0:258].rearrange("p (h d) -> p h d", h=2)[:, :, 128]),
                             r=[psr[nb[hb]], r_smj[jb]], w=[r_smj[jb]])
                    k.op("dve", lambda e: e.scalar_tensor_tensor(out=smj[:, 8:12], in0=smj[:, 4:8], scalar=-1.0, in1=smj[:, 4:8], op0=ALU.mult, op1=ALU.max), r=[r_smj[jb]], w=[r_smj[jb]])
                    k.op("dve", lambda e: e.tensor_tensor(out=smj[:, 12:16], in0=smj[:, 8:12], in1=clampv[:, j, :], op=ALU.max), r=[r_smj[jb], r_g], w=[r_smj[jb]])
                    k.op("dve", lambda e: e.scalar_tensor_tensor(out=smj[:, 16:20], in0=smj[:, 12:16], scalar=EPS, in1=smj[:, 12:16], op0=ALU.mult, op1=ALU.mult), r=[r_smj[jb]], w=[r_smj[jb]])
                    k.op("dve", lambda e: e.tensor_tensor(out=smj[:, 20:24], in0=smj[:, 16:20], in1=smj[:, 0:4], op=ALU.add), r=[r_smj[jb]], w=[r_smj[jb]])
                    k.op("act", lambda e: e.activation(out=smj[:, 24:28], in_=smj[:, 20:24], func=AF.Ln), r=[r_smj[jb]], w=[r_smj[jb]])
                    k.op("act", lambda e: e.activation(out=smj[:, 28:32], in_=smj[:, 24:28], func=AF.Exp, scale=-0.5), r=[r_smj[jb]], w=[r_smj[jb]])
                    for hb in range(2):
                        k.op("dve", lambda e: e.tensor_tensor(out=ybf[:, jb, 2 * hb:2 * hb + 2, :], in0=ps[nb[hb]][:, 0:258].rearrange("p (h d) -> p h d", h=2)[:, :, 0:128],
                                                              in1=smj[:, 28 + 2 * hb:30 + 2 * hb].unsqueeze(2).to_broadcast([128, 2, 128]), op=ALU.mult),
                             r=[psr[nb[hb]], r_smj[jb]], w=[r_ybf[jb]])

                def ml_stage3b(j):
                    js = slice(j * 128, (j + 1) * 128)
                    jb = j % 2
                    yb = 2
                    k.pe([(lambda e, h=h: e.transpose(out=psb[yb][:, h * 128:(h + 1) * 128], in_=ybf[:, jb, h, :], identity=ident_b[:])) for h in range(MH)],
                         r=[r_ybf[jb], r_const], w=[psr[yb]])
                    k.op("dve", lambda e: e.tensor_tensor(out=mixT[:, 4:8, js], in0=psb[yb][:, 0:512].rearrange("p (h t) -> p h t", h=MH), in1=sgo[:, :, js], op=ALU.mult),
                         r=[psr[yb]] + r_sgo, w=r_mix[4:8])

                for j in range(NSUB):
                    ml_stage1(j)
                    ml_stage2(j)
                    ml_stage3a(j)
                    ml_stage3b(j)
                dump("mixml", mixT[:, 4:8, :], r_mix[4:8])
                attn_range(p_mid, len(pairs), False)
                dump("mixfox", mixT[:, 0:4, :], r_mix[0:4])


                k.alias(V_ml, V_hn)
                sbo = [slab_acquire(f"out{half}") for half in range(2)]

                def wout_stage(sub):
                    for half in range(2):
                        pb = mm_tok(sbo[half], sub, mixT, r_mix)
                        k.op("dve", lambda e: e.tensor_tensor(out=R[:, sub, half * 512:(half + 1) * 512], in0=R[:, sub, half * 512:(half + 1) * 512],
                                                              in1=ps[pb][:], op=ALU.add), r=[psr[pb], r_R[sub]], w=[r_R[sub]])

                stage_then_norm(wout_stage, 1, after_stage=lambda: [slab_release(b) for b in sbo])
                dump("x1", R[:], r_R)
                def up_stage(qd):
                    ab = qd % 2
                    r_aT = r_U1a if ab == 0 else r_U1b
                    for sl in range(2):
                        sb = slab_acquire(f"up{qd * 1024 + sl * 512}")
                        for c4 in range(4):
                            pb = mm_feat(sb, c4)
                            if c4 == 3:
                                slab_release(sb)
                            kk = sl * 4 + c4
                            k.op("act", lambda e: e.activation(out=aT[:, ab, kk, :], in_=ps[pb][:], func=AF.Relu), r=[psr[pb]], w=[r_aT[kk]])
                            k.op("dve", lambda e: e.tensor_tensor(out=aT[:, ab, kk, :], in0=aT[:, ab, kk, :], in1=aT[:, ab, kk, :], op=ALU.mult), r=[r_aT[kk]], w=[r_aT[kk]])

                def down_one(qd, sbx, sub, half):
                    ab = qd % 2
                    r_aT = r_U1a if ab == 0 else r_U1b
                    pb = mm_tok(sbx, sub, aT[:, ab], r_aT)
                    k.op("dve", lambda e: e.tensor_tensor(out=R[:, sub, half * 512:(half + 1) * 512], in0=R[:, sub, half * 512:(half + 1) * 512],
                                                          in1=ps[pb][:], op=ALU.add), r=[psr[pb], r_R[sub]], w=[r_R[sub]])

                up_stage(0)
                for qd in range(4):
                    if qd + 1 < 4:
                        up_stage(qd + 1)
                    if qd < 3:
                        for half in range(2):
                            sbx = slab_acquire(f"down{qd}_{half}")
                            for sub in range(NSUB):
                                down_one(qd, sbx, sub, half)
                            slab_release(sbx)
                    else:
                        sbd = [slab_acquire(f"down{qd}_{half}") for half in range(2)]

                        def down_stage(sub):
                            for half in range(2):
                                down_one(3, sbd[half], sub, half)

                        stage_then_norm(down_stage, 2, after_stage=lambda: [slab_release(b) for b in sbd])
                dump("x2", R[:], r_R)
                if g_idx + 1 < NSEQ * NT:
                    nseq, nti = divmod(g_idx + 1, NT)
                    k.alias(r_U1a + r_U1b, bU["rR"])
                    k.dma("pool", "xin", bU["R"], x_d[nseq, nti * T:(nti + 1) * T, :].rearrange("(j p) d -> p j d", p=128), w=bU["rR"])
                k.alias(V_conv + V_att, V_ple2)
                k.dma("pool", "pin", p32, p_d[seq, t0:t0 + T, :].rearrange("(j p) d -> p j d", p=128), w=[r_p])
                k.op("dve", lambda e: e.tensor_copy(out=pbf, in_=p32), r=[r_p], w=[r_p])
                for sub in range(NSUB):
                    pb = bank("mm")
                    k.pe([(lambda e, c=c: e.transpose(out=psb[pb][:, c * 128:(c + 1) * 128], in_=pbf[:, sub, c * 128:(c + 1) * 128], identity=ident_b[:])) for c in range(2)],
                         r=[r_p, r_const], w=[psr[pb]])
                    k.op("dve", lambda e: e.tensor_copy(out=pT[:, :, sub * 128:(sub + 1) * 128], in_=psb[pb][:, 0:256].rearrange("p (c t) -> p c t", c=2)),
                         r=[psr[pb]], w=[r_p])
                sbg = [slab_acquire(f"pg{half}") for half in range(2)]

                def ple_stage(sub):
                    for half in range(2):
                        gbk2 = mm_tok(sbg[half], sub, hT, [r_hT[sub]])
                        ebk = bank("mm")
                        k.pe([(lambda e, c=c: e.matmul(ps[ebk][:], lhsT=pT[:, c, sub * 128:(sub + 1) * 128], rhs=pleW[:, c, half * 512:(half + 1) * 512],
                                                        start=(c == 0), stop=(c == 1))) for c in range(2)], r=[r_p, r_pleW], w=[psr[ebk]])
                        si = half
                        k.op("act", lambda e: e.activation(out=sig[si], in_=ps[gbk2][:], func=AF.Exp, scale=-1.0), r=[psr[gbk2]], w=[r_sig[si]])
                        k.op("act", lambda e: e.activation(out=sig[si], in_=sig[si], func=AF.Ln, bias=ones_f[:, 0:1]), r=[r_sig[si], r_const], w=[r_sig[si]])
                        k.op("act", lambda e: e.activation(out=sig[si], in_=sig[si], func=AF.Exp, scale=-1.0), r=[r_sig[si]], w=[r_sig[si]])
                        k.op("dve", lambda e: e.tensor_tensor(out=sig[si], in0=sig[si], in1=ps[ebk][:], op=ALU.mult), r=[r_sig[si], psr[ebk]], w=[r_sig[si]])
                        k.op("dve", lambda e: e.tensor_tensor(out=R[:, sub, half * 512:(half + 1) * 512], in0=R[:, sub, half * 512:(half + 1) * 512],
                                                              in1=sig[si], op=ALU.add), r=[r_sig[si], r_R[sub]], w=[r_R[sub]])

                def final_stage(sub):
                    k.op("dve", lambda e: e.memset(ms2[:, sub:sub + 1], 0.0), w=[r_ms2[sub]])
                    k.op("act", lambda e: e.activation(out=junkF, in_=R[:, sub, :], func=AF.Square, scale=1.0 / 32, accum_out=ms2[:, sub:sub + 1]),
                         r=[r_R[sub], r_ms2[sub]], w=[r_cvacc[1], r_ms2[sub]])
                    k.op("act", lambda e: e.activation(out=ms2[:, 4 + sub:5 + sub], in_=ms2[:, sub:sub + 1], func=AF.Ln, bias=epsc[:]), r=[r_ms2[sub], r_const], w=[r_ms2[sub]])
                    k.op("act", lambda e: e.activation(out=rstd2[:, sub:sub + 1], in_=ms2[:, 4 + sub:5 + sub], func=AF.Exp, scale=-0.5), r=[r_ms2[sub]], w=[r_ms2[sub]])
                    k.op("dve", lambda e: e.scalar_tensor_tensor(out=R[:, sub, :], in0=R[:, sub, :], scalar=rstd2[:, sub:sub + 1], in1=gfin[:], op0=ALU.mult, op1=ALU.mult),
                         r=[r_R[sub], r_ms2[sub], r_const], w=[r_R[sub]])
                    k.dma("pool", f"yout{sub}", y_d[seq, t0 + sub * 128:t0 + (sub + 1) * 128, :], R[:, sub, :], r=[r_R[sub]])

                ple_stage(0)
                ple_stage(1)
                final_stage(0)
                ple_stage(2)
                final_stage(1)
                ple_stage(3)
                for b_ in sbg:
                    slab_release(b_)
                final_stage(2)
                final_stage(3)

        k.finalize(final_wait_streams=("yout0", "yout1", "yout2", "yout3", "dbg"))
        print("instructions:", k.nins, "waits:", k.nwait, "counts:", k.cnt, "sim_us: %.1f" % k.sim_time)
    return nc


_NC_CACHE = {}


def _prep_shared(inp):
    f = lambda a: np.ascontiguousarray(np.asarray(a, dtype=np.float32))
    w_in = f(inp["w_in"][0])
    col = lambda g: f(g).reshape(8, 128).T
    gout = np.concatenate([f(inp["g_fox_out"][0]), f(inp["g_mlstm_out"][0])])
    sh = {
        "w_in": w_in,
        "w_out": f(inp["w_out"][0]),
        "w_up": f(inp["w_up"][0]),
        "w_down": f(inp["w_down"][0]),
        "w_ple": f(inp["w_ple"][0]),
        "w_pg": f(inp["w_ple_gate"][0]),
        "wgate": f(np.concatenate([w_in[:, 1536:1544], w_in[:, 3080:3088]], axis=1)),
        "gcol": f(np.concatenate([col(inp["g_mix"][0]), col(inp["g_mlp"][0]), col(inp["g_ple"][0]), col(gout)], axis=1)),
        "gfox": f(f(inp["g_fox_out"][0]).reshape(8, 64).T),
        "gfin": f(inp["g_final"]),
        "gbias": f(np.concatenate([inp["b_fox_f"][0], inp["b_mlstm_i"][0], inp["b_mlstm_f"][0]])),
        "wconv": f(f(inp["w_conv"][0]).reshape(4, 8, 128).transpose(2, 1, 0)),
        "gmixrow": f(f(inp["g_mix"][0]).reshape(1, D)),
        "wc3row": f(f(inp["w_conv"][0])[3].reshape(1, D)),
    }
    return sh


def run(inp, n_cores=8, dbg=None):
    x = np.asarray(inp["x"], dtype=np.float32)
    p = np.asarray(inp["p"], dtype=np.float32)[0]
    B, S, _ = x.shape
    NSEQ = B // n_cores
    key = (NSEQ, S, tuple(sorted(dbg.items())) if dbg else None)
    if key not in _NC_CACHE:
        _NC_CACHE[key] = build_nc(NSEQ, S, dbg)
    nc = _NC_CACHE[key]
    sh = _prep_shared(inp)
    in_maps = []
    for c in range(n_cores):
        m = dict(sh)
        m["x"] = np.ascontiguousarray(x[c * NSEQ:(c + 1) * NSEQ])
        m["p"] = np.ascontiguousarray(p[c * NSEQ:(c + 1) * NSEQ])
        in_maps.append(m)
    res = run_bass_kernel_spmd(nc, in_maps, core_ids=list(range(n_cores)))
    y = np.concatenate([r["y"] for r in res.results], axis=0)
    if dbg:
        return y, res.results
    return y


def kernel(**inputs):
    return run(inputs, 8).astype(np.float32)
```

```python
import heapq
import os
import sys
import types
import numpy as np
import concourse.bass as bass
import concourse.mybir as mybir
from concourse.bass_utils import run_bass_kernel_spmd
from contextlib import ExitStack

F32 = mybir.dt.float32
BF16 = mybir.dt.bfloat16
ALU = mybir.AluOpType
AF = mybir.ActivationFunctionType
AX = mybir.AxisListType

D = 1024
T = 512
NSUB = 4
FH = 8
MH = 4
INC = 3600
DFF = 4096
PLE = 256
EPS = 1e-6
NSLAB = 2
NPT = 3


class Res:
    __slots__ = ("name", "w", "r")

    def __init__(self, name):
        self.name = name
        self.w = None
        self.r = []


def _freeze(fn):
    if fn.__closure__ is None:
        return fn
    cells = tuple(types.CellType(c.cell_contents) for c in fn.__closure__)
    return types.FunctionType(fn.__code__, fn.__globals__, fn.__name__, fn.__defaults__, cells)


class _Probe:
    def __getattr__(self, name):
        def f(*a, **kw):
            out = kw.get("out", a[0] if a else None)
            return out, kw
        return f


def _free_elems(ap):
    n = 1
    for d in ap.shape[1:]:
        n *= int(d)
    return n


_DT_SIZE = {}


class KB:
    ENG = ("pe", "act", "dve", "pool", "sp")

    def __init__(self, nc, es):
        self.nc = nc
        self.es = es
        self.eng = {"pe": nc.tensor, "act": nc.scalar, "dve": nc.vector, "pool": nc.gpsimd, "sp": nc.sync}
        self.sem = {e: es.enter_context(nc.semaphore("s_" + e)) for e in self.ENG}
        self.ops = []
        self.dsem = {}

    def _deps(self, r, w):
        d = set()
        for x in r:
            if x.w is not None:
                d.add(x.w)
        for x in w:
            if x.w is not None:
                d.add(x.w)
            d.update(x.r)
        return d

    def _mark(self, oid, r, w):
        for x in r:
            x.r.append(oid)
        for x in w:
            x.w = oid
            x.r = []

    def _add(self, **kw):
        oid = len(self.ops)
        kw["id"] = oid
        kw["line"] = sys._getframe(2).f_lineno
        self.ops.append(kw)
        return oid

    def op(self, e, fn, r=(), w=(), f=None):
        if f is None:
            out, kw = fn(_Probe())
            f = _free_elems(out)
        base = {"act": 0.20, "dve": 0.12, "pool": 0.25}.get(e, 0.1)
        rate = {"act": 1400.0, "dve": 960.0, "pool": 500.0}.get(e, 1000.0)
        oid = self._add(eng=e, kind="op", fns=[_freeze(fn)], deps=self._deps(r, w), dur=base + f / rate, lat=0.0)
        self._mark(oid, r, w)
        return oid

    def pe(self, fns, r=(), w=(), n=None):
        dur = 0.0
        for fn in fns:
            out, kw = fn(_Probe())
            ni = _free_elems(out)
            passes = 4.0 if ("rhs" in kw and kw["rhs"].dtype == F32) else 1.0
            dur += 0.03 + passes * ni / 2200.0
        oid = self._add(eng="pe", kind="op", fns=[_freeze(fn) for fn in fns], deps=self._deps(r, w), dur=dur, lat=0.0)
        self._mark(oid, r, w)
        return oid

    def dma(self, q, stream, out, in_, r=(), w=(), nbytes=None):
        if stream not in self.dsem:
            self.dsem[stream] = self.es.enter_context(self.nc.semaphore("d_" + stream))
        if nbytes is None:
            nel = 1
            for d in out.shape:
                nel *= int(d)
            nbytes = nel * (2 if out.dtype == BF16 else 4)
        fn = lambda e, out=out, in_=in_: e.dma_start(out=out, in_=in_)
        oid = self._add(eng=q, kind="dma", fns=[fn], deps=self._deps(r, w), dur=(1.0 if q == "pool" else 0.06), lat=2.0 + nbytes / 300000.0, stream=stream)
        self._mark(oid, r, w)
        return oid

    def alias(self, old, new):
        ids = set()
        for o in old:
            if o.w is not None:
                ids.add(o.w)
            ids.update(o.r)
        for n in new:
            n.r = list(set(n.r) | ids)

    def finalize(self, final_wait_streams=()):
        ops = self.ops
        n = len(ops)
        succ = [[] for _ in range(n)]
        ndep = [0] * n
        for o in ops:
            o["deps"].discard(o["id"])
            ndep[o["id"]] = len(o["deps"])
            for d in o["deps"]:
                succ[d].append(o["id"])
        fin = [0.0] * n
        start = [0.0] * n
        efree = {e: 0.0 for e in self.ENG}
        ready_t = [0.0] * n
        heap = []
        for o in ops:
            if ndep[o["id"]] == 0:
                heapq.heappush(heap, (0.0, o["id"]))
        order = []
        while heap:
            est, oid = heapq.heappop(heap)
            o = ops[oid]
            real = max(ready_t[oid], efree[o["eng"]])
            if real > est + 1e-9:
                heapq.heappush(heap, (real, oid))
                continue
            start[oid] = real
            efree[o["eng"]] = real + o["dur"]
            fin[oid] = real + o["dur"] + o["lat"]
            order.append(oid)
            for s_ in succ[oid]:
                lat = 0.04 if ops[s_]["eng"] == o["eng"] else 0.18
                ready_t[s_] = max(ready_t[s_], fin[oid] + lat)
                ndep[s_] -= 1
                if ndep[s_] == 0:
                    heapq.heappush(heap, (max(ready_t[s_], efree[ops[s_]["eng"]]), s_))
        assert len(order) == n, (len(order), n)
        self.sim_time = max(fin) if n else 0.0
        if os.environ.get("KB_ANALYZE"):
            for eng_name in ("pe", "act", "dve"):
                prev_end = 0.0
                busy = 0.0
                blame = {}
                for oid in order:
                    o = ops[oid]
                    if o["eng"] != eng_name:
                        continue
                    gap = start[oid] - prev_end
                    if gap > 0.05 and o["deps"]:
                        d = max(o["deps"], key=lambda d_: fin[d_])
                        key = (ops[d]["eng"], ops[d]["line"], o["line"])
                        blame[key] = blame.get(key, 0.0) + gap
                    busy += o["dur"]
                    prev_end = start[oid] + o["dur"]
                print("ANALYZE %s busy %.0f us of %.0f (%.0f%%)" % (eng_name, busy, self.sim_time, 100 * busy / self.sim_time))
                for key, g in sorted(blame.items(), key=lambda kv: -kv[1])[:14]:
                    print("   idle %.0f us waiting for %s op@line %d (consumer line %d)" % (g, key[0], key[1], key[2]))
        cnt = {e: 0 for e in self.ENG}
        dcnt = {}
        tok = [None] * n
        seen = {e: {} for e in self.ENG}
        streams = {e: [] for e in self.ENG}
        nwait = 0
        for oid in order:
            o = ops[oid]
            e = o["eng"]
            need = {}
            for d in o["deps"]:
                key, val = tok[d]
                if key == "pe" and e == "pe" and ops[d]["kind"] == "op":
                    continue
                if need.get(key, 0) < val:
                    need[key] = val
            for key, val in need.items():
                if seen[e].get(key, 0) >= val:
                    continue
                semh = self.sem[key] if key in self.sem else self.dsem[key[4:]]
                self.eng[e].wait_ge(semh, val)
                seen[e][key] = val
                streams[e].append(("wait", key, val))
                nwait += 1
            ins = None
            for fn in o["fns"]:
                ins = fn(self.eng[e])
            if o["kind"] == "dma":
                st = o["stream"]
                dcnt[st] = dcnt.get(st, 0) + 1
                ins.then_inc(self.dsem[st], 16)
                tok[oid] = ("dma:" + st, 16 * dcnt[st])
                streams[e].append(("inc", "dma:" + st, 16))
            else:
                cnt[e] += 1
                ins.then_inc(self.sem[e], 1)
                tok[oid] = (e, cnt[e])
                streams[e].append(("inc", e, 1))
        for st in final_wait_streams:
            if st in dcnt:
                self.eng["pool"].wait_ge(self.dsem[st], 16 * dcnt[st])
                streams["pool"].append(("wait", "dma:" + st, 16 * dcnt[st]))
        semv = {}
        pos = {e: 0 for e in self.ENG}
        progress = True
        while progress:
            progress = False
            for e in self.ENG:
                st = streams[e]
                while pos[e] < len(st):
                    kind, key, val = st[pos[e]]
                    if kind == "wait":
                        if semv.get(key, 0) < val:
                            break
                    else:
                        semv[key] = semv.get(key, 0) + val
                    pos[e] += 1
                    progress = True
        for e in self.ENG:
            assert pos[e] == len(streams[e]), ("DEADLOCK in emitted program", e, pos[e], len(streams[e]), streams[e][pos[e]])
        self.cnt = cnt
        self.nwait = nwait
        self.nins = sum(len(o["fns"]) for o in ops)


def build_nc(NSEQ, S, dbg=None):
    nc = bass.Bass("TRN2", target_bir_lowering=False)
    NT = S // T
    NKT = S // 128
    di = lambda n, sh: nc.dram_tensor(n, sh, F32, kind="ExternalInput").ap()
    x_d = di("x", [NSEQ, S, D])
    p_d = di("p", [NSEQ, S, PLE])
    w_in_d = di("w_in", [D, INC])
    w_out_d = di("w_out", [D, D])
    w_up_d = di("w_up", [D, DFF])
    w_down_d = di("w_down", [DFF, D])
    w_ple_d = di("w_ple", [PLE, D])
    w_pg_d = di("w_pg", [D, D])
    wgate_d = di("wgate", [D, 16])
    gcol_d = di("gcol", [128, 32])
    gfox_d = di("gfox", [64, 8])
    gfin_d = di("gfin", [D])
    gbias_d = di("gbias", [16])
    wconv_d = di("wconv", [128, 8, 4])
    gmixrow_d = di("gmixrow", [1, D])
    wc3row_d = di("wc3row", [1, D])
    y_d = nc.dram_tensor("y", [NSEQ, S, D], F32, kind="ExternalOutput").ap()
    dbg_d = {}
    if dbg:
        for n, sh in dbg.items():
            dbg_d[n] = nc.dram_tensor("dbg_" + n, sh, F32, kind="ExternalOutput").ap()
    wb = lambda n, sh: nc.dram_tensor(n, sh, BF16, kind="Internal").ap()
    wb_in = wb("wb_in", [D, INC])
    wb_out = wb("wb_out", [D, D])
    wb_up = wb("wb_up", [D, DFF])
    wb_down = wb("wb_down", [DFF, D])
    wb_ple = wb("wb_ple", [PLE, D])
    wb_pg = wb("wb_pg", [D, D])

    es = ExitStack()
    with es:
        es.enter_context(nc.allow_low_precision("bf16 matmul operands / activations are intended (bf16-reference regime)"))
        k = KB(nc, es)
        SB = lambda n, sh, dt: es.enter_context(nc.sbuf_tensor("sb_" + n, sh, dt))
        Kc = SB("Kc", [128, FH, S], BF16)
        Vcf = SB("Vc", [128, NKT * FH * 65 + 64], BF16)
        Vc = Vcf[:, 0:NKT * FH * 65].rearrange("p (k h e) -> p k h e", k=NKT, h=FH)
        Cst = SB("Cst", [128, MH, 129], F32)
        Csb = SB("Csb", [128, MH, 129], BF16)
        BufA = SB("BufA", [128, 8192], BF16)
        hT = SB("hT", [128, 8, T], BF16)
        mixT = SB("mixT", [128, 8, T], BF16)
        slabs = [SB(f"slab{i}", [128, 8, 512], BF16) for i in range(NSLAB)]
        BufB = SB("BufB", [128, 8192], BF16)
        U2 = SB("U2", [128, 2560], F32)
        U3 = SB("U3", [128, 2080], F32)
        sgo = SB("sgo", [128, MH, T], BF16)
        pleW = SB("pleW", [128, 2, D], BF16)
        ginv = SB("ginv", [128, 4], F32)
        hml = SB("hml", [128, 3, NSUB, FH], BF16)
        ident_f = SB("ident_f", [128, 128], F32)
        ident_b = SB("ident_b", [128, 128], BF16)
        tri_f = SB("tri_f", [128, 128], F32)
        ones_f = SB("ones_f", [128, 128], F32)
        maskT = SB("maskT", [128, 128], BF16)
        Amat = SB("Amat", [128, 64], BF16)
        gfin = SB("gfin", [128, D], F32)
        gcol = SB("gcol", [128, 32], F32)
        gfox = SB("gfox", [64, 8], F32)
        gb = SB("gb", [128, 16], F32)
        wcv = SB("wcv", [128, 8, 4], F32)
        Wg = SB("Wg", [128, 8, 16], BF16)
        halo = SB("halo", [128, 8, 3], F32)
        tot = SB("tot", [1, 16], F32)
        Mt = SB("Mt", [4, 8], F32)
        DG = SB("DG", [4, 32], F32)
        A4 = SB("A4", [4, 4], F32)
        wdec = SB("wdec", [4, 4], F32)
        ms = SB("ms", [128, 8], F32)
        ms2 = SB("ms2", [128, 8], F32)
        h0col = SB("h0col", [128, 8], F32)
        s00 = SB("s00", [1, 4], F32)
        rstd2 = SB("rstd2", [128, 4], F32)
        sigb = SB("sigb", [128, 2, 512], F32)
        rstd = SB("rstd", [128, 8], F32)
        epsc = SB("epsc", [128, 1], F32)
        lnsc = SB("lnsc", [128, 1], F32)
        zg = SB("zg", [128, NSUB, 16], F32)
        e1 = SB("e1", [128, NSUB, 16], F32)
        lsp = SB("lsp", [128, NSUB, 16], F32)
        cpos = SB("cpos", [128, NSUB, 16], F32)
        r1t = SB("r1t", [128, 64], F32)
        alpha = SB("alpha", [128, NSUB, 4], F32)
        mbc = SB("mbc", [128, 32], F32)
        d1 = SB("d1", [128, NSUB, 4], F32)
        es_t = SB("es_t", [128, NSUB, 4], F32)
        clampv = SB("clampv", [128, NSUB, 4], F32)
        sm = SB("sm", [128, 2, 32], F32)
        ps = [es.enter_context(nc.psum_tensor(f"ps{i}", [128, 512], F32)) for i in range(8)]
        psr = [Res(f"ps{i}") for i in range(8)]
        psb = [ps[i][:].bitcast(BF16) for i in range(8)]

        def buf_views(X):
            return dict(
                R=X[:].bitcast(F32).rearrange("p (s d) -> p s d", s=NSUB),
                aT=X[:].rearrange("p (b c t) -> p b c t", b=2, c=8),
                QT=X[:, 0:4096].rearrange("p (h t) -> p h t", h=FH),
                qcT=X[:, 4096:6144].rearrange("p (h t) -> p h t", h=MH),
                kcT=X[:, 6144:8192].rearrange("p (h t) -> p h t", h=MH),
                rR=[Res(f"R{s_}") for s_ in range(NSUB)],
                rUa=[Res(f"U1a{h}") for h in range(FH)],
                rUb=[Res(f"U1b{h}") for h in range(8)],
            )

        bufs = [buf_views(BufA), buf_views(BufB)]
        R = bufs[0]["R"]
        r_R = bufs[0]["rR"]
        aT, QT, qcT, kcT = bufs[1]["aT"], bufs[1]["QT"], bufs[1]["qcT"], bufs[1]["kcT"]
        r_U1a, r_U1b = bufs[1]["rUa"], bufs[1]["rUb"]
        U2b = U2[:].bitcast(BF16)
        U3b = U3[:].bitcast(BF16)
        Pt = [U2b[:, i * 512:(i + 1) * 512] for i in range(NPT)]
        sqx = U2b[:, 1536:2048]
        lnr = U2[:, 1024:1536]
        cv_acc = [U2[:, 1536 + i * 512:1536 + (i + 1) * 512] for i in range(2)]
        junkF = U2b[:, 4096:5120]
        p32 = U2[:, 0:1024].rearrange("p (s d) -> p s d", s=NSUB)
        pbf = U2b[:, 2048:3072].rearrange("p (s d) -> p s d", s=NSUB)
        pT = U2b[:, 3072:4096].rearrange("p (c t) -> p c t", c=2)
        ktm = U3b[:, 0:2048].rearrange("p (s h d) -> p s h d", s=NSUB, h=MH)
        vx = U3b[:, 2048:2048 + NSUB * MH * 129].rearrange("p (s h d) -> p s h d", s=NSUB, h=MH)
        hn = U3b[:, 0:2048].rearrange("p (b d) -> p b d", b=2)
        EK = U3b[:, 2048:3072].rearrange("p (s h e) -> p s h e", s=NSUB, h=FH)
        EQ = U3b[:, 3072:4096].rearrange("p (s h e) -> p s h e", s=NSUB, h=FH)
        sig = [sigb[:, i, :] for i in range(2)]
        prod = [U3[:, 1024 + i * 512:1024 + (i + 1) * 512] for i in range(2)]
        Ssb = SB("Ssb", [128, 2, MH, 128], BF16)
        ybf = SB("ybf", [128, 2, MH, 128], BF16)
        print("sbuf bytes remaining:", nc.sbuf_bytes_remaining)

        r_Kc = [Res(f"Kc{h}") for h in range(FH)]
        r_Vc = Res("Vc")
        r_C = [Res(f"C{h}") for h in range(MH)]
        r_Csb = [Res(f"Csb{h}") for h in range(MH)]
        r_hT = [Res(f"hT{s}") for s in range(NSUB)]
        r_mix = [Res(f"mix{c}") for c in range(8)]
        r_slab = [Res(f"slab{i}") for i in range(NSLAB)]
        r_U2 = Res("U2")
        r_cvacc = [Res(f"cvacc{i}") for i in range(2)]
        r_Pt = [Res(f"Pt{i}") for i in range(NPT)]
        r_sqx = Res("sqx")
        r_lnr = Res("lnr")
        r_p = Res("pbufs")
        r_ktm = [Res(f"ktm{s}") for s in range(NSUB)]
        r_vx = [Res(f"vx{s}") for s in range(NSUB)]
        r_hn = [Res("hn0"), Res("hn1")]
        r_sig = [Res("sig0"), Res("sig1")]
        r_prod = [Res("prod0"), Res("prod1")]
        r_sgo = [Res(f"sgo{h}") for h in range(MH)]
        r_EK = Res("EK")
        r_EQ = Res("EQ")
        r_pleW = Res("pleW")
        r_s00 = Res("s00")
        r_h0 = Res("h0col")
        V_E = [r_EK, r_EQ]
        r_const = Res("const")
        r_halo = [Res(f"halo{c}") for c in range(8)]
        r_tot = Res("tot")
        r_Mt = Res("Mt")
        r_g = Res("gates")
        r_ms = Res("ms")
        r_msS = [Res(f"ms{i}") for i in range(NSUB)]
        r_ms2 = [Res(f"ms2_{i}") for i in range(NSUB)]
        r_sm = Res("sm")
        r_smj = [Res("smj0"), Res("smj1")]
        r_Ssb = [Res("Ssb0"), Res("Ssb1")]
        r_ybf = [Res("ybf0"), Res("ybf1")]
        r_w = {n: Res("w_" + n) for n in ["in", "out", "up", "down", "ple", "pg"]}
        V_att = r_Pt + [r_sqx, r_lnr]
        V_conv = r_cvacc
        V_ple2 = [r_p]
        V_hn = r_hn
        V_ml = r_ktm + r_vx
        V_ple3 = r_sig + r_prod

        bank_rr = {"mm": [0, [0, 1, 2]], "st": [0, [3, 4, 5]], "pv": [0, [6, 7]], "all": [0, [0, 1, 2, 3, 4, 5, 6, 7]]}

        def bank(cls):
            st = bank_rr[cls]
            b = st[1][st[0] % len(st[1])]
            st[0] += 1
            return b

        k.dma("pool", "xin", R[:], x_d[0, 0:T, :].rearrange("(j p) d -> p j d", p=128), w=r_R)
        k.dma("pool", "setup2", Wg[:], wgate_d.rearrange("(c p) n -> p c n", p=128), w=[r_const])
        k.dma("sp", "setup", gcol[:], gcol_d, w=[r_const])
        k.dma("sp", "setup", gfox[:], gfox_d, w=[r_const])
        k.dma("sp", "setup", gfin[:], gfin_d.partition_broadcast(128), w=[r_const])
        k.dma("sp", "setup", gb[:], gbias_d.partition_broadcast(128), w=[r_const])
        k.dma("sp", "setup", wcv[:], wconv_d, w=[r_const])
        k.op("dve", lambda e: e.memset(ident_f[:], 1.0), w=[r_const])
        k.op("pool", lambda e: e.affine_select(out=ident_f[:], in_=ident_f[:], pattern=[[-1, 128]], compare_op=ALU.is_equal,
                                               fill=0.0, base=0, channel_multiplier=1), r=[r_const], w=[r_const])
        k.op("dve", lambda e: e.memset(tri_f[:], 1.0), w=[r_const])
        k.op("pool", lambda e: e.affine_select(out=tri_f[:], in_=tri_f[:], pattern=[[1, 128]], compare_op=ALU.is_ge,
                                               fill=0.0, base=0, channel_multiplier=-1), r=[r_const], w=[r_const])
        k.op("dve", lambda e: e.memset(ones_f[:], 1.0), w=[r_const])
        k.op("dve", lambda e: e.tensor_copy(out=ident_b[:], in_=ident_f[:]), r=[r_const], w=[r_const])
        k.op("dve", lambda e: e.tensor_copy(out=maskT[:], in_=tri_f[:]), r=[r_const], w=[r_const])
        k.op("dve", lambda e: e.memset(Amat[0:64, :], 1.0 / 64), w=[r_const])
        k.op("dve", lambda e: e.memset(Amat[64:65, :], EPS), w=[r_const])
        k.op("dve", lambda e: e.memset(epsc[:], EPS), w=[r_const])
        k.op("dve", lambda e: e.memset(lnsc[:], float(-0.5 * np.log(128.0))), w=[r_const])
        k.op("dve", lambda e: e.reciprocal(out=ginv[:], in_=gcol[:, 28:32]), r=[r_const], w=[r_const])
        k.op("dve", lambda e: e.memset(Vcf[:], 0.0), w=[r_Vc])
        k.op("dve", lambda e: e.memset(Vc[:, :, :, 64:65], 1.0), r=[r_Vc], w=[r_Vc])

        slab_i = [0]

        r_wblk = {}

        def cast_blk(name, dst, src):
            r_wblk[name] = Res("wb_" + name)
            k.dma("pool", "wc_" + name, dst, src, w=[r_wblk[name]])

        for c0 in (0, 512, 1024, 1544, 2056, 2568, 3088):
            cast_blk(f"in{c0}", wb_in[:, c0:c0 + 512], w_in_d[:, c0:c0 + 512])
        for hf in range(2):
            cast_blk(f"out{hf}", wb_out[:, hf * 512:(hf + 1) * 512], w_out_d[:, hf * 512:(hf + 1) * 512])
        for qd in range(4):
            for sl in range(2):
                c0 = qd * 1024 + sl * 512
                cast_blk(f"up{c0}", wb_up[:, c0:c0 + 512], w_up_d[:, c0:c0 + 512])
            for hf in range(2):
                cast_blk(f"down{qd}_{hf}", wb_down[qd * 1024:(qd + 1) * 1024, hf * 512:(hf + 1) * 512], w_down_d[qd * 1024:(qd + 1) * 1024, hf * 512:(hf + 1) * 512])
        cast_blk("ple", wb_ple, w_ple_d)
        for hf in range(2):
            cast_blk(f"pg{hf}", wb_pg[:, hf * 512:(hf + 1) * 512], w_pg_d[:, hf * 512:(hf + 1) * 512])
        k.dma("sp", "setup3", pleW[:], wb_ple.rearrange("(c p) n -> p c n", p=128), r=[r_wblk["ple"]], w=[r_pleW])
        tile_specs = []
        for c0 in (0, 512, 1024, 1544, 2056, 2568, 3088):
            tile_specs.append((f"in{c0}", wb_in[:, c0:c0 + 512].rearrange("(c p) n -> p c n", p=128)))
        for hf in range(2):
            tile_specs.append((f"out{hf}", wb_out[:, hf * 512:(hf + 1) * 512].rearrange("(c p) n -> p c n", p=128)))
        def up_specs(qd):
            for sl in range(2):
                c0 = qd * 1024 + sl * 512
                tile_specs.append((f"up{c0}", wb_up[:, c0:c0 + 512].rearrange("(c p) n -> p c n", p=128)))

        def down_specs(qd):
            for hf in range(2):
                tile_specs.append((f"down{qd}_{hf}", wb_down[qd * 1024:(qd + 1) * 1024, hf * 512:(hf + 1) * 512].rearrange("(c p) n -> p c n", p=128)))

        up_specs(0)
        for qd in range(4):
            if qd + 1 < 4:
                up_specs(qd + 1)
            down_specs(qd)
        for hf in range(2):
            tile_specs.append((f"pg{hf}", wb_pg[:, hf * 512:(hf + 1) * 512].rearrange("(c p) n -> p c n", p=128)))
        slab_specs = tile_specs * (NSEQ * NT)
        slab_pos = [0]
        slab_ready = []
        slab_free = list(range(NSLAB))

        def slab_topup():
            while slab_free and slab_pos[0] < len(slab_specs):
                b = slab_free.pop(0)
                name, src = slab_specs[slab_pos[0]]
                slab_pos[0] += 1
                k.dma("sp", f"slab{b}", slabs[b][:], src, r=[r_wblk[name]], w=[r_slab[b]])
                slab_ready.append((b, name))

        def slab_acquire(name):
            slab_topup()
            b, nm = slab_ready.pop(0)
            assert nm == name, (nm, name)
            return b

        def slab_release(b):
            slab_free.append(b)
            slab_topup()

        def wslab(wap, c0, ncols=512):
            return wap[:, c0:c0 + ncols].rearrange("(c p) n -> p c n", p=128)

        dbg_done = set()

        def dump(name, ap, rlist):
            if name in dbg_d and name not in dbg_done:
                dbg_done.add(name)
                k.dma("pool", "dbg", dbg_d[name], ap, r=rlist)

        def norm_a(sub):
            b = sub % 2
            k.op("dve", lambda e: e.memset(ms[:, sub:sub + 1], 0.0), w=[r_msS[sub]])
            k.op("act", lambda e: e.activation(out=hn[:, b, :], in_=R[:, sub, :], func=AF.Square, scale=1.0 / 32, accum_out=ms[:, sub:sub + 1]),
                 r=[r_R[sub], r_msS[sub]], w=[r_hn[b], r_msS[sub]])
            k.op("act", lambda e: e.activation(out=ms[:, 4 + sub:5 + sub], in_=ms[:, sub:sub + 1], func=AF.Ln, bias=epsc[:]), r=[r_msS[sub], r_const], w=[r_msS[sub]])
            k.op("act", lambda e: e.activation(out=rstd[:, sub:sub + 1], in_=ms[:, 4 + sub:5 + sub], func=AF.Exp, scale=-0.5), r=[r_msS[sub]], w=[r_msS[sub]])
            k.op("dve", lambda e: e.tensor_scalar(out=hn[:, b, :], in0=R[:, sub, :], scalar1=rstd[:, sub:sub + 1], scalar2=None, op0=ALU.mult),
                 r=[r_R[sub], r_msS[sub]], w=[r_hn[b]])

        def norm_b(gidx, sub):
            b = sub % 2
            pb = bank("mm")
            k.pe([(lambda e, c=c: e.transpose(out=psb[pb][:, c * 128:(c + 1) * 128], in_=hn[:, b, c * 128:(c + 1) * 128], identity=ident_b[:]))
                  for c in range(8)], r=[r_hn[b], r_const], w=[psr[pb]])
            k.op("dve", lambda e: e.tensor_tensor(
                out=hT[:, :, sub * 128:(sub + 1) * 128], in0=psb[pb].rearrange("p (c t) -> p c t", c=8),
                in1=gcol[:, gidx * 8:(gidx + 1) * 8].unsqueeze(2).to_broadcast([128, 8, 128]), op=ALU.mult),
                 r=[psr[pb], r_const], w=[r_hT[sub]])

        def stage_then_norm(stage_fn, gidx, after_stage=None):
            stage_fn(0)
            stage_fn(1)
            norm_a(0)
            stage_fn(2)
            norm_a(1)
            norm_b(gidx, 0)
            stage_fn(3)
            if after_stage is not None:
                after_stage()
            norm_a(2)
            norm_b(gidx, 1)
            norm_a(3)
            norm_b(gidx, 2)
            norm_b(gidx, 3)


        def mm_feat(sb, c4, kchunks=8, rhs_fn=None):
            pb = bank("mm")
            k.pe([(lambda e, c=c, pb=pb: e.matmul(ps[pb][:], lhsT=slabs[sb][:, c, c4 * 128:(c4 + 1) * 128], rhs=hT[:, c, :],
                                                   start=(c == 0), stop=(c == kchunks - 1))) for c in range(kchunks)],
                 r=[r_slab[sb]] + r_hT, w=[psr[pb]])
            return pb

        def mm_tok(sb, sub, lhs, rl, kchunks=8, col0=0):
            pb = bank("mm")
            k.pe([(lambda e, c=c, pb=pb: e.matmul(ps[pb][:], lhsT=lhs[:, c, sub * 128:(sub + 1) * 128], rhs=slabs[sb][:, c, col0:col0 + 512],
                                                   start=(c == 0), stop=(c == kchunks - 1))) for c in range(kchunks)],
                 r=[r_slab[sb]] + rl, w=[psr[pb]])
            return pb

        def tok0_precise():
            rowA = U3[0:1, 0:1024]
            rowB = lnr[0:1, :]
            Wf = sigb[:].rearrange("p a (c n) -> p (a c) n", c=4)
            WS = V_hn + [r_lnr]
            for hf in range(2):
                k.dma("sp", "rows", rowB, gmixrow_d[0:1, hf * 512:(hf + 1) * 512], w=WS)
                k.op("dve", lambda e: e.scalar_tensor_tensor(out=rowA[:, hf * 512:(hf + 1) * 512], in0=R[0:1, 0, hf * 512:(hf + 1) * 512], scalar=rstd[0:1, 0:1], in1=rowB,
                                                             op0=ALU.mult, op1=ALU.mult), r=WS + [r_R[0], r_msS[0]], w=WS)
            hb_ = bank("mm")
            k.pe([(lambda e, c=c: e.matmul(ps[hb_][:, c:c + 1], lhsT=rowA[:, c * 128:(c + 1) * 128], rhs=ones_f[0:1, 0:1], start=True, stop=True)) for c in range(8)],
                 r=WS + [r_const], w=[psr[hb_]])
            k.op("dve", lambda e: e.tensor_copy(out=h0col[:], in_=ps[hb_][:, 0:8]), r=[psr[hb_]], w=[r_h0])
            zb = [bank("mm"), bank("mm")]
            for i in range(8):
                k.dma("sp", "wf", Wf, w_in_d[:, 1544 + i * 128:1544 + (i + 1) * 128].rearrange("(c p) n -> p c n", p=128), w=r_sig)
                k.pe([(lambda e, c=c: e.matmul(ps[zb[i // 4]][0:1, (i % 4) * 128:(i % 4 + 1) * 128], lhsT=h0col[:, c:c + 1], rhs=Wf[:, c, :], start=(c == 0), stop=(c == 7)))
                      for c in range(8)], r=r_sig + [r_h0], w=[psr[zb[i // 4]]])
            for hf in range(2):
                k.dma("sp", "rows", rowA[:, hf * 512:(hf + 1) * 512], wc3row_d[0:1, hf * 512:(hf + 1) * 512], w=WS)
                k.op("dve", lambda e: e.tensor_tensor(out=rowA[:, hf * 512:(hf + 1) * 512], in0=ps[zb[hf]][0:1, :], in1=rowA[:, hf * 512:(hf + 1) * 512], op=ALU.mult),
                     r=WS + [psr[zb[hf]]], w=WS)
                k.op("act", lambda e: e.activation(out=rowB, in_=rowA[:, hf * 512:(hf + 1) * 512], func=AF.Exp, scale=-1.0), r=WS, w=WS)
                k.op("act", lambda e: e.activation(out=rowB, in_=rowB, func=AF.Ln, bias=ones_f[0:1, 0:1]), r=WS + [r_const], w=WS)
                k.op("act", lambda e: e.activation(out=rowB, in_=rowB, func=AF.Exp, scale=-1.0), r=WS, w=WS)
                k.op("dve", lambda e: e.tensor_tensor(out=rowA[:, hf * 512:(hf + 1) * 512], in0=rowA[:, hf * 512:(hf + 1) * 512], in1=rowB, op=ALU.mult), r=WS, w=WS)
            k.op("dve", lambda e: e.tensor_tensor(out=rowB, in0=rowA[:, 0:512], in1=rowA[:, 512:1024], op=ALU.mult), r=WS, w=WS)
            k.op("dve", lambda e: e.tensor_reduce(out=s00[:], in_=rowB.rearrange("p (h d) -> p h d", h=MH), axis=AX.X, op=ALU.add), r=WS, w=[r_s00])

        for seq in range(NSEQ):
            for h in range(MH):
                k.op("pool", lambda e, h=h: e.memset(Cst[:, h, :], 0.0), w=[r_C[h]])
            k.op("pool", lambda e: e.memset(halo[:], 0.0), w=r_halo)
            k.op("pool", lambda e: e.memset(tot[:], 0.0), w=[r_tot])
            k.op("pool", lambda e: e.memset(Mt[:], 0.0), w=[r_Mt])
            for ti in range(NT):
                t0 = ti * T
                kt0 = t0 // 128
                nkt = kt0 + NSUB
                g_idx = seq * NT + ti
                bR, bU = bufs[g_idx % 2], bufs[(g_idx + 1) % 2]
                R, r_R = bR["R"], bR["rR"]
                aT, QT, qcT, kcT = bU["aT"], bU["QT"], bU["qcT"], bU["kcT"]
                r_U1a, r_U1b = bU["rUa"], bU["rUb"]
                k.alias(bU["rR"], r_U1a + r_U1b)
                k.alias(V_ple2, V_att + V_conv)
                k.alias(V_ml + V_hn, V_E)
                norm_a(0)
                norm_a(1)
                norm_b(0, 0)
                norm_a(2)
                norm_b(0, 1)
                norm_a(3)
                norm_b(0, 2)
                norm_b(0, 3)
                gbk = bank("mm")
                for sub in range(NSUB):
                    k.pe([(lambda e, c=c: e.matmul(ps[gbk][:, sub * 16:(sub + 1) * 16], lhsT=hT[:, c, sub * 128:(sub + 1) * 128], rhs=Wg[:, c, :],
                                                   start=(c == 0), stop=(c == 7))) for c in range(8)], r=[r_hT[sub], r_const], w=[psr[gbk]])
                k.op("dve", lambda e: e.tensor_tensor(out=zg[:], in0=ps[gbk][:, 0:64].rearrange("p (s g) -> p s g", s=NSUB),
                                                      in1=gb[:].unsqueeze(1).to_broadcast([128, NSUB, 16]), op=ALU.add), r=[psr[gbk], r_const], w=[r_g])
                k.op("act", lambda e: e.activation(out=e1[:], in_=zg[:], func=AF.Exp, scale=-1.0), r=[r_g], w=[r_g])
                k.op("act", lambda e: e.activation(out=lsp[:], in_=e1[:], func=AF.Ln, bias=ones_f[:, 0:1]), r=[r_g, r_const], w=[r_g])
                k.op("pool", lambda e: e.memset(EK[:], 0.0), w=[r_EK])
                k.op("pool", lambda e: e.memset(EQ[:], 0.0), w=[r_EQ])
                k.op("pool", lambda e: e.memset(EK[:, :, :, 0:3], 1.0), r=[r_EK], w=[r_EK])
                k.op("pool", lambda e: e.memset(EQ[:, :, :, 3:6], 1.0), r=[r_EQ], w=[r_EQ])
                sb = slab_acquire("in0")
                for j in range(4):
                    pb = mm_feat(sb, j)
                    k.op("act", lambda e: e.activation(out=QT[0:64, 2 * j, :], in_=ps[pb][0:64, :], func=AF.Copy, scale=0.125), r=[psr[pb]], w=[r_U1a[2 * j]])
                    k.op("dve", lambda e: e.tensor_scalar(out=QT[0:64, 2 * j + 1, :], in0=ps[pb][64:128, :], scalar1=0.125, scalar2=None, op0=ALU.mult),
                         r=[psr[pb]], w=[r_U1a[2 * j + 1]])
                slab_release(sb)
                cbk = bank("mm")
                for j in range(NSUB):
                    fns = [lambda e, j=j: e.matmul(ps[cbk][:, j * 16:(j + 1) * 16], lhsT=tri_f[:], rhs=lsp[:, j, :], start=True, stop=False)]
                    for i in range(j):
                        fns.append(lambda e, j=j, i=i: e.matmul(ps[cbk][:, j * 16:(j + 1) * 16], lhsT=ones_f[:], rhs=lsp[:, i, :], start=False, stop=False))
                    fns.append(lambda e, j=j: e.matmul(ps[cbk][:, j * 16:(j + 1) * 16], lhsT=ones_f[0:1, :], rhs=tot[0:1, :], start=False, stop=True))
                    k.pe(fns, r=[r_g, r_tot, r_const], w=[psr[cbk]])
                k.op("dve", lambda e: e.tensor_copy(out=cpos[:], in_=ps[cbk][:, 0:64].rearrange("p (s g) -> p s g", s=NSUB)), r=[psr[cbk]], w=[r_g])
                tbk = bank("mm")
                fns = [(lambda e, i=i: e.matmul(ps[tbk][0:1, 0:16], lhsT=ones_f[:, 0:1], rhs=lsp[:, i, :], start=(i == 0), stop=False)) for i in range(NSUB)]
                fns.append(lambda e: e.matmul(ps[tbk][0:1, 0:16], lhsT=ones_f[0:1, 0:1], rhs=tot[0:1, :], start=False, stop=True))
                k.pe(fns, r=[r_g, r_tot, r_const], w=[psr[tbk]])
                k.op("dve", lambda e: e.tensor_copy(out=tot[:], in_=ps[tbk][0:1, 0:16]), r=[psr[tbk]], w=[r_tot])
                sb = slab_acquire("in512")
                for j in range(4):
                    pb = mm_feat(sb, j)
                    k.op("act", lambda e: e.activation(out=Kc[0:64, 2 * j, t0:t0 + T], in_=ps[pb][0:64, :], func=AF.Copy), r=[psr[pb]], w=[r_Kc[2 * j]])
                    k.op("dve", lambda e: e.tensor_copy(out=Kc[0:64, 2 * j + 1, t0:t0 + T], in_=ps[pb][64:128, :]), r=[psr[pb]], w=[r_Kc[2 * j + 1]])
                slab_release(sb)
                cf = cpos[:, :, 0:8]
                r1v = r1t[:, 0:32].rearrange("p (s h) -> p s h", s=NSUB)
                r2v = r1t[:, 32:64].rearrange("p (s h) -> p s h", s=NSUB)
                k.op("dve", lambda e: e.tensor_copy(out=hml[:, 0], in_=cf), r=[r_g], w=[r_sm])
                k.op("dve", lambda e: e.tensor_tensor(out=r1v, in0=cf, in1=hml[:, 0], op=ALU.subtract), r=[r_g, r_sm], w=[r_sm])
                k.op("dve", lambda e: e.tensor_copy(out=hml[:, 1], in_=r1v), r=[r_sm], w=[r_sm])
                k.op("dve", lambda e: e.tensor_tensor(out=r2v, in0=r1v, in1=hml[:, 1], op=ALU.subtract), r=[r_sm], w=[r_sm])
                k.op("dve", lambda e: e.tensor_copy(out=hml[:, 2], in_=r2v), r=[r_sm], w=[r_sm])
                for i3 in range(3):
                    k.op("dve", lambda e: e.tensor_copy(out=EK[:, :, :, 3 + i3], in_=hml[:, i3]), r=[r_sm, r_EK], w=[r_EK])
                    k.op("dve", lambda e: e.tensor_scalar(out=EQ[:, :, :, i3], in0=hml[:, i3], scalar1=-1.0, scalar2=None, op0=ALU.mult), r=[r_sm, r_EQ], w=[r_EQ])
                k.op("dve", lambda e: e.tensor_tensor(out=alpha[:], in0=zg[:, :, 8:12], in1=cpos[:, :, 12:16], op=ALU.add), r=[r_g], w=[r_g])
                abk = bank("mm")
                k.pe([(lambda e, j=j: e.matmul(ps[abk][0:4, j * 128:(j + 1) * 128], lhsT=alpha[:, j, :], rhs=ident_f[:], start=True, stop=True)) for j in range(NSUB)],
                     r=[r_g, r_const], w=[psr[abk]])
                k.op("dve", lambda e: e.tensor_reduce(out=A4[:], in_=ps[abk][0:4, :].rearrange("p (s t) -> p s t", s=NSUB), axis=AX.X, op=ALU.max),
                     r=[psr[abk]], w=[r_Mt])
                for j in range(NSUB):
                    k.op("dve", lambda e: e.tensor_tensor(out=Mt[:, j + 1:j + 2], in0=Mt[:, j:j + 1], in1=A4[:, j:j + 1], op=ALU.max), r=[r_Mt], w=[r_Mt])
                k.op("dve", lambda e: e.tensor_tensor(out=wdec[:], in0=Mt[:, 0:4], in1=Mt[:, 1:5], op=ALU.subtract), r=[r_Mt], w=[r_Mt])
                k.op("act", lambda e: e.activation(out=wdec[:], in_=wdec[:], func=AF.Exp), r=[r_Mt], w=[r_Mt])
                k.op("dve", lambda e: e.tensor_tensor(out=DG[:, 0:16].rearrange("p (j h) -> p j h", j=NSUB), in0=Mt[:, 1:5].unsqueeze(2).to_broadcast([4, NSUB, 4]),
                                                      in1=ident_f[0:4, 0:4].unsqueeze(1).to_broadcast([4, NSUB, 4]), op=ALU.mult), r=[r_Mt, r_const], w=[r_Mt])
                k.op("dve", lambda e: e.tensor_tensor(out=DG[:, 16:32].rearrange("p (j h) -> p j h", j=NSUB), in0=wdec[:].unsqueeze(2).to_broadcast([4, NSUB, 4]),
                                                      in1=ident_f[0:4, 0:4].unsqueeze(1).to_broadcast([4, NSUB, 4]), op=ALU.mult), r=[r_Mt, r_const], w=[r_Mt])
                k.op("dve", lambda e: e.tensor_copy(out=Mt[:, 0:1], in_=Mt[:, 4:5]), r=[r_Mt], w=[r_Mt])
                bbk = bank("mm")
                k.pe([lambda e: e.matmul(ps[bbk][:, 0:32], lhsT=ones_f[0:4, :], rhs=DG[:], start=True, stop=True)], r=[r_Mt, r_const], w=[psr[bbk]])
                k.op("dve", lambda e: e.tensor_copy(out=mbc[:], in_=ps[bbk][:, 0:32]), r=[psr[bbk]], w=[r_g])
                mbM = mbc[:, 0:16].rearrange("p (j h) -> p j h", j=NSUB)
                k.op("dve", lambda e: e.tensor_tensor(out=d1[:], in0=alpha[:], in1=mbM, op=ALU.subtract), r=[r_g], w=[r_g])
                k.op("act", lambda e: e.activation(out=es_t[:], in_=d1[:], func=AF.Exp, bias=lnsc[:]), r=[r_g, r_const], w=[r_g])
                k.op("dve", lambda e: e.tensor_tensor(out=d1[:], in0=cpos[:, :, 12:16], in1=mbM, op=ALU.subtract), r=[r_g], w=[r_g])
                k.op("act", lambda e: e.activation(out=clampv[:], in_=d1[:], func=AF.Exp), r=[r_g], w=[r_g])
                sb = slab_acquire("in1024")
                for sub in range(NSUB):
                    pb = mm_tok(sb, sub, hT, [r_hT[sub]])
                    if sub % 2 == 0:
                        k.op("act", lambda e: e.activation(out=Vc[:, kt0 + sub, :, 0:64], in_=ps[pb][:].rearrange("p (h d) -> p h d", h=FH), func=AF.Copy), r=[psr[pb]], w=[r_Vc])
                    else:
                        k.op("dve", lambda e: e.tensor_copy(out=Vc[:, kt0 + sub, :, 0:64], in_=ps[pb][:].rearrange("p (h d) -> p h d", h=FH)), r=[psr[pb]], w=[r_Vc])
                slab_release(sb)
                exk = [bank("all") for _ in range(3)]
                exq = [bank("all") for _ in range(3)]
                hgroups = [(0, 3), (3, 3), (6, 2)]
                for EX, rEX, banks in ((EK, r_EK, exk), (EQ, r_EQ, exq)):
                    for gi, (h0, n) in enumerate(hgroups):
                        k.pe([(lambda e, sub=sub: e.matmul(ps[banks[gi]][0:n * 32, sub * 128:(sub + 1) * 128], lhsT=EX[:, sub, h0:h0 + n, :].rearrange("p h e -> p (h e)"),
                                                           rhs=ident_b[:], start=True, stop=True)) for sub in range(NSUB)], r=[rEX, r_const], w=[psr[banks[gi]]])
                for h in range(FH):
                    gi, row = h // 3, 32 * (h % 3)
                    k.op("dve", lambda e: e.tensor_copy(out=Kc[64:70, h, t0:t0 + T], in_=ps[exk[gi]][row:row + 6, :]), r=[psr[exk[gi]]], w=[r_Kc[h]])
                    k.op("act", lambda e: e.activation(out=QT[64:70, h, :], in_=ps[exq[gi]][row:row + 6, :], func=AF.Copy), r=[psr[exq[gi]]], w=[r_U1a[h]])

                if ti == 0:
                    tok0_precise()
                k.alias(V_hn, r_ktm)
                k.alias(V_E, r_vx)
                bg = []
                slab_of = {}

                def conv_a(which, col0, c4):
                    if c4 == 0:
                        slab_of[col0] = slab_acquire(f"in{col0}")
                    sbx = slab_of[col0]
                    ch = which * 4 + c4
                    pb = mm_feat(sbx, c4)
                    if c4 == 3:
                        slab_release(sbx)
                    cb = ch % 2
                    acc = cv_acc[cb]
                    k.op("dve", lambda e: e.tensor_scalar(out=acc, in0=ps[pb][:, 0:512], scalar1=wcv[:, ch, 3:4], scalar2=None, op0=ALU.mult),
                         r=[psr[pb], r_const], w=[r_cvacc[cb]])
                    for jj in (2, 1, 0):
                        sh = 3 - jj
                        k.op("dve", lambda e: e.scalar_tensor_tensor(out=acc[:, sh:512], in0=ps[pb][:, 0:512 - sh], scalar=wcv[:, ch, jj:jj + 1], in1=acc[:, sh:512],
                                                                     op0=ALU.mult, op1=ALU.add), r=[psr[pb], r_cvacc[cb], r_const], w=[r_cvacc[cb]])
                        k.op("dve", lambda e: e.scalar_tensor_tensor(out=acc[:, 0:sh], in0=halo[:, ch, 3 - sh:3], scalar=wcv[:, ch, jj:jj + 1], in1=acc[:, 0:sh],
                                                                     op0=ALU.mult, op1=ALU.add), r=[r_halo[ch], r_cvacc[cb], r_const], w=[r_cvacc[cb]])
                    k.op("dve", lambda e: e.tensor_copy(out=halo[:, ch, :], in_=ps[pb][:, 509:512]), r=[psr[pb]], w=[r_halo[ch]])

                def conv_b(which, dst, rr_off, c4):
                    ch = which * 4 + c4
                    cb = ch % 2
                    acc = cv_acc[cb]
                    etmp = Ssb[:, cb, :, :].rearrange("p h t -> p (h t)")
                    k.op("act", lambda e: e.activation(out=etmp, in_=acc, func=AF.Exp, scale=-1.0), r=[r_cvacc[cb]], w=[r_Ssb[cb]])
                    k.op("act", lambda e: e.activation(out=etmp, in_=etmp, func=AF.Ln, bias=ones_f[:, 0:1]), r=[r_Ssb[cb], r_const], w=[r_Ssb[cb]])
                    k.op("act", lambda e: e.activation(out=etmp, in_=etmp, func=AF.Exp, scale=-1.0), r=[r_Ssb[cb]], w=[r_Ssb[cb]])
                    k.op("dve", lambda e: e.tensor_tensor(out=dst[:, c4, :], in0=acc, in1=etmp, op=ALU.mult), r=[r_cvacc[cb], r_Ssb[cb]], w=[r_U1b[rr_off + c4]])

                def mv_unit(sub):
                    if sub == 0:
                        slab_of["mv"] = slab_acquire("in2568")
                    pb = mm_tok(slab_of["mv"], sub, hT, [r_hT[sub]])
                    if sub == NSUB - 1:
                        slab_release(slab_of["mv"])
                    k.op("pool", lambda e: e.memset(vx[:, sub, :, 128:129], 1.0), w=[r_vx[sub]])
                    k.op("dve", lambda e: e.tensor_copy(out=vx[:, sub, :, 0:128], in_=ps[pb][:].rearrange("p (h d) -> p h d", h=MH)), r=[psr[pb]], w=[r_vx[sub]])
                    k.op("dve", lambda e: e.tensor_tensor(out=vx[:, sub, :, :], in0=vx[:, sub, :, :], in1=es_t[:, sub, :].unsqueeze(2).to_broadcast([128, MH, 129]), op=ALU.mult),
                         r=[r_vx[sub], r_g], w=[r_vx[sub]])

                def mo_unit(c4):
                    if c4 == 0:
                        slab_of["mo"] = slab_acquire("in3088")
                    pb = mm_feat(slab_of["mo"], c4)
                    if c4 == 3:
                        slab_release(slab_of["mo"])
                    k.op("act", lambda e: e.activation(out=sgo[:, c4, :], in_=ps[pb][:], func=AF.Exp, scale=-1.0), r=[psr[pb]], w=[r_sgo[c4]])
                    k.op("act", lambda e: e.activation(out=sgo[:, c4, :], in_=sgo[:, c4, :], func=AF.Ln, bias=ones_f[:, 0:1]), r=[r_sgo[c4], r_const], w=[r_sgo[c4]])
                    k.op("act", lambda e: e.activation(out=sgo[:, c4, :], in_=sgo[:, c4, :], func=AF.Exp, scale=-1.0), r=[r_sgo[c4]], w=[r_sgo[c4]])
                    k.op("dve", lambda e: e.tensor_scalar(out=sgo[:, c4, :], in0=sgo[:, c4, :], scalar1=gcol[:, 28 + c4:29 + c4], scalar2=None, op0=ALU.mult),
                         r=[r_sgo[c4], r_const], w=[r_sgo[c4]])

                ua, ub_ = [], []
                for which, col0, dst, rr_off in ((0, 1544, qcT, 0), (1, 2056, kcT, 4)):
                    for c4 in range(4):
                        ua.append(lambda which=which, col0=col0, c4=c4: conv_a(which, col0, c4))
                        ub_.append(lambda which=which, dst=dst, rr_off=rr_off, c4=c4: conv_b(which, dst, rr_off, c4))
                bg.append(ua[0])
                for n in range(1, 8):
                    bg.append(ua[n])
                    bg.append(ub_[n - 1])
                bg.append(ub_[7])
                for sub in range(NSUB):
                    bg.append(lambda sub=sub: mv_unit(sub))
                post_units = [lambda c4=c4: mo_unit(c4) for c4 in range(4)]


                pairs = [(h, kt) for h in range(FH) for kt in range(nkt)]
                sbank = {}
                pvbank = {}

                def lo_of(kt):
                    return max(0, (kt - kt0) * 128)

                def emit_S(idx):
                    h, kt = pairs[idx]
                    b = bank("st")
                    sbank[idx] = b
                    lo = lo_of(kt)
                    k.pe([lambda e: e.matmul(ps[b][:, lo:T], lhsT=Kc[0:70, h, kt * 128:(kt + 1) * 128], rhs=QT[0:70, h, lo:T], start=True, stop=True)],
                         r=[r_Kc[h], r_U1a[h]], w=[psr[b]])

                def epilogue(h):
                    pvb = pvbank[h]
                    k.op("act", lambda e: e.activation(out=sqx[0:65, :], in_=ps[pvb][0:65, :], func=AF.Square), r=[psr[pvb]], w=[r_sqx])
                    rb = bank("mm")
                    k.pe([lambda e: e.matmul(ps[rb][0:64, :], lhsT=Amat[0:65, :], rhs=sqx[0:65, :], start=True, stop=True)], r=[r_sqx, r_const], w=[psr[rb]])
                    k.op("act", lambda e: e.activation(out=lnr[0:64, :], in_=ps[rb][0:64, :], func=AF.Ln), r=[psr[rb]], w=[r_lnr])
                    k.op("act", lambda e: e.activation(out=lnr[0:64, :], in_=lnr[0:64, :], func=AF.Exp, scale=-0.5), r=[r_lnr], w=[r_lnr])
                    po = (h % 2) * 64
                    k.op("dve", lambda e: e.scalar_tensor_tensor(out=mixT[po:po + 64, h // 2, :], in0=ps[pvb][0:64, :], scalar=gfox[0:64, h:h + 1], in1=lnr[0:64, :],
                                                                 op0=ALU.mult, op1=ALU.mult), r=[psr[pvb], r_lnr, r_const], w=[r_mix[h // 2]])

                def attn_range(p0, p1, with_bg):
                    for i_ in range(p0, min(p0 + 2, p1)):
                        emit_S(i_)
                    for idx in range(p0, p1):
                        h, kt = pairs[idx]
                        if idx + 2 < p1:
                            emit_S(idx + 2)
                        b = sbank[idx]
                        lo = lo_of(kt)
                        pi = idx % NPT
                        k.op("act", lambda e: e.activation(out=Pt[pi][:, lo:T], in_=ps[b][:, lo:T], func=AF.Exp), r=[psr[b]], w=[r_Pt[pi]])
                        if kt >= kt0:
                            k.op("dve", lambda e: e.tensor_tensor(out=Pt[pi][:, lo:lo + 128], in0=Pt[pi][:, lo:lo + 128], in1=maskT[:], op=ALU.mult),
                                 r=[r_Pt[pi], r_const], w=[r_Pt[pi]])
                        if kt == 0:
                            pvbank[h] = bank("pv")
                        pvb = pvbank[h]
                        vo = (kt * FH + h) * 65
                        k.pe([lambda e: e.matmul(ps[pvb][:, lo:T], lhsT=Vcf[:, vo:vo + 128], rhs=Pt[pi][:, lo:T], start=(kt == 0), stop=(kt == nkt - 1))],
                             r=[r_Pt[pi], r_Vc], w=[psr[pvb]])
                        if kt == nkt - 1:
                            epilogue(h)
                        if with_bg and bg and (idx + 1) % bg_stride == 0:
                            bg.pop(0)()

                p_mid = (FH // 2) * nkt
                bg_stride = max(1, p_mid // (len(bg) + 2))
                attn_range(0, p_mid, True)
                while bg:
                    bg.pop(0)()
                for u in post_units:
                    u()
                st1 = {}
                st2 = {}

                def ml_stage1(j):
                    js = slice(j * 128, (j + 1) * 128)
                    jb = j % 2
                    ktb = 0
                    k.pe([(lambda e, h=h: e.transpose(out=psb[ktb][:, h * 128:(h + 1) * 128], in_=kcT[:, h, js], identity=ident_b[:])) for h in range(MH)],
                         r=r_U1b[4:8] + [r_const], w=[psr[ktb]])
                    k.op("dve", lambda e: e.tensor_copy(out=ktm[:, j, :, :], in_=psb[ktb][:, 0:512].rearrange("p (h d) -> p h d", h=MH)), r=[psr[ktb]], w=[r_ktm[j]])
                    stb = 1
                    k.pe([(lambda e, h=h: e.matmul(ps[stb][:, h * 128:(h + 1) * 128], lhsT=kcT[:, h, js], rhs=qcT[:, h, js], start=True, stop=True)) for h in range(MH)],
                         r=r_U1b, w=[psr[stb]])
                    k.op("dve", lambda e: e.tensor_tensor(out=Ssb[:, jb, :, :], in0=ps[stb][:].rearrange("p (h t) -> p h t", h=MH),
                                                          in1=maskT[:].unsqueeze(1).to_broadcast([128, MH, 128]), op=ALU.mult), r=[psr[stb], r_const], w=[r_Ssb[jb]])
                    if ti == 0 and j == 0:
                        k.op("dve", lambda e: e.tensor_copy(out=Ssb[0:1, jb, :, 0:1], in_=s00[:].unsqueeze(2)), r=[r_s00, r_Ssb[jb]], w=[r_Ssb[jb]])
                    ub = [2, 1]
                    k.pe([(lambda e, h=h: e.matmul(ps[2][:, h * 129:(h + 1) * 129], lhsT=ktm[:, j, h, :], rhs=vx[:, j, h, :], start=True, stop=True)) for h in range(3)],
                         r=[r_ktm[j], r_vx[j]], w=[psr[2]])
                    k.pe([lambda e: e.matmul(ps[1][:, 0:129], lhsT=ktm[:, j, 3, :], rhs=vx[:, j, 3, :], start=True, stop=True)], r=[r_ktm[j], r_vx[j]], w=[psr[1]])
                    st1[j] = ub

                def ml_stage2(j):
                    js = slice(j * 128, (j + 1) * 128)
                    jb = j % 2
                    ub = st1[j]
                    wbc = mbc[:, 16 + 4 * j:20 + 4 * j].unsqueeze(2).to_broadcast([128, MH, 129])
                    k.op("dve", lambda e: e.tensor_tensor(out=Cst[:], in0=Cst[:], in1=wbc, op=ALU.mult), r=r_C + [r_g], w=r_C)
                    k.op("dve", lambda e: e.tensor_copy(out=Csb[:], in_=Cst[:]), r=r_C, w=r_Csb)
                    k.op("dve", lambda e: e.tensor_tensor(out=Cst[:, 0:3, :], in0=Cst[:, 0:3, :], in1=ps[2][:, 0:387].rearrange("p (h d) -> p h d", h=3), op=ALU.add),
                         r=r_C + [psr[2]], w=r_C)
                    k.op("dve", lambda e: e.tensor_tensor(out=Cst[:, 3, :], in0=Cst[:, 3, :], in1=ps[1][:, 0:129], op=ALU.add), r=r_C + [psr[1]], w=r_C)
                    nb = [0, 1]
                    for hb in range(2):
                        fns = []
                        for hh in range(2):
                            h = 2 * hb + hh
                            fns.append(lambda e, h=h, hh=hh: e.matmul(ps[nb[hb]][:, hh * 129:(hh + 1) * 129], lhsT=Ssb[:, jb, h, :], rhs=vx[:, j, h, :], start=True, stop=False))
                            fns.append(lambda e, h=h, hh=hh: e.matmul(ps[nb[hb]][:, hh * 129:(hh + 1) * 129], lhsT=qcT[:, h, js], rhs=Csb[:, h, :], start=False, stop=True))
                        k.pe(fns, r=[r_Ssb[jb], r_vx[j]] + r_U1b[0:4] + r_Csb, w=[psr[nb[hb]]])
                    st2[j] = nb

                def ml_stage3a(j):
                    jb = j % 2
                    nb = st2[j]
                    smj = sm[:, jb, :]
                    k.op("dve", lambda e: e.memset(smj[:, 0:4], 0.0), w=[r_smj[jb]])
                    for h in range(MH):
                        hb, hh = h // 2, h % 2
                        k.op("act", lambda e: e.activation(out=ybf[:, jb, h, :], in_=ps[nb[hb]][:, hh * 129:hh * 129 + 128], func=AF.Square, scale=float(128 ** -0.5),
                                                           accum_out=smj[:, h:h + 1]), r=[psr[nb[hb]], r_smj[jb]], w=[r_ybf[jb], r_smj[jb]])
                    for hb in range(2):
                        k.op("dve", lambda e: e.tensor_copy(out=smj[:, 4 + 2 * hb:6 + 2 * hb], in_=ps[nb[hb]][:, 0:258].rearrange("p (h d) -> p h d", h=2)[:, :, 128]),
                             r=[psr[nb[hb]], r_smj[jb]], w=[r_smj[jb]])
                    k.op("dve", lambda e: e.scalar_tensor_tensor(out=smj[:, 8:12], in0=smj[:, 4:8], scalar=-1.0, in1=smj[:, 4:8], op0=ALU.mult, op1=ALU.max), r=[r_smj[jb]], w=[r_smj[jb]])
                    k.op("dve", lambda e: e.tensor_tensor(out=smj[:, 12:16], in0=smj[:, 8:12], in1=clampv[:, j, :], op=ALU.max), r=[r_smj[jb], r_g], w=[r_smj[jb]])
                    k.op("dve", lambda e: e.scalar_tensor_tensor(out=smj[:, 16:20], in0=smj[:, 12:16], scalar=EPS, in1=smj[:, 12:16], op0=ALU.mult, op1=ALU.mult), r=[r_smj[jb]], w=[r_smj[jb]])
                    k.op("dve", lambda e: e.tensor_tensor(out=smj[:, 20:24], in0=smj[:, 16:20], in1=smj[:, 0:4], op=ALU.add), r=[r_smj[jb]], w=[r_smj[jb]])
                    k.op("act", lambda e: e.activation(out=smj[:, 24:28], in_=smj[:, 20:24], func=AF.Ln), r=[r_smj[jb]], w=[r_smj[jb]])
                    k.op("act", lambda e: e.activation(out=smj[:, 28:32], in_=smj[:, 24:28], func=AF.Exp, scale=-0.5), r=[r_smj[jb]], w=[r_smj[jb]])
                    for hb in range(2):
                        k.op("dve", lambda e: e.tensor_tensor(out=ybf[:, jb, 2 * hb:2 * hb + 2, :], in0=ps[nb[hb]][:, 0:258].rearrange("p (h d) -> p h d", h=2)[:, :, 0:128],
                                                              in1=smj[:, 28 + 2 * hb:30 + 2 * hb].unsqueeze(2).to_broadcast([128, 2, 128]), op=ALU.mult),
                             r=[psr[nb[hb]], r_smj[jb]], w=[r_ybf[jb]])

                def ml_stage3b(j):
                    js = slice(j * 128, (j + 1) * 128)
                    jb = j % 2
                    yb = 2
                    k.pe([(lambda e, h=h: e.transpose(out=psb[yb][:, h * 128:(h + 1) * 128], in_=ybf[:, jb, h, :], identity=ident_b[:])) for h in range(MH)],
                         r=[r_ybf[jb], r_const], w=[psr[yb]])
                    k.op("dve", lambda e: e.tensor_tensor(out=mixT[:, 4:8, js], in0=psb[yb][:, 0:512].rearrange("p (h t) -> p h t", h=MH), in1=sgo[:, :, js], op=ALU.mult),
                         r=[psr[yb]] + r_sgo, w=r_mix[4:8])

                for j in range(NSUB):
                    ml_stage1(j)
                    ml_stage2(j)
                    ml_stage3a(j)
                    ml_stage3b(j)
                dump("mixml", mixT[:, 4:8, :], r_mix[4:8])
                attn_range(p_mid, len(pairs), False)
                dump("mixfox", mixT[:, 0:4, :], r_mix[0:4])


                k.alias(V_ml, V_hn)
                sbo = [slab_acquire(f"out{half}") for half in range(2)]

                def wout_stage(sub):
                    for half in range(2):
                        pb = mm_tok(sbo[half], sub, mixT, r_mix)
                        k.op("dve", lambda e: e.tensor_tensor(out=R[:, sub, half * 512:(half + 1) * 512], in0=R[:, sub, half * 512:(half + 1) * 512],
                                                              in1=ps[pb][:], op=ALU.add), r=[psr[pb], r_R[sub]], w=[r_R[sub]])

                stage_then_norm(wout_stage, 1, after_stage=lambda: [slab_release(b) for b in sbo])
                dump("x1", R[:], r_R)
                def up_stage(qd):
                    ab = qd % 2
                    r_aT = r_U1a if ab == 0 else r_U1b
                    for sl in range(2):
                        sb = slab_acquire(f"up{qd * 1024 + sl * 512}")
                        for c4 in range(4):
                            pb = mm_feat(sb, c4)
                            if c4 == 3:
                                slab_release(sb)
                            kk = sl * 4 + c4
                            k.op("act", lambda e: e.activation(out=aT[:, ab, kk, :], in_=ps[pb][:], func=AF.Relu), r=[psr[pb]], w=[r_aT[kk]])
                            k.op("dve", lambda e: e.tensor_tensor(out=aT[:, ab, kk, :], in0=aT[:, ab, kk, :], in1=aT[:, ab, kk, :], op=ALU.mult), r=[r_aT[kk]], w=[r_aT[kk]])

                def down_one(qd, sbx, sub, half):
                    ab = qd % 2
                    r_aT = r_U1a if ab == 0 else r_U1b
                    pb = mm_tok(sbx, sub, aT[:, ab], r_aT)
                    k.op("dve", lambda e: e.tensor_tensor(out=R[:, sub, half * 512:(half + 1) * 512], in0=R[:, sub, half * 512:(half + 1) * 512],
                                                          in1=ps[pb][:], op=ALU.add), r=[psr[pb], r_R[sub]], w=[r_R[sub]])

                up_stage(0)
                for qd in range(4):
                    if qd + 1 < 4:
                        up_stage(qd + 1)
                    if qd < 3:
                        for half in range(2):
                            sbx = slab_acquire(f"down{qd}_{half}")
                            for sub in range(NSUB):
                                down_one(qd, sbx, sub, half)
                            slab_release(sbx)
                    else:
                        sbd = [slab_acquire(f"down{qd}_{half}") for half in range(2)]

                        def down_stage(sub):
                            for half in range(2):
                                down_one(3, sbd[half], sub, half)

                        stage_then_norm(down_stage, 2, after_stage=lambda: [slab_release(b) for b in sbd])
                dump("x2", R[:], r_R)
                if g_idx + 1 < NSEQ * NT:
                    nseq, nti = divmod(g_idx + 1, NT)
                    k.alias(r_U1a + r_U1b, bU["rR"])
                    k.dma("pool", "xin", bU["R"], x_d[nseq, nti * T:(nti + 1) * T, :].rearrange("(j p) d -> p j d", p=128), w=bU["rR"])
                k.alias(V_conv + V_att, V_ple2)
                k.dma("pool", "pin", p32, p_d[seq, t0:t0 + T, :].rearrange("(j p) d -> p j d", p=128), w=[r_p])
                k.op("dve", lambda e: e.tensor_copy(out=pbf, in_=p32), r=[r_p], w=[r_p])
                for sub in range(NSUB):
                    pb = bank("mm")
                    k.pe([(lambda e, c=c: e.transpose(out=psb[pb][:, c * 128:(c + 1) * 128], in_=pbf[:, sub, c * 128:(c + 1) * 128], identity=ident_b[:])) for c in range(2)],
                         r=[r_p, r_const], w=[psr[pb]])
                    k.op("dve", lambda e: e.tensor_copy(out=pT[:, :, sub * 128:(sub + 1) * 128], in_=psb[pb][:, 0:256].rearrange("p (c t) -> p c t", c=2)),
                         r=[psr[pb]], w=[r_p])
                sbg = [slab_acquire(f"pg{half}") for half in range(2)]

                def ple_stage(sub):
                    for half in range(2):
                        gbk2 = mm_tok(sbg[half], sub, hT, [r_hT[sub]])
                        ebk = bank("mm")
                        k.pe([(lambda e, c=c: e.matmul(ps[ebk][:], lhsT=pT[:, c, sub * 128:(sub + 1) * 128], rhs=pleW[:, c, half * 512:(half + 1) * 512],
                                                        start=(c == 0), stop=(c == 1))) for c in range(2)], r=[r_p, r_pleW], w=[psr[ebk]])
                        si = half
                        k.op("act", lambda e: e.activation(out=sig[si], in_=ps[gbk2][:], func=AF.Exp, scale=-1.0), r=[psr[gbk2]], w=[r_sig[si]])
                        k.op("act", lambda e: e.activation(out=sig[si], in_=sig[si], func=AF.Ln, bias=ones_f[:, 0:1]), r=[r_sig[si], r_const], w=[r_sig[si]])
                        k.op("act", lambda e: e.activation(out=sig[si], in_=sig[si], func=AF.Exp, scale=-1.0), r=[r_sig[si]], w=[r_sig[si]])
                        k.op("dve", lambda e: e.tensor_tensor(out=sig[si], in0=sig[si], in1=ps[ebk][:], op=ALU.mult), r=[r_sig[si], psr[ebk]], w=[r_sig[si]])
                        k.op("dve", lambda e: e.tensor_tensor(out=R[:, sub, half * 512:(half + 1) * 512], in0=R[:, sub, half * 512:(half + 1) * 512],
                                                              in1=sig[si], op=ALU.add), r=[r_sig[si], r_R[sub]], w=[r_R[sub]])

                def final_stage(sub):
                    k.op("dve", lambda e: e.memset(ms2[:, sub:sub + 1], 0.0), w=[r_ms2[sub]])
                    k.op("act", lambda e: e.activation(out=junkF, in_=R[:, sub, :], func=AF.Square, scale=1.0 / 32, accum_out=ms2[:, sub:sub + 1]),
                         r=[r_R[sub], r_ms2[sub]], w=[r_cvacc[1], r_ms2[sub]])
                    k.op("act", lambda e: e.activation(out=ms2[:, 4 + sub:5 + sub], in_=ms2[:, sub:sub + 1], func=AF.Ln, bias=epsc[:]), r=[r_ms2[sub], r_const], w=[r_ms2[sub]])
                    k.op("act", lambda e: e.activation(out=rstd2[:, sub:sub + 1], in_=ms2[:, 4 + sub:5 + sub], func=AF.Exp, scale=-0.5), r=[r_ms2[sub]], w=[r_ms2[sub]])
                    k.op("dve", lambda e: e.scalar_tensor_tensor(out=R[:, sub, :], in0=R[:, sub, :], scalar=rstd2[:, sub:sub + 1], in1=gfin[:], op0=ALU.mult, op1=ALU.mult),
                         r=[r_R[sub], r_ms2[sub], r_const], w=[r_R[sub]])
                    k.dma("pool", f"yout{sub}", y_d[seq, t0 + sub * 128:t0 + (sub + 1) * 128, :], R[:, sub, :], r=[r_R[sub]])

                ple_stage(0)
                ple_stage(1)
                final_stage(0)
                ple_stage(2)
                final_stage(1)
                ple_stage(3)
                for b_ in sbg:
                    slab_release(b_)
                final_stage(2)
                final_stage(3)

        k.finalize(final_wait_streams=("yout0", "yout1", "yout2", "yout3", "dbg"))
        print("instructions:", k.nins, "waits:", k.nwait, "counts:", k.cnt, "sim_us: %.1f" % k.sim_time)
    return nc


_NC_CACHE = {}


def _prep_shared(inp):
    f = lambda a: np.ascontiguousarray(np.asarray(a, dtype=np.float32))
    w_in = f(inp["w_in"][0])
    col = lambda g: f(g).reshape(8, 128).T
    gout = np.concatenate([f(inp["g_fox_out"][0]), f(inp["g_mlstm_out"][0])])
    sh = {
        "w_in": w_in,
        "w_out": f(inp["w_out"][0]),
        "w_up": f(inp["w_up"][0]),
        "w_down": f(inp["w_down"][0]),
        "w_ple": f(inp["w_ple"][0]),
        "w_pg": f(inp["w_ple_gate"][0]),
        "wgate": f(np.concatenate([w_in[:, 1536:1544], w_in[:, 3080:3088]], axis=1)),
        "gcol": f(np.concatenate([col(inp["g_mix"][0]), col(inp["g_mlp"][0]), col(inp["g_ple"][0]), col(gout)], axis=1)),
        "gfox": f(f(inp["g_fox_out"][0]).reshape(8, 64).T),
        "gfin": f(inp["g_final"]),
        "gbias": f(np.concatenate([inp["b_fox_f"][0], inp["b_mlstm_i"][0], inp["b_mlstm_f"][0]])),
        "wconv": f(f(inp["w_conv"][0]).reshape(4, 8, 128).transpose(2, 1, 0)),
        "gmixrow": f(f(inp["g_mix"][0]).reshape(1, D)),
        "wc3row": f(f(inp["w_conv"][0])[3].reshape(1, D)),
    }
    return sh


def run(inp, n_cores=8, dbg=None):
    x = np.asarray(inp["x"], dtype=np.float32)
    p = np.asarray(inp["p"], dtype=np.float32)[0]
    B, S, _ = x.shape
    NSEQ = B // n_cores
    key = (NSEQ, S, tuple(sorted(dbg.items())) if dbg else None)
    if key not in _NC_CACHE:
        _NC_CACHE[key] = build_nc(NSEQ, S, dbg)
    nc = _NC_CACHE[key]
    sh = _prep_shared(inp)
    in_maps = []
    for c in range(n_cores):
        m = dict(sh)
        m["x"] = np.ascontiguousarray(x[c * NSEQ:(c + 1) * NSEQ])
        m["p"] = np.ascontiguousarray(p[c * NSEQ:(c + 1) * NSEQ])
        in_maps.append(m)
    res = run_bass_kernel_spmd(nc, in_maps, core_ids=list(range(n_cores)))
    y = np.concatenate([r["y"] for r in res.results], axis=0)
    if dbg:
        return y, res.results
    return y


def kernel(**inputs):
    return run(inputs, 8).astype(np.float32)
```

```python
import heapq
import os
import sys
import types
import numpy as np
import concourse.bass as bass
import concourse.mybir as mybir
from concourse.bass_utils import run_bass_kernel_spmd
from contextlib import ExitStack

F32 = mybir.dt.float32
BF16 = mybir.dt.bfloat16
ALU = mybir.AluOpType
AF = mybir.ActivationFunctionType
AX = mybir.AxisListType

D = 1024
T = 512
NSUB = 4
FH = 8
MH = 4
INC = 3600
DFF = 4096
PLE = 256
EPS = 1e-6
NSLAB = 2
NPT = 3


class Res:
    __slots__ = ("name", "w", "r")

    def __init__(self, name):
        self.name = name
        self.w = None
        self.r = []


def _freeze(fn):
    if fn.__closure__ is None:
        return fn
    cells = tuple(types.CellType(c.cell_contents) for c in fn.__closure__)
    return types.FunctionType(fn.__code__, fn.__globals__, fn.__name__, fn.__defaults__, cells)


class _Probe:
    def __getattr__(self, name):
        def f(*a, **kw):
            out = kw.get("out", a[0] if a else None)
            return out, kw
        return f


def _free_elems(ap):
    n = 1
    for d in ap.shape[1:]:
        n *= int(d)
    return n


_DT_SIZE = {}


class KB:
    ENG = ("pe", "act", "dve", "pool", "sp")

    def __init__(self, nc, es):
        self.nc = nc
        self.es = es
        self.eng = {"pe": nc.tensor, "act": nc.scalar, "dve": nc.vector, "pool": nc.gpsimd, "sp": nc.sync}
        self.sem = {e: es.enter_context(nc.semaphore("s_" + e)) for e in self.ENG}
        self.ops = []
        self.dsem = {}

    def _deps(self, r, w):
        d = set()
        for x in r:
            if x.w is not None:
                d.add(x.w)
        for x in w:
            if x.w is not None:
                d.add(x.w)
            d.update(x.r)
        return d

    def _mark(self, oid, r, w):
        for x in r:
            x.r.append(oid)
        for x in w:
            x.w = oid
            x.r = []

    def _add(self, **kw):
        oid = len(self.ops)
        kw["id"] = oid
        kw["line"] = sys._getframe(2).f_lineno
        self.ops.append(kw)
        return oid

    def op(self, e, fn, r=(), w=(), f=None):
        if f is None:
            out, kw = fn(_Probe())
            f = _free_elems(out)
        base = {"act": 0.20, "dve": 0.12, "pool": 0.25}.get(e, 0.1)
        rate = {"act": 1400.0, "dve": 960.0, "pool": 500.0}.get(e, 1000.0)
        oid = self._add(eng=e, kind="op", fns=[_freeze(fn)], deps=self._deps(r, w), dur=base + f / rate, lat=0.0)
        self._mark(oid, r, w)
        return oid

    def pe(self, fns, r=(), w=(), n=None):
        dur = 0.0
        for fn in fns:
            out, kw = fn(_Probe())
            ni = _free_elems(out)
            passes = 4.0 if ("rhs" in kw and kw["rhs"].dtype == F32) else 1.0
            dur += 0.03 + passes * ni / 2200.0
        oid = self._add(eng="pe", kind="op", fns=[_freeze(fn) for fn in fns], deps=self._deps(r, w), dur=dur, lat=0.0)
        self._mark(oid, r, w)
        return oid

    def dma(self, q, stream, out, in_, r=(), w=(), nbytes=None):
        if stream not in self.dsem:
            self.dsem[stream] = self.es.enter_context(self.nc.semaphore("d_" + stream))
        if nbytes is None:
            nel = 1
            for d in out.shape:
                nel *= int(d)
            nbytes = nel * (2 if out.dtype == BF16 else 4)
        fn = lambda e, out=out, in_=in_: e.dma_start(out=out, in_=in_)
        oid = self._add(eng=q, kind="dma", fns=[fn], deps=self._deps(r, w), dur=(1.0 if q == "pool" else 0.06), lat=2.0 + nbytes / 300000.0, stream=stream)
        self._mark(oid, r, w)
        return oid

    def alias(self, old, new):
        ids = set()
        for o in old:
            if o.w is not None:
                ids.add(o.w)
            ids.update(o.r)
        for n in new:
            n.r = list(set(n.r) | ids)

    def finalize(self, final_wait_streams=()):
        ops = self.ops
        n = len(ops)
        succ = [[] for _ in range(n)]
        ndep = [0] * n
        for o in ops:
            o["deps"].discard(o["id"])
            ndep[o["id"]] = len(o["deps"])
            for d in o["deps"]:
                succ[d].append(o["id"])
        fin = [0.0] * n
        start = [0.0] * n
        efree = {e: 0.0 for e in self.ENG}
        ready_t = [0.0] * n
        heap = []
        for o in ops:
            if ndep[o["id"]] == 0:
                heapq.heappush(heap, (0.0, o["id"]))
        order = []
        while heap:
            est, oid = heapq.heappop(heap)
            o = ops[oid]
            real = max(ready_t[oid], efree[o["eng"]])
            if real > est + 1e-9:
                heapq.heappush(heap, (real, oid))
                continue
            start[oid] = real
            efree[o["eng"]] = real + o["dur"]
            fin[oid] = real + o["dur"] + o["lat"]
            order.append(oid)
            for s_ in succ[oid]:
                lat = 0.04 if ops[s_]["eng"] == o["eng"] else 0.18
                ready_t[s_] = max(ready_t[s_], fin[oid] + lat)
                ndep[s_] -= 1
                if ndep[s_] == 0:
                    heapq.heappush(heap, (max(ready_t[s_], efree[ops[s_]["eng"]]), s_))
        assert len(order) == n, (len(order), n)
        self.sim_time = max(fin) if n else 0.0
        if os.environ.get("KB_ANALYZE"):
            for eng_name in ("pe", "act", "dve"):
                prev_end = 0.0
                busy = 0.0
                blame = {}
                for oid in order:
                    o = ops[oid]
                    if o["eng"] != eng_name:
                        continue
                    gap = start[oid] - prev_end
                    if gap > 0.05 and o["deps"]:
                        d = max(o["deps"], key=lambda d_: fin[d_])
                        key = (ops[d]["eng"], ops[d]["line"], o["line"])
                        blame[key] = blame.get(key, 0.0) + gap
                    busy += o["dur"]
                    prev_end = start[oid] + o["dur"]
                print("ANALYZE %s busy %.0f us of %.0f (%.0f%%)" % (eng_name, busy, self.sim_time, 100 * busy / self.sim_time))
                for key, g in sorted(blame.items(), key=lambda kv: -kv[1])[:14]:
                    print("   idle %.0f us waiting for %s op@line %d (consumer line %d)" % (g, key[0], key[1], key[2]))
        cnt = {e: 0 for e in self.ENG}
        dcnt = {}
        tok = [None] * n
        seen = {e: {} for e in self.ENG}
        streams = {e: [] for e in self.ENG}
        nwait = 0
        for oid in order:
            o = ops[oid]
            e = o["eng"]
            need = {}
            for d in o["deps"]:
                key, val = tok[d]
                if key == "pe" and e == "pe" and ops[d]["kind"] == "op":
                    continue
                if need.get(key, 0) < val:
                    need[key] = val
            for key, val in need.items():
                if seen[e].get(key, 0) >= val:
                    continue
                semh = self.sem[key] if key in self.sem else self.dsem[key[4:]]
                self.eng[e].wait_ge(semh, val)
                seen[e][key] = val
                streams[e].append(("wait", key, val))
                nwait += 1
            ins = None
            for fn in o["fns"]:
                ins = fn(self.eng[e])
            if o["kind"] == "dma":
                st = o["stream"]
                dcnt[st] = dcnt.get(st, 0) + 1
                ins.then_inc(self.dsem[st], 16)
                tok[oid] = ("dma:" + st, 16 * dcnt[st])
                streams[e].append(("inc", "dma:" + st, 16))
            else:
                cnt[e] += 1
                ins.then_inc(self.sem[e], 1)
                tok[oid] = (e, cnt[e])
                streams[e].append(("inc", e, 1))
        for st in final_wait_streams:
            if st in dcnt:
                self.eng["pool"].wait_ge(self.dsem[st], 16 * dcnt[st])
                streams["pool"].append(("wait", "dma:" + st, 16 * dcnt[st]))
        semv = {}
        pos = {e: 0 for e in self.ENG}
        progress = True
        while progress:
            progress = False
            for e in self.ENG:
                st = streams[e]
                while pos[e] < len(st):
                    kind, key, val = st[pos[e]]
                    if kind == "wait":
                        if semv.get(key, 0) < val:
                            break
                    else:
                        semv[key] = semv.get(key, 0) + val
                    pos[e] += 1
                    progress = True
        for e in self.ENG:
            assert pos[e] == len(streams[e]), ("DEADLOCK in emitted program", e, pos[e], len(streams[e]), streams[e][pos[e]])
        self.cnt = cnt
        self.nwait = nwait
        self.nins = sum(len(o["fns"]) for o in ops)


def build_nc(NSEQ, S, dbg=None):
    nc = bass.Bass("TRN2", target_bir_lowering=False)
    NT = S // T
    NKT = S // 128
    di = lambda n, sh: nc.dram_tensor(n, sh, F32, kind="ExternalInput").ap()
    x_d = di("x", [NSEQ, S, D])
    p_d = di("p", [NSEQ, S, PLE])
    w_in_d = di("w_in", [D, INC])
    w_out_d = di("w_out", [D, D])
    w_up_d = di("w_up", [D, DFF])
    w_down_d = di("w_down", [DFF, D])
    w_ple_d = di("w_ple", [PLE, D])
    w_pg_d = di("w_pg", [D, D])
    wgate_d = di("wgate", [D, 16])
    gcol_d = di("gcol", [128, 32])
    gfox_d = di("gfox", [64, 8])
    gfin_d = di("gfin", [D])
    gbias_d = di("gbias", [16])
    wconv_d = di("wconv", [128, 8, 4])
    gmixrow_d = di("gmixrow", [1, D])
    wc3row_d = di("wc3row", [1, D])
    y_d = nc.dram_tensor("y", [NSEQ, S, D], F32, kind="ExternalOutput").ap()
    dbg_d = {}
    if dbg:
        for n, sh in dbg.items():
            dbg_d[n] = nc.dram_tensor("dbg_" + n, sh, F32, kind="ExternalOutput").ap()
    wb = lambda n, sh: nc.dram_tensor(n, sh, BF16, kind="Internal").ap()
    wb_in = wb("wb_in", [D, INC])
    wb_out = wb("wb_out", [D, D])
    wb_up = wb("wb_up", [D, DFF])
    wb_down = wb("wb_down", [DFF, D])
    wb_ple = wb("wb_ple", [PLE, D])
    wb_pg = wb("wb_pg", [D, D])

    es = ExitStack()
    with es:
        es.enter_context(nc.allow_low_precision("bf16 matmul operands / activations are intended (bf16-reference regime)"))
        k = KB(nc, es)
        SB = lambda n, sh, dt: es.enter_context(nc.sbuf_tensor("sb_" + n, sh, dt))
        Kc = SB("Kc", [128, FH, S], BF16)
        Vcf = SB("Vc", [128, NKT * FH * 65 + 64], BF16)
        Vc = Vcf[:, 0:NKT * FH * 65].rearrange("p (k h e) -> p k h e", k=NKT, h=FH)
        Cst = SB("Cst", [128, MH, 129], F32)
        Csb = SB("Csb", [128, MH, 129], BF16)
        BufA = SB("BufA", [128, 8192], BF16)
        hT = SB("hT", [128, 8, T], BF16)
        mixT = SB("mixT", [128, 8, T], BF16)
        slabs = [SB(f"slab{i}", [128, 8, 512], BF16) for i in range(NSLAB)]
        BufB = SB("BufB", [128, 8192], BF16)
        U2 = SB("U2", [128, 2560], F32)
        U3 = SB("U3", [128, 2080], F32)
        sgo = SB("sgo", [128, MH, T], BF16)
        pleW = SB("pleW", [128, 2, D], BF16)
        ginv = SB("ginv", [128, 4], F32)
        hml = SB("hml", [128, 3, NSUB, FH], BF16)
        ident_f = SB("ident_f", [128, 128], F32)
        ident_b = SB("ident_b", [128, 128], BF16)
        tri_f = SB("tri_f", [128, 128], F32)
        ones_f = SB("ones_f", [128, 128], F32)
        maskT = SB("maskT", [128, 128], BF16)
        Amat = SB("Amat", [128, 64], BF16)
        gfin = SB("gfin", [128, D], F32)
        gcol = SB("gcol", [128, 32], F32)
        gfox = SB("gfox", [64, 8], F32)
        gb = SB("gb", [128, 16], F32)
        wcv = SB("wcv", [128, 8, 4], F32)
        Wg = SB("Wg", [128, 8, 16], BF16)
        halo = SB("halo", [128, 8, 3], F32)
        tot = SB("tot", [1, 16], F32)
        Mt = SB("Mt", [4, 8], F32)
        DG = SB("DG", [4, 32], F32)
        A4 = SB("A4", [4, 4], F32)
        wdec = SB("wdec", [4, 4], F32)
        ms = SB("ms", [128, 8], F32)
        ms2 = SB("ms2", [128, 8], F32)
        h0col = SB("h0col", [128, 8], F32)
        s00 = SB("s00", [1, 4], F32)
        rstd2 = SB("rstd2", [128, 4], F32)
        sigb = SB("sigb", [128, 2, 512], F32)
        rstd = SB("rstd", [128, 8], F32)
        epsc = SB("epsc", [128, 1], F32)
        lnsc = SB("lnsc", [128, 1], F32)
        zg = SB("zg", [128, NSUB, 16], F32)
        e1 = SB("e1", [128, NSUB, 16], F32)
        lsp = SB("lsp", [128, NSUB, 16], F32)
        cpos = SB("cpos", [128, NSUB, 16], F32)
        r1t = SB("r1t", [128, 64], F32)
        alpha = SB("alpha", [128, NSUB, 4], F32)
        mbc = SB("mbc", [128, 32], F32)
        d1 = SB("d1", [128, NSUB, 4], F32)
        es_t = SB("es_t", [128, NSUB, 4], F32)
        clampv = SB("clampv", [128, NSUB, 4], F32)
        sm = SB("sm", [128, 2, 32], F32)
        ps = [es.enter_context(nc.psum_tensor(f"ps{i}", [128, 512], F32)) for i in range(8)]
        psr = [Res(f"ps{i}") for i in range(8)]
        psb = [ps[i][:].bitcast(BF16) for i in range(8)]

        def buf_views(X):
            return dict(
                R=X[:].bitcast(F32).rearrange("p (s d) -> p s d", s=NSUB),
                aT=X[:].rearrange("p (b c t) -> p b c t", b=2, c=8),
                QT=X[:, 0:4096].rearrange("p (h t) -> p h t", h=FH),
                qcT=X[:, 4096:6144].rearrange("p (h t) -> p h t", h=MH),
                kcT=X[:, 6144:8192].rearrange("p (h t) -> p h t", h=MH),
                rR=[Res(f"R{s_}") for s_ in range(NSUB)],
                rUa=[Res(f"U1a{h}") for h in range(FH)],
                rUb=[Res(f"U1b{h}") for h in range(8)],
            )

        bufs = [buf_views(BufA), buf_views(BufB)]
        R = bufs[0]["R"]
        r_R = bufs[0]["rR"]
        aT, QT, qcT, kcT = bufs[1]["aT"], bufs[1]["QT"], bufs[1]["qcT"], bufs[1]["kcT"]
        r_U1a, r_U1b = bufs[1]["rUa"], bufs[1]["rUb"]
        U2b = U2[:].bitcast(BF16)
        U3b = U3[:].bitcast(BF16)
        Pt = [U2b[:, i * 512:(i + 1) * 512] for i in range(NPT)]
        sqx = U2b[:, 1536:2048]
        lnr = U2[:, 1024:1536]
        cv_acc = [U2[:, 1536 + i * 512:1536 + (i + 1) * 512] for i in range(2)]
        junkF = U2b[:, 4096:5120]
        p32 = U2[:, 0:1024].rearrange("p (s d) -> p s d", s=NSUB)
        pbf = U2b[:, 2048:3072].rearrange("p (s d) -> p s d", s=NSUB)
        pT = U2b[:, 3072:4096].rearrange("p (c t) -> p c t", c=2)
        ktm = U3b[:, 0:2048].rearrange("p (s h d) -> p s h d", s=NSUB, h=MH)
        vx = U3b[:, 2048:2048 + NSUB * MH * 129].rearrange("p (s h d) -> p s h d", s=NSUB, h=MH)
        hn = U3b[:, 0:2048].rearrange("p (b d) -> p b d", b=2)
        EK = U3b[:, 2048:3072].rearrange("p (s h e) -> p s h e", s=NSUB, h=FH)
        EQ = U3b[:, 3072:4096].rearrange("p (s h e) -> p s h e", s=NSUB, h=FH)
        sig = [sigb[:, i, :] for i in range(2)]
        prod = [U3[:, 1024 + i * 512:1024 + (i + 1) * 512] for i in range(2)]
        Ssb = SB("Ssb", [128, 2, MH, 128], BF16)
        ybf = SB("ybf", [128, 2, MH, 128], BF16)
        print("sbuf bytes remaining:", nc.sbuf_bytes_remaining)

        r_Kc = [Res(f"Kc{h}") for h in range(FH)]
        r_Vc = Res("Vc")
        r_C = [Res(f"C{h}") for h in range(MH)]
        r_Csb = [Res(f"Csb{h}") for h in range(MH)]
        r_hT = [Res(f"hT{s}") for s in range(NSUB)]
        r_mix = [Res(f"mix{c}") for c in range(8)]
        r_slab = [Res(f"slab{i}") for i in range(NSLAB)]
        r_U2 = Res("U2")
        r_cvacc = [Res(f"cvacc{i}") for i in range(2)]
        r_Pt = [Res(f"Pt{i}") for i in range(NPT)]
        r_sqx = Res("sqx")
        r_lnr = Res("lnr")
        r_p = Res("pbufs")
        r_ktm = [Res(f"ktm{s}") for s in range(NSUB)]
        r_vx = [Res(f"vx{s}") for s in range(NSUB)]
        r_hn = [Res("hn0"), Res("hn1")]
        r_sig = [Res("sig0"), Res("sig1")]
        r_prod = [Res("prod0"), Res("prod1")]
        r_sgo = [Res(f"sgo{h}") for h in range(MH)]
        r_EK = Res("EK")
        r_EQ = Res("EQ")
        r_pleW = Res("pleW")
        r_s00 = Res("s00")
        r_h0 = Res("h0col")
        V_E = [r_EK, r_EQ]
        r_const = Res("const")
        r_halo = [Res(f"halo{c}") for c in range(8)]
        r_tot = Res("tot")
        r_Mt = Res("Mt")
        r_g = Res("gates")
        r_ms = Res("ms")
        r_msS = [Res(f"ms{i}") for i in range(NSUB)]
        r_ms2 = [Res(f"ms2_{i}") for i in range(NSUB)]
        r_sm = Res("sm")
        r_smj = [Res("smj0"), Res("smj1")]
        r_Ssb = [Res("Ssb0"), Res("Ssb1")]
        r_ybf = [Res("ybf0"), Res("ybf1")]
        r_w = {n: Res("w_" + n) for n in ["in", "out", "up", "down", "ple", "pg"]}
        V_att = r_Pt + [r_sqx, r_lnr]
        V_conv = r_cvacc
        V_ple2 = [r_p]
        V_hn = r_hn
        V_ml = r_ktm + r_vx
        V_ple3 = r_sig + r_prod

        bank_rr = {"mm": [0, [0, 1, 2]], "st": [0, [3, 4, 5]], "pv": [0, [6, 7]], "all": [0, [0, 1, 2, 3, 4, 5, 6, 7]]}

        def bank(cls):
            st = bank_rr[cls]
            b = st[1][st[0] % len(st[1])]
            st[0] += 1
            return b

        k.dma("pool", "xin", R[:], x_d[0, 0:T, :].rearrange("(j p) d -> p j d", p=128), w=r_R)
        k.dma("pool", "setup2", Wg[:], wgate_d.rearrange("(c p) n -> p c n", p=128), w=[r_const])
        k.dma("sp", "setup", gcol[:], gcol_d, w=[r_const])
        k.dma("sp", "setup", gfox[:], gfox_d, w=[r_const])
        k.dma("sp", "setup", gfin[:], gfin_d.partition_broadcast(128), w=[r_const])
        k.dma("sp", "setup", gb[:], gbias_d.partition_broadcast(128), w=[r_const])
        k.dma("sp", "setup", wcv[:], wconv_d, w=[r_const])
        k.op("dve", lambda e: e.memset(ident_f[:], 1.0), w=[r_const])
        k.op("pool", lambda e: e.affine_select(out=ident_f[:], in_=ident_f[:], pattern=[[-1, 128]], compare_op=ALU.is_equal,
                                               fill=0.0, base=0, channel_multiplier=1), r=[r_const], w=[r_const])
        k.op("dve", lambda e: e.memset(tri_f[:], 1.0), w=[r_const])
        k.op("pool", lambda e: e.affine_select(out=tri_f[:], in_=tri_f[:], pattern=[[1, 128]], compare_op=ALU.is_ge,
                                               fill=0.0, base=0, channel_multiplier=-1), r=[r_const], w=[r_const])
        k.op("dve", lambda e: e.memset(ones_f[:], 1.0), w=[r_const])
        k.op("dve", lambda e: e.tensor_copy(out=ident_b[:], in_=ident_f[:]), r=[r_const], w=[r_const])
        k.op("dve", lambda e: e.tensor_copy(out=maskT[:], in_=tri_f[:]), r=[r_const], w=[r_const])
        k.op("dve", lambda e: e.memset(Amat[0:64, :], 1.0 / 64), w=[r_const])
        k.op("dve", lambda e: e.memset(Amat[64:65, :], EPS), w=[r_const])
        k.op("dve", lambda e: e.memset(epsc[:], EPS), w=[r_const])
        k.op("dve", lambda e: e.memset(lnsc[:], float(-0.5 * np.log(128.0))), w=[r_const])
        k.op("dve", lambda e: e.reciprocal(out=ginv[:], in_=gcol[:, 28:32]), r=[r_const], w=[r_const])
        k.op("dve", lambda e: e.memset(Vcf[:], 0.0), w=[r_Vc])
        k.op("dve", lambda e: e.memset(Vc[:, :, :, 64:65], 1.0), r=[r_Vc], w=[r_Vc])

        slab_i = [0]

        r_wblk = {}

        def cast_blk(name, dst, src):
            r_wblk[name] = Res("wb_" + name)
            k.dma("pool", "wc_" + name, dst, src, w=[r_wblk[name]])

        for c0 in (0, 512, 1024, 1544, 2056, 2568, 3088):
            cast_blk(f"in{c0}", wb_in[:, c0:c0 + 512], w_in_d[:, c0:c0 + 512])
        for hf in range(2):
            cast_blk(f"out{hf}", wb_out[:, hf * 512:(hf + 1) * 512], w_out_d[:, hf * 512:(hf + 1) * 512])
        for qd in range(4):
            for sl in range(2):
                c0 = qd * 1024 + sl * 512
                cast_blk(f"up{c0}", wb_up[:, c0:c0 + 512], w_up_d[:, c0:c0 + 512])
            for hf in range(2):
                cast_blk(f"down{qd}_{hf}", wb_down[qd * 1024:(qd + 1) * 1024, hf * 512:(hf + 1) * 512], w_down_d[qd * 1024:(qd + 1) * 1024, hf * 512:(hf + 1) * 512])
        cast_blk("ple", wb_ple, w_ple_d)
        for hf in range(2):
            cast_blk(f"pg{hf}", wb_pg[:, hf * 512:(hf + 1) * 512], w_pg_d[:, hf * 512:(hf + 1) * 512])
        k.dma("sp", "setup3", pleW[:], wb_ple.rearrange("(c p) n -> p c n", p=128), r=[r_wblk["ple"]], w=[r_pleW])
        tile_specs = []
        for c0 in (0, 512, 1024, 1544, 2056, 2568, 3088):
            tile_specs.append((f"in{c0}", wb_in[:, c0:c0 + 512].rearrange("(c p) n -> p c n", p=128)))
        for hf in range(2):
            tile_specs.append((f"out{hf}", wb_out[:, hf * 512:(hf + 1) * 512].rearrange("(c p) n -> p c n", p=128)))
        def up_specs(qd):
            for sl in range(2):
                c0 = qd * 1024 + sl * 512
                tile_specs.append((f"up{c0}", wb_up[:, c0:c0 + 512].rearrange("(c p) n -> p c n", p=128)))

        def down_specs(qd):
            for hf in range(2):
                tile_specs.append((f"down{qd}_{hf}", wb_down[qd * 1024:(qd + 1) * 1024, hf * 512:(hf + 1) * 512].rearrange("(c p) n -> p c n", p=128)))

        up_specs(0)
        for qd in range(4):
            if qd + 1 < 4:
                up_specs(qd + 1)
            down_specs(qd)
        for hf in range(2):
            tile_specs.append((f"pg{hf}", wb_pg[:, hf * 512:(hf + 1) * 512].rearrange("(c p) n -> p c n", p=128)))
        slab_specs = tile_specs * (NSEQ * NT)
        slab_pos = [0]
        slab_ready = []
        slab_free = list(range(NSLAB))

        def slab_topup():
            while slab_free and slab_pos[0] < len(slab_specs):
                b = slab_free.pop(0)
                name, src = slab_specs[slab_pos[0]]
                slab_pos[0] += 1
                k.dma("sp", f"slab{b}", slabs[b][:], src, r=[r_wblk[name]], w=[r_slab[b]])
                slab_ready.append((b, name))

        def slab_acquire(name):
            slab_topup()
            b, nm = slab_ready.pop(0)
            assert nm == name, (nm, name)
            return b

        def slab_release(b):
            slab_free.append(b)
            slab_topup()

        def wslab(wap, c0, ncols=512):
            return wap[:, c0:c0 + ncols].rearrange("(c p) n -> p c n", p=128)

        dbg_done = set()

        def dump(name, ap, rlist):
            if name in dbg_d and name not in dbg_done:
                dbg_done.add(name)
                k.dma("pool", "dbg", dbg_d[name], ap, r=rlist)

        def norm_a(sub):
            b = sub % 2
            k.op("dve", lambda e: e.memset(ms[:, sub:sub + 1], 0.0), w=[r_msS[sub]])
            k.op("act", lambda e: e.activation(out=hn[:, b, :], in_=R[:, sub, :], func=AF.Square, scale=1.0 / 32, accum_out=ms[:, sub:sub + 1]),
                 r=[r_R[sub], r_msS[sub]], w=[r_hn[b], r_msS[sub]])
            k.op("act", lambda e: e.activation(out=ms[:, 4 + sub:5 + sub], in_=ms[:, sub:sub + 1], func=AF.Ln, bias=epsc[:]), r=[r_msS[sub], r_const], w=[r_msS[sub]])
            k.op("act", lambda e: e.activation(out=rstd[:, sub:sub + 1], in_=ms[:, 4 + sub:5 + sub], func=AF.Exp, scale=-0.5), r=[r_msS[sub]], w=[r_msS[sub]])
            k.op("dve", lambda e: e.tensor_scalar(out=hn[:, b, :], in0=R[:, sub, :], scalar1=rstd[:, sub:sub + 1], scalar2=None, op0=ALU.mult),
                 r=[r_R[sub], r_msS[sub]], w=[r_hn[b]])

        def norm_b(gidx, sub):
            b = sub % 2
            pb = bank("mm")
            k.pe([(lambda e, c=c: e.transpose(out=psb[pb][:, c * 128:(c + 1) * 128], in_=hn[:, b, c * 128:(c + 1) * 128], identity=ident_b[:]))
                  for c in range(8)], r=[r_hn[b], r_const], w=[psr[pb]])
            k.op("dve", lambda e: e.tensor_tensor(
                out=hT[:, :, sub * 128:(sub + 1) * 128], in0=psb[pb].rearrange("p (c t) -> p c t", c=8),
                in1=gcol[:, gidx * 8:(gidx + 1) * 8].unsqueeze(2).to_broadcast([128, 8, 128]), op=ALU.mult),
                 r=[psr[pb], r_const], w=[r_hT[sub]])

        def stage_then_norm(stage_fn, gidx, after_stage=None):
            stage_fn(0)
            stage_fn(1)
            norm_a(0)
            stage_fn(2)
            norm_a(1)
            norm_b(gidx, 0)
            stage_fn(3)
            if after_stage is not None:
                after_stage()
            norm_a(2)
            norm_b(gidx, 1)
            norm_a(3)
            norm_b(gidx, 2)
            norm_b(gidx, 3)


        def mm_feat(sb, c4, kchunks=8, rhs_fn=None):
            pb = bank("mm")
            k.pe([(lambda e, c=c, pb=pb: e.matmul(ps[pb][:], lhsT=slabs[sb][:, c, c4 * 128:(c4 + 1) * 128], rhs=hT[:, c, :],
                                                   start=(c == 0), stop=(c == kchunks - 1))) for c in range(kchunks)],
                 r=[r_slab[sb]] + r_hT, w=[psr[pb]])
            return pb

        def mm_tok(sb, sub, lhs, rl, kchunks=8, col0=0):
            pb = bank("mm")
            k.pe([(lambda e, c=c, pb=pb: e.matmul(ps[pb][:], lhsT=lhs[:, c, sub * 128:(sub + 1) * 128], rhs=slabs[sb][:, c, col0:col0 + 512],
                                                   start=(c == 0), stop=(c == kchunks - 1))) for c in range(kchunks)],
                 r=[r_slab[sb]] + rl, w=[psr[pb]])
            return pb

        def tok0_precise():
            rowA = U3[0:1, 0:1024]
            rowB = lnr[0:1, :]
            Wf = sigb[:].rearrange("p a (c n) -> p (a c) n", c=4)
            WS = V_hn + [r_lnr]
            for hf in range(2):
                k.dma("sp", "rows", rowB, gmixrow_d[0:1, hf * 512:(hf + 1) * 512], w=WS)
                k.op("dve", lambda e: e.scalar_tensor_tensor(out=rowA[:, hf * 512:(hf + 1) * 512], in0=R[0:1, 0, hf * 512:(hf + 1) * 512], scalar=rstd[0:1, 0:1], in1=rowB,
                                                             op0=ALU.mult, op1=ALU.mult), r=WS + [r_R[0], r_msS[0]], w=WS)
            hb_ = bank("mm")
            k.pe([(lambda e, c=c: e.matmul(ps[hb_][:, c:c + 1], lhsT=rowA[:, c * 128:(c + 1) * 128], rhs=ones_f[0:1, 0:1], start=True, stop=True)) for c in range(8)],
                 r=WS + [r_const], w=[psr[hb_]])
            k.op("dve", lambda e: e.tensor_copy(out=h0col[:], in_=ps[hb_][:, 0:8]), r=[psr[hb_]], w=[r_h0])
            zb = [bank("mm"), bank("mm")]
            for i in range(8):
                k.dma("sp", "wf", Wf, w_in_d[:, 1544 + i * 128:1544 + (i + 1) * 128].rearrange("(c p) n -> p c n", p=128), w=r_sig)
                k.pe([(lambda e, c=c: e.matmul(ps[zb[i // 4]][0:1, (i % 4) * 128:(i % 4 + 1) * 128], lhsT=h0col[:, c:c + 1], rhs=Wf[:, c, :], start=(c == 0), stop=(c == 7)))
                      for c in range(8)], r=r_sig + [r_h0], w=[psr[zb[i // 4]]])
            for hf in range(2):
                k.dma("sp", "rows", rowA[:, hf * 512:(hf + 1) * 512], wc3row_d[0:1, hf * 512:(hf + 1) * 512], w=WS)
                k.op("dve", lambda e: e.tensor_tensor(out=rowA[:, hf * 512:(hf + 1) * 512], in0=ps[zb[hf]][0:1, :], in1=rowA[:, hf * 512:(hf + 1) * 512], op=ALU.mult),
                     r=WS + [psr[zb[hf]]], w=WS)
                k.op("act", lambda e: e.activation(out=rowB, in_=rowA[:, hf * 512:(hf + 1) * 512], func=AF.Exp, scale=-1.0), r=WS, w=WS)
                k.op("act", lambda e: e.activation(out=rowB, in_=rowB, func=AF.Ln, bias=ones_f[0:1, 0:1]), r=WS + [r_const], w=WS)
                k.op("act", lambda e: e.activation(out=rowB, in_=rowB, func=AF.Exp, scale=-1.0), r=WS, w=WS)
                k.op("dve", lambda e: e.tensor_tensor(out=rowA[:, hf * 512:(hf + 1) * 512], in0=rowA[:, hf * 512:(hf + 1) * 512], in1=rowB, op=ALU.mult), r=WS, w=WS)
            k.op("dve", lambda e: e.tensor_tensor(out=rowB, in0=rowA[:, 0:512], in1=rowA[:, 512:1024], op=ALU.mult), r=WS, w=WS)
            k.op("dve", lambda e: e.tensor_reduce(out=s00[:], in_=rowB.rearrange("p (h d) -> p h d", h=MH), axis=AX.X, op=ALU.add), r=WS, w=[r_s00])

        for seq in range(NSEQ):
            for h in range(MH):
                k.op("pool", lambda e, h=h: e.memset(Cst[:, h, :], 0.0), w=[r_C[h]])
            k.op("pool", lambda e: e.memset(halo[:], 0.0), w=r_halo)
            k.op("pool", lambda e: e.memset(tot[:], 0.0), w=[r_tot])
            k.op("pool", lambda e: e.memset(Mt[:], 0.0), w=[r_Mt])
            for ti in range(NT):
                t0 = ti * T
                kt0 = t0 // 128
                nkt = kt0 + NSUB
                g_idx = seq * NT + ti
                bR, bU = bufs[g_idx % 2], bufs[(g_idx + 1) % 2]
                R, r_R = bR["R"], bR["rR"]
                aT, QT, qcT, kcT = bU["aT"], bU["QT"], bU["qcT"], bU["kcT"]
                r_U1a, r_U1b = bU["rUa"], bU["rUb"]
                k.alias(bU["rR"], r_U1a + r_U1b)
                k.alias(V_ple2, V_att + V_conv)
                k.alias(V_ml + V_hn, V_E)
                norm_a(0)
                norm_a(1)
                norm_b(0, 0)
                norm_a(2)
                norm_b(0, 1)
                norm_a(3)
                norm_b(0, 2)
                norm_b(0, 3)
                gbk = bank("mm")
                for sub in range(NSUB):
                    k.pe([(lambda e, c=c: e.matmul(ps[gbk][:, sub * 16:(sub + 1) * 16], lhsT=hT[:, c, sub * 128:(sub + 1) * 128], rhs=Wg[:, c, :],
                                                   start=(c == 0), stop=(c == 7))) for c in range(8)], r=[r_hT[sub], r_const], w=[psr[gbk]])
                k.op("dve", lambda e: e.tensor_tensor(out=zg[:], in0=ps[gbk][:, 0:64].rearrange("p (s g) -> p s g", s=NSUB),
                                                      in1=gb[:].unsqueeze(1).to_broadcast([128, NSUB, 16]), op=ALU.add), r=[psr[gbk], r_const], w=[r_g])
                k.op("act", lambda e: e.activation(out=e1[:], in_=zg[:], func=AF.Exp, scale=-1.0), r=[r_g], w=[r_g])
                k.op("act", lambda e: e.activation(out=lsp[:], in_=e1[:], func=AF.Ln, bias=ones_f[:, 0:1]), r=[r_g, r_const], w=[r_g])
                k.op("pool", lambda e: e.memset(EK[:], 0.0), w=[r_EK])
                k.op("pool", lambda e: e.memset(EQ[:], 0.0), w=[r_EQ])
                k.op("pool", lambda e: e.memset(EK[:, :, :, 0:3], 1.0), r=[r_EK], w=[r_EK])
                k.op("pool", lambda e: e.memset(EQ[:, :, :, 3:6], 1.0), r=[r_EQ], w=[r_EQ])
                sb = slab_acquire("in0")
                for j in range(4):
                    pb = mm_feat(sb, j)
                    k.op("act", lambda e: e.activation(out=QT[0:64, 2 * j, :], in_=ps[pb][0:64, :], func=AF.Copy, scale=0.125), r=[psr[pb]], w=[r_U1a[2 * j]])
                    k.op("dve", lambda e: e.tensor_scalar(out=QT[0:64, 2 * j + 1, :], in0=ps[pb][64:128, :], scalar1=0.125, scalar2=None, op0=ALU.mult),
                         r=[psr[pb]], w=[r_U1a[2 * j + 1]])
                slab_release(sb)
                cbk = bank("mm")
                for j in range(NSUB):
                    fns = [lambda e, j=j: e.matmul(ps[cbk][:, j * 16:(j + 1) * 16], lhsT=tri_f[:], rhs=lsp[:, j, :], start=True, stop=False)]
                    for i in range(j):
                        fns.append(lambda e, j=j, i=i: e.matmul(ps[cbk][:, j * 16:(j + 1) * 16], lhsT=ones_f[:], rhs=lsp[:, i, :], start=False, stop=False))
                    fns.append(lambda e, j=j: e.matmul(ps[cbk][:, j * 16:(j + 1) * 16], lhsT=ones_f[0:1, :], rhs=tot[0:1, :], start=False, stop=True))
                    k.pe(fns, r=[r_g, r_tot, r_const], w=[psr[cbk]])
                k.op("dve", lambda e: e.tensor_copy(out=cpos[:], in_=ps[cbk][:, 0:64].rearrange("p (s g) -> p s g", s=NSUB)), r=[psr[cbk]], w=[r_g])
                tbk = bank("mm")
                fns = [(lambda e, i=i: e.matmul(ps[tbk][0:1, 0:16], lhsT=ones_f[:, 0:1], rhs=lsp[:, i, :], start=(i == 0), stop=False)) for i in range(NSUB)]
                fns.append(lambda e: e.matmul(ps[tbk][0:1, 0:16], lhsT=ones_f[0:1, 0:1], rhs=tot[0:1, :], start=False, stop=True))
                k.pe(fns, r=[r_g, r_tot, r_const], w=[psr[tbk]])
                k.op("dve", lambda e: e.tensor_copy(out=tot[:], in_=ps[tbk][0:1, 0:16]), r=[psr[tbk]], w=[r_tot])
                sb = slab_acquire("in512")
                for j in range(4):
                    pb = mm_feat(sb, j)
                    k.op("dve", lambda e: e.tensor_copy(out=Kc[0:64, 2 * j, t0:t0 + T], in_=ps[pb][0:64, :]), r=[psr[pb]], w=[r_Kc[2 * j]])
                    k.op("dve", lambda e: e.tensor_copy(out=Kc[0:64, 2 * j + 1, t0:t0 + T], in_=ps[pb][64:128, :]), r=[psr[pb]], w=[r_Kc[2 * j + 1]])
                slab_release(sb)
                cf = cpos[:, :, 0:8]
                r1v = r1t[:, 0:32].rearrange("p (s h) -> p s h", s=NSUB)
                r2v = r1t[:, 32:64].rearrange("p (s h) -> p s h", s=NSUB)
                k.op("dve", lambda e: e.tensor_copy(out=hml[:, 0], in_=cf), r=[r_g], w=[r_sm])
                k.op("dve", lambda e: e.tensor_tensor(out=r1v, in0=cf, in1=hml[:, 0], op=ALU.subtract), r=[r_g, r_sm], w=[r_sm])
                k.op("dve", lambda e: e.tensor_copy(out=hml[:, 1], in_=r1v), r=[r_sm], w=[r_sm])
                k.op("dve", lambda e: e.tensor_tensor(out=r2v, in0=r1v, in1=hml[:, 1], op=ALU.subtract), r=[r_sm], w=[r_sm])
                k.op("dve", lambda e: e.tensor_copy(out=hml[:, 2], in_=r2v), r=[r_sm], w=[r_sm])
                for i3 in range(3):
                    k.op("dve", lambda e: e.tensor_copy(out=EK[:, :, :, 3 + i3], in_=hml[:, i3]), r=[r_sm, r_EK], w=[r_EK])
                    k.op("dve", lambda e: e.tensor_scalar(out=EQ[:, :, :, i3], in0=hml[:, i3], scalar1=-1.0, scalar2=None, op0=ALU.mult), r=[r_sm, r_EQ], w=[r_EQ])
                k.op("dve", lambda e: e.tensor_tensor(out=alpha[:], in0=zg[:, :, 8:12], in1=cpos[:, :, 12:16], op=ALU.add), r=[r_g], w=[r_g])
                abk = bank("mm")
                k.pe([(lambda e, j=j: e.matmul(ps[abk][0:4, j * 128:(j + 1) * 128], lhsT=alpha[:, j, :], rhs=ident_f[:], start=True, stop=True)) for j in range(NSUB)],
                     r=[r_g, r_const], w=[psr[abk]])
                k.op("dve", lambda e: e.tensor_reduce(out=A4[:], in_=ps[abk][0:4, :].rearrange("p (s t) -> p s t", s=NSUB), axis=AX.X, op=ALU.max),
                     r=[psr[abk]], w=[r_Mt])
                for j in range(NSUB):
                    k.op("dve", lambda e: e.tensor_tensor(out=Mt[:, j + 1:j + 2], in0=Mt[:, j:j + 1], in1=A4[:, j:j + 1], op=ALU.max), r=[r_Mt], w=[r_Mt])
                k.op("dve", lambda e: e.tensor_tensor(out=wdec[:], in0=Mt[:, 0:4], in1=Mt[:, 1:5], op=ALU.subtract), r=[r_Mt], w=[r_Mt])
                k.op("act", lambda e: e.activation(out=wdec[:], in_=wdec[:], func=AF.Exp), r=[r_Mt], w=[r_Mt])
                k.op("dve", lambda e: e.tensor_tensor(out=DG[:, 0:16].rearrange("p (j h) -> p j h", j=NSUB), in0=Mt[:, 1:5].unsqueeze(2).to_broadcast([4, NSUB, 4]),
                                                      in1=ident_f[0:4, 0:4].unsqueeze(1).to_broadcast([4, NSUB, 4]), op=ALU.mult), r=[r_Mt, r_const], w=[r_Mt])
                k.op("dve", lambda e: e.tensor_tensor(out=DG[:, 16:32].rearrange("p (j h) -> p j h", j=NSUB), in0=wdec[:].unsqueeze(2).to_broadcast([4, NSUB, 4]),
                                                      in1=ident_f[0:4, 0:4].unsqueeze(1).to_broadcast([4, NSUB, 4]), op=ALU.mult), r=[r_Mt, r_const], w=[r_Mt])
                k.op("dve", lambda e: e.tensor_copy(out=Mt[:, 0:1], in_=Mt[:, 4:5]), r=[r_Mt], w=[r_Mt])
                bbk = bank("mm")
                k.pe([lambda e: e.matmul(ps[bbk][:, 0:32], lhsT=ones_f[0:4, :], rhs=DG[:], start=True, stop=True)], r=[r_Mt, r_const], w=[psr[bbk]])
                k.op("dve", lambda e: e.tensor_copy(out=mbc[:], in_=ps[bbk][:, 0:32]), r=[psr[bbk]], w=[r_g])
                mbM = mbc[:, 0:16].rearrange("p (j h) -> p j h", j=NSUB)
                k.op("dve", lambda e: e.tensor_tensor(out=d1[:], in0=alpha[:], in1=mbM, op=ALU.subtract), r=[r_g], w=[r_g])
                k.op("act", lambda e: e.activation(out=es_t[:], in_=d1[:], func=AF.Exp, bias=lnsc[:]), r=[r_g, r_const], w=[r_g])
                k.op("dve", lambda e: e.tensor_tensor(out=d1[:], in0=cpos[:, :, 12:16], in1=mbM, op=ALU.subtract), r=[r_g], w=[r_g])
                k.op("act", lambda e: e.activation(out=clampv[:], in_=d1[:], func=AF.Exp), r=[r_g], w=[r_g])
                sb = slab_acquire("in1024")
                for sub in range(NSUB):
                    pb = mm_tok(sb, sub, hT, [r_hT[sub]])
                    if sub % 2 == 0:
                        k.op("act", lambda e: e.activation(out=Vc[:, kt0 + sub, :, 0:64], in_=ps[pb][:].rearrange("p (h d) -> p h d", h=FH), func=AF.Copy), r=[psr[pb]], w=[r_Vc])
                    else:
                        k.op("dve", lambda e: e.tensor_copy(out=Vc[:, kt0 + sub, :, 0:64], in_=ps[pb][:].rearrange("p (h d) -> p h d", h=FH)), r=[psr[pb]], w=[r_Vc])
                slab_release(sb)
                exk = [bank("all") for _ in range(3)]
                exq = [bank("all") for _ in range(3)]
                hgroups = [(0, 3), (3, 3), (6, 2)]
                for EX, rEX, banks in ((EK, r_EK, exk), (EQ, r_EQ, exq)):
                    for gi, (h0, n) in enumerate(hgroups):
                        k.pe([(lambda e, sub=sub: e.matmul(ps[banks[gi]][0:n * 32, sub * 128:(sub + 1) * 128], lhsT=EX[:, sub, h0:h0 + n, :].rearrange("p h e -> p (h e)"),
                                                           rhs=ident_b[:], start=True, stop=True)) for sub in range(NSUB)], r=[rEX, r_const], w=[psr[banks[gi]]])
                for h in range(FH):
                    gi, row = h // 3, 32 * (h % 3)
                    k.op("dve", lambda e: e.tensor_copy(out=Kc[64:70, h, t0:t0 + T], in_=ps[exk[gi]][row:row + 6, :]), r=[psr[exk[gi]]], w=[r_Kc[h]])
                    k.op("act", lambda e: e.activation(out=QT[64:70, h, :], in_=ps[exq[gi]][row:row + 6, :], func=AF.Copy), r=[psr[exq[gi]]], w=[r_U1a[h]])

                if ti == 0:
                    tok0_precise()
                k.alias(V_hn, r_ktm)
                k.alias(V_E, r_vx)
                bg = []
                slab_of = {}

                def conv_a(which, col0, c4):
                    if c4 == 0:
                        slab_of[col0] = slab_acquire(f"in{col0}")
                    sbx = slab_of[col0]
                    ch = which * 4 + c4
                    pb = mm_feat(sbx, c4)
                    if c4 == 3:
                        slab_release(sbx)
                    cb = ch % 2
                    acc = cv_acc[cb]
                    k.op("dve", lambda e: e.tensor_scalar(out=acc, in0=ps[pb][:, 0:512], scalar1=wcv[:, ch, 3:4], scalar2=None, op0=ALU.mult),
                         r=[psr[pb], r_const], w=[r_cvacc[cb]])
                    for jj in (2, 1, 0):
                        sh = 3 - jj
                        k.op("dve", lambda e: e.scalar_tensor_tensor(out=acc[:, sh:512], in0=ps[pb][:, 0:512 - sh], scalar=wcv[:, ch, jj:jj + 1], in1=acc[:, sh:512],
                                                                     op0=ALU.mult, op1=ALU.add), r=[psr[pb], r_cvacc[cb], r_const], w=[r_cvacc[cb]])
                        k.op("dve", lambda e: e.scalar_tensor_tensor(out=acc[:, 0:sh], in0=halo[:, ch, 3 - sh:3], scalar=wcv[:, ch, jj:jj + 1], in1=acc[:, 0:sh],
                                                                     op0=ALU.mult, op1=ALU.add), r=[r_halo[ch], r_cvacc[cb], r_const], w=[r_cvacc[cb]])
                    k.op("dve", lambda e: e.tensor_copy(out=halo[:, ch, :], in_=ps[pb][:, 509:512]), r=[psr[pb]], w=[r_halo[ch]])

                def conv_b(which, dst, rr_off, c4):
                    ch = which * 4 + c4
                    cb = ch % 2
                    acc = cv_acc[cb]
                    etmp = Ssb[:, cb, :, :].rearrange("p h t -> p (h t)")
                    k.op("act", lambda e: e.activation(out=etmp, in_=acc, func=AF.Exp, scale=-1.0), r=[r_cvacc[cb]], w=[r_Ssb[cb]])
                    k.op("act", lambda e: e.activation(out=etmp, in_=etmp, func=AF.Ln, bias=ones_f[:, 0:1]), r=[r_Ssb[cb], r_const], w=[r_Ssb[cb]])
                    k.op("act", lambda e: e.activation(out=etmp, in_=etmp, func=AF.Exp, scale=-1.0), r=[r_Ssb[cb]], w=[r_Ssb[cb]])
                    k.op("dve", lambda e: e.tensor_tensor(out=dst[:, c4, :], in0=acc, in1=etmp, op=ALU.mult), r=[r_cvacc[cb], r_Ssb[cb]], w=[r_U1b[rr_off + c4]])

                def mv_unit(sub):
                    if sub == 0:
                        slab_of["mv"] = slab_acquire("in2568")
                    pb = mm_tok(slab_of["mv"], sub, hT, [r_hT[sub]])
                    if sub == NSUB - 1:
                        slab_release(slab_of["mv"])
                    k.op("pool", lambda e: e.memset(vx[:, sub, :, 128:129], 1.0), w=[r_vx[sub]])
                    k.op("dve", lambda e: e.tensor_copy(out=vx[:, sub, :, 0:128], in_=ps[pb][:].rearrange("p (h d) -> p h d", h=MH)), r=[psr[pb]], w=[r_vx[sub]])
                    k.op("dve", lambda e: e.tensor_tensor(out=vx[:, sub, :, :], in0=vx[:, sub, :, :], in1=es_t[:, sub, :].unsqueeze(2).to_broadcast([128, MH, 129]), op=ALU.mult),
                         r=[r_vx[sub], r_g], w=[r_vx[sub]])

                def mo_unit(c4):
                    if c4 == 0:
                        slab_of["mo"] = slab_acquire("in3088")
                    pb = mm_feat(slab_of["mo"], c4)
                    if c4 == 3:
                        slab_release(slab_of["mo"])
                    k.op("act", lambda e: e.activation(out=sgo[:, c4, :], in_=ps[pb][:], func=AF.Exp, scale=-1.0), r=[psr[pb]], w=[r_sgo[c4]])
                    k.op("act", lambda e: e.activation(out=sgo[:, c4, :], in_=sgo[:, c4, :], func=AF.Ln, bias=ones_f[:, 0:1]), r=[r_sgo[c4], r_const], w=[r_sgo[c4]])
                    k.op("act", lambda e: e.activation(out=sgo[:, c4, :], in_=sgo[:, c4, :], func=AF.Exp, scale=-1.0), r=[r_sgo[c4]], w=[r_sgo[c4]])
                    k.op("dve", lambda e: e.tensor_scalar(out=sgo[:, c4, :], in0=sgo[:, c4, :], scalar1=gcol[:, 28 + c4:29 + c4], scalar2=None, op0=ALU.mult),
                         r=[r_sgo[c4], r_const], w=[r_sgo[c4]])

                ua, ub_ = [], []
                for which, col0, dst, rr_off in ((0, 1544, qcT, 0), (1, 2056, kcT, 4)):
                    for c4 in range(4):
                        ua.append(lambda which=which, col0=col0, c4=c4: conv_a(which, col0, c4))
                        ub_.append(lambda which=which, dst=dst, rr_off=rr_off, c4=c4: conv_b(which, dst, rr_off, c4))
                bg.append(ua[0])
                for n in range(1, 8):
                    bg.append(ua[n])
                    bg.append(ub_[n - 1])
                bg.append(ub_[7])
                for sub in range(NSUB):
                    bg.append(lambda sub=sub: mv_unit(sub))
                post_units = [lambda c4=c4: mo_unit(c4) for c4 in range(4)]


                pairs = [(h, kt) for h in range(FH) for kt in range(nkt)]
                sbank = {}
                pvbank = {}

                def lo_of(kt):
                    return max(0, (kt - kt0) * 128)

                def emit_S(idx):
                    h, kt = pairs[idx]
                    b = bank("st")
                    sbank[idx] = b
                    lo = lo_of(kt)
                    k.pe([lambda e: e.matmul(ps[b][:, lo:T], lhsT=Kc[0:70, h, kt * 128:(kt + 1) * 128], rhs=QT[0:70, h, lo:T], start=True, stop=True)],
                         r=[r_Kc[h], r_U1a[h]], w=[psr[b]])

                def epilogue(h):
                    pvb = pvbank[h]
                    k.op("act", lambda e: e.activation(out=sqx[0:65, :], in_=ps[pvb][0:65, :], func=AF.Square), r=[psr[pvb]], w=[r_sqx])
                    rb = bank("mm")
                    k.pe([lambda e: e.matmul(ps[rb][0:64, :], lhsT=Amat[0:65, :], rhs=sqx[0:65, :], start=True, stop=True)], r=[r_sqx, r_const], w=[psr[rb]])
                    k.op("act", lambda e: e.activation(out=lnr[0:64, :], in_=ps[rb][0:64, :], func=AF.Ln), r=[psr[rb]], w=[r_lnr])
                    k.op("act", lambda e: e.activation(out=lnr[0:64, :], in_=lnr[0:64, :], func=AF.Exp, scale=-0.5), r=[r_lnr], w=[r_lnr])
                    po = (h % 2) * 64
                    k.op("dve", lambda e: e.scalar_tensor_tensor(out=mixT[po:po + 64, h // 2, :], in0=ps[pvb][0:64, :], scalar=gfox[0:64, h:h + 1], in1=lnr[0:64, :],
                                                                 op0=ALU.mult, op1=ALU.mult), r=[psr[pvb], r_lnr, r_const], w=[r_mix[h // 2]])

                def attn_range(p0, p1, with_bg):
                    for i_ in range(p0, min(p0 + 2, p1)):
                        emit_S(i_)
                    for idx in range(p0, p1):
                        h, kt = pairs[idx]
                        if idx + 2 < p1:
                            emit_S(idx + 2)
                        b = sbank[idx]
                        lo = lo_of(kt)
                        pi = idx % NPT
                        k.op("act", lambda e: e.activation(out=Pt[pi][:, lo:T], in_=ps[b][:, lo:T], func=AF.Exp), r=[psr[b]], w=[r_Pt[pi]])
                        if kt >= kt0:
                            k.op("dve", lambda e: e.tensor_tensor(out=Pt[pi][:, lo:lo + 128], in0=Pt[pi][:, lo:lo + 128], in1=maskT[:], op=ALU.mult),
                                 r=[r_Pt[pi], r_const], w=[r_Pt[pi]])
                        if kt == 0:
                            pvbank[h] = bank("pv")
                        pvb = pvbank[h]
                        vo = (kt * FH + h) * 65
                        k.pe([lambda e: e.matmul(ps[pvb][:, lo:T], lhsT=Vcf[:, vo:vo + 128], rhs=Pt[pi][:, lo:T], start=(kt == 0), stop=(kt == nkt - 1))],
                             r=[r_Pt[pi], r_Vc], w=[psr[pvb]])
                        if kt == nkt - 1:
                            epilogue(h)
                        if with_bg and bg and (idx + 1) % bg_stride == 0:
                            bg.pop(0)()

                p_mid = (FH // 2) * nkt
                bg_stride = max(1, p_mid // (len(bg) + 2))
                attn_range(0, p_mid, True)
                while bg:
                    bg.pop(0)()
                for u in post_units:
                    u()
                st1 = {}
                st2 = {}

                def ml_stage1(j):
                    js = slice(j * 128, (j + 1) * 128)
                    jb = j % 2
                    ktb = 0
                    k.pe([(lambda e, h=h: e.transpose(out=psb[ktb][:, h * 128:(h + 1) * 128], in_=kcT[:, h, js], identity=ident_b[:])) for h in range(MH)],
                         r=r_U1b[4:8] + [r_const], w=[psr[ktb]])
                    k.op("dve", lambda e: e.tensor_copy(out=ktm[:, j, :, :], in_=psb[ktb][:, 0:512].rearrange("p (h d) -> p h d", h=MH)), r=[psr[ktb]], w=[r_ktm[j]])
                    stb = 1
                    k.pe([(lambda e, h=h: e.matmul(ps[stb][:, h * 128:(h + 1) * 128], lhsT=kcT[:, h, js], rhs=qcT[:, h, js], start=True, stop=True)) for h in range(MH)],
                         r=r_U1b, w=[psr[stb]])
                    k.op("dve", lambda e: e.tensor_tensor(out=Ssb[:, jb, :, :], in0=ps[stb][:].rearrange("p (h t) -> p h t", h=MH),
                                                          in1=maskT[:].unsqueeze(1).to_broadcast([128, MH, 128]), op=ALU.mult), r=[psr[stb], r_const], w=[r_Ssb[jb]])
                    if ti == 0 and j == 0:
                        k.op("dve", lambda e: e.tensor_copy(out=Ssb[0:1, jb, :, 0:1], in_=s00[:].unsqueeze(2)), r=[r_s00, r_Ssb[jb]], w=[r_Ssb[jb]])
                    ub = [2, 1]
                    k.pe([(lambda e, h=h: e.matmul(ps[2][:, h * 129:(h + 1) * 129], lhsT=ktm[:, j, h, :], rhs=vx[:, j, h, :], start=True, stop=True)) for h in range(3)],
                         r=[r_ktm[j], r_vx[j]], w=[psr[2]])
                    k.pe([lambda e: e.matmul(ps[1][:, 0:129], lhsT=ktm[:, j, 3, :], rhs=vx[:, j, 3, :], start=True, stop=True)], r=[r_ktm[j], r_vx[j]], w=[psr[1]])
                    st1[j] = ub

                def ml_stage2(j):
                    js = slice(j * 128, (j + 1) * 128)
                    jb = j % 2
                    ub = st1[j]
                    wbc = mbc[:, 16 + 4 * j:20 + 4 * j].unsqueeze(2).to_broadcast([128, MH, 129])
                    k.op("dve", lambda e: e.tensor_tensor(out=Cst[:], in0=Cst[:], in1=wbc, op=ALU.mult), r=r_C + [r_g], w=r_C)
                    k.op("dve", lambda e: e.tensor_copy(out=Csb[:], in_=Cst[:]), r=r_C, w=r_Csb)
                    k.op("dve", lambda e: e.tensor_tensor(out=Cst[:, 0:3, :], in0=Cst[:, 0:3, :], in1=ps[2][:, 0:387].rearrange("p (h d) -> p h d", h=3), op=ALU.add),
                         r=r_C + [psr[2]], w=r_C)
                    k.op("dve", lambda e: e.tensor_tensor(out=Cst[:, 3, :], in0=Cst[:, 3, :], in1=ps[1][:, 0:129], op=ALU.add), r=r_C + [psr[1]], w=r_C)
                    nb = [0, 1]
                    for hb in range(2):
                        fns = []
                        for hh in range(2):
                            h = 2 * hb + hh
                            fns.append(lambda e, h=h, hh=hh: e.matmul(ps[nb[hb]][:, hh * 129:(hh + 1) * 129], lhsT=Ssb[:, jb, h, :], rhs=vx[:, j, h, :], start=True, stop=False))
                            fns.append(lambda e, h=h, hh=hh: e.matmul(ps[nb[hb]][:, hh * 129:(hh + 1) * 129], lhsT=qcT[:, h, js], rhs=Csb[:, h, :], start=False, stop=True))
                        k.pe(fns, r=[r_Ssb[jb], r_vx[j]] + r_U1b[0:4] + r_Csb, w=[psr[nb[hb]]])
                    st2[j] = nb

                def ml_stage3a(j):
                    jb = j % 2
                    nb = st2[j]
                    smj = sm[:, jb, :]
                    k.op("dve", lambda e: e.memset(smj[:, 0:4], 0.0), w=[r_smj[jb]])
                    for h in range(MH):
                        hb, hh = h // 2, h % 2
                        k.op("act", lambda e: e.activation(out=ybf[:, jb, h, :], in_=ps[nb[hb]][:, hh * 129:hh * 129 + 128], func=AF.Square, scale=float(128 ** -0.5),
                                                           accum_out=smj[:, h:h + 1]), r=[psr[nb[hb]], r_smj[jb]], w=[r_ybf[jb], r_smj[jb]])
                    for hb in range(2):
                        k.op("dve", lambda e: e.tensor_copy(out=smj[:, 4 + 2 * hb:6 + 2 * hb], in_=ps[nb[hb]][:, 0:258].rearrange("p (h d) -> p h d", h=2)[:, :, 128]),
                             r=[psr[nb[hb]], r_smj[jb]], w=[r_smj[jb]])
                    k.op("dve", lambda e: e.scalar_tensor_tensor(out=smj[:, 8:12], in0=smj[:, 4:8], scalar=-1.0, in1=smj[:, 4:8], op0=ALU.mult, op1=ALU.max), r=[r_smj[jb]], w=[r_smj[jb]])
                    k.op("dve", lambda e: e.tensor_tensor(out=smj[:, 12:16], in0=smj[:, 8:12], in1=clampv[:, j, :], op=ALU.max), r=[r_smj[jb], r_g], w=[r_smj[jb]])
                    k.op("dve", lambda e: e.scalar_tensor_tensor(out=smj[:, 16:20], in0=smj[:, 12:16], scalar=EPS, in1=smj[:, 12:16], op0=ALU.mult, op1=ALU.mult), r=[r_smj[jb]], w=[r_smj[jb]])
                    k.op("dve", lambda e: e.tensor_tensor(out=smj[:, 20:24], in0=smj[:, 16:20], in1=smj[:, 0:4], op=ALU.add), r=[r_smj[jb]], w=[r_smj[jb]])
                    k.op("act", lambda e: e.activation(out=smj[:, 24:28], in_=smj[:, 20:24], func=AF.Ln), r=[r_smj[jb]], w=[r_smj[jb]])
                    k.op("act", lambda e: e.activation(out=smj[:, 28:32], in_=smj[:, 24:28], func=AF.Exp, scale=-0.5), r=[r_smj[jb]], w=[r_smj[jb]])
                    for hb in range(2):
                        k.op("dve", lambda e: e.tensor_tensor(out=ybf[:, jb, 2 * hb:2 * hb + 2, :], in0=ps[nb[hb]][:, 0:258].rearrange("p (h d) -> p h d", h=2)[:, :, 0:128],
                                                              in1=smj[:, 28 + 2 * hb:30 + 2 * hb].unsqueeze(2).to_broadcast([128, 2, 128]), op=ALU.mult),
                             r=[psr[nb[hb]], r_smj[jb]], w=[r_ybf[jb]])

                def ml_stage3b(j):
                    js = slice(j * 128, (j + 1) * 128)
                    jb = j % 2
                    yb = 2
                    k.pe([(lambda e, h=h: e.transpose(out=psb[yb][:, h * 128:(h + 1) * 128], in_=ybf[:, jb, h, :], identity=ident_b[:])) for h in range(MH)],
                         r=[r_ybf[jb], r_const], w=[psr[yb]])
                    k.op("dve", lambda e: e.tensor_tensor(out=mixT[:, 4:8, js], in0=psb[yb][:, 0:512].rearrange("p (h t) -> p h t", h=MH), in1=sgo[:, :, js], op=ALU.mult),
                         r=[psr[yb]] + r_sgo, w=r_mix[4:8])

                for j in range(NSUB):
                    ml_stage1(j)
                    ml_stage2(j)
                    ml_stage3a(j)
                    ml_stage3b(j)
                dump("mixml", mixT[:, 4:8, :], r_mix[4:8])
                attn_range(p_mid, len(pairs), False)
                dump("mixfox", mixT[:, 0:4, :], r_mix[0:4])


                k.alias(V_ml, V_hn)
                sbo = [slab_acquire(f"out{half}") for half in range(2)]

                def wout_stage(sub):
                    for half in range(2):
                        pb = mm_tok(sbo[half], sub, mixT, r_mix)
                        k.op("dve", lambda e: e.tensor_tensor(out=R[:, sub, half * 512:(half + 1) * 512], in0=R[:, sub, half * 512:(half + 1) * 512],
                                                              in1=ps[pb][:], op=ALU.add), r=[psr[pb], r_R[sub]], w=[r_R[sub]])

                stage_then_norm(wout_stage, 1, after_stage=lambda: [slab_release(b) for b in sbo])
                dump("x1", R[:], r_R)
                def up_stage(qd):
                    ab = qd % 2
                    r_aT = r_U1a if ab == 0 else r_U1b
                    for sl in range(2):
                        sb = slab_acquire(f"up{qd * 1024 + sl * 512}")
                        for c4 in range(4):
                            pb = mm_feat(sb, c4)
                            if c4 == 3:
                                slab_release(sb)
                            kk = sl * 4 + c4
                            k.op("act", lambda e: e.activation(out=aT[:, ab, kk, :], in_=ps[pb][:], func=AF.Relu), r=[psr[pb]], w=[r_aT[kk]])
                            k.op("dve", lambda e: e.tensor_tensor(out=aT[:, ab, kk, :], in0=aT[:, ab, kk, :], in1=aT[:, ab, kk, :], op=ALU.mult), r=[r_aT[kk]], w=[r_aT[kk]])

                def down_one(qd, sbx, sub, half):
                    ab = qd % 2
                    r_aT = r_U1a if ab == 0 else r_U1b
                    pb = mm_tok(sbx, sub, aT[:, ab], r_aT)
                    k.op("dve", lambda e: e.tensor_tensor(out=R[:, sub, half * 512:(half + 1) * 512], in0=R[:, sub, half * 512:(half + 1) * 512],
                                                          in1=ps[pb][:], op=ALU.add), r=[psr[pb], r_R[sub]], w=[r_R[sub]])

                up_stage(0)
                for qd in range(4):
                    if qd + 1 < 4:
                        up_stage(qd + 1)
                    if qd < 3:
                        for half in range(2):
                            sbx = slab_acquire(f"down{qd}_{half}")
                            for sub in range(NSUB):
                                down_one(qd, sbx, sub, half)
                            slab_release(sbx)
                    else:
                        sbd = [slab_acquire(f"down{qd}_{half}") for half in range(2)]

                        def down_stage(sub):
                            for half in range(2):
                                down_one(3, sbd[half], sub, half)

                        stage_then_norm(down_stage, 2, after_stage=lambda: [slab_release(b) for b in sbd])
                dump("x2", R[:], r_R)
                if g_idx + 1 < NSEQ * NT:
                    nseq, nti = divmod(g_idx + 1, NT)
                    k.alias(r_U1a + r_U1b, bU["rR"])
                    k.dma("pool", "xin", bU["R"], x_d[nseq, nti * T:(nti + 1) * T, :].rearrange("(j p) d -> p j d", p=128), w=bU["rR"])
                k.alias(V_conv + V_att, V_ple2)
                k.dma("pool", "pin", p32, p_d[seq, t0:t0 + T, :].rearrange("(j p) d -> p j d", p=128), w=[r_p])
                k.op("dve", lambda e: e.tensor_copy(out=pbf, in_=p32), r=[r_p], w=[r_p])
                for sub in range(NSUB):
                    pb = bank("mm")
                    k.pe([(lambda e, c=c: e.transpose(out=psb[pb][:, c * 128:(c + 1) * 128], in_=pbf[:, sub, c * 128:(c + 1) * 128], identity=ident_b[:])) for c in range(2)],
                         r=[r_p, r_const], w=[psr[pb]])
                    k.op("dve", lambda e: e.tensor_copy(out=pT[:, :, sub * 128:(sub + 1) * 128], in_=psb[pb][:, 0:256].rearrange("p (c t) -> p c t", c=2)),
                         r=[psr[pb]], w=[r_p])
                sbg = [slab_acquire(f"pg{half}") for half in range(2)]

                def ple_stage(sub):
                    for half in range(2):
                        gbk2 = mm_tok(sbg[half], sub, hT, [r_hT[sub]])
                        ebk = bank("mm")
                        k.pe([(lambda e, c=c: e.matmul(ps[ebk][:], lhsT=pT[:, c, sub * 128:(sub + 1) * 128], rhs=pleW[:, c, half * 512:(half + 1) * 512],
                                                        start=(c == 0), stop=(c == 1))) for c in range(2)], r=[r_p, r_pleW], w=[psr[ebk]])
                        si = half
                        k.op("act", lambda e: e.activation(out=sig[si], in_=ps[gbk2][:], func=AF.Exp, scale=-1.0), r=[psr[gbk2]], w=[r_sig[si]])
                        k.op("act", lambda e: e.activation(out=sig[si], in_=sig[si], func=AF.Ln, bias=ones_f[:, 0:1]), r=[r_sig[si], r_const], w=[r_sig[si]])
                        k.op("act", lambda e: e.activation(out=sig[si], in_=sig[si], func=AF.Exp, scale=-1.0), r=[r_sig[si]], w=[r_sig[si]])
                        k.op("dve", lambda e: e.tensor_tensor(out=sig[si], in0=sig[si], in1=ps[ebk][:], op=ALU.mult), r=[r_sig[si], psr[ebk]], w=[r_sig[si]])
                        k.op("dve", lambda e: e.tensor_tensor(out=R[:, sub, half * 512:(half + 1) * 512], in0=R[:, sub, half * 512:(half + 1) * 512],
                                                              in1=sig[si], op=ALU.add), r=[r_sig[si], r_R[sub]], w=[r_R[sub]])

                def final_stage(sub):
                    k.op("dve", lambda e: e.memset(ms2[:, sub:sub + 1], 0.0), w=[r_ms2[sub]])
                    k.op("act", lambda e: e.activation(out=junkF, in_=R[:, sub, :], func=AF.Square, scale=1.0 / 32, accum_out=ms2[:, sub:sub + 1]),
                         r=[r_R[sub], r_ms2[sub]], w=[r_cvacc[1], r_ms2[sub]])
                    k.op("act", lambda e: e.activation(out=ms2[:, 4 + sub:5 + sub], in_=ms2[:, sub:sub + 1], func=AF.Ln, bias=epsc[:]), r=[r_ms2[sub], r_const], w=[r_ms2[sub]])
                    k.op("act", lambda e: e.activation(out=rstd2[:, sub:sub + 1], in_=ms2[:, 4 + sub:5 + sub], func=AF.Exp, scale=-0.5), r=[r_ms2[sub]], w=[r_ms2[sub]])
                    k.op("dve", lambda e: e.scalar_tensor_tensor(out=R[:, sub, :], in0=R[:, sub, :], scalar=rstd2[:, sub:sub + 1], in1=gfin[:], op0=ALU.mult, op1=ALU.mult),
                         r=[r_R[sub], r_ms2[sub], r_const], w=[r_R[sub]])
                    k.dma("pool", f"yout{sub}", y_d[seq, t0 + sub * 128:t0 + (sub + 1) * 128, :], R[:, sub, :], r=[r_R[sub]])

                ple_stage(0)
                ple_stage(1)
                final_stage(0)
                ple_stage(2)
                final_stage(1)
                ple_stage(3)
                for b_ in sbg:
                    slab_release(b_)
                final_stage(2)
                final_stage(3)

        k.finalize(final_wait_streams=("yout0", "yout1", "yout2", "yout3", "dbg"))
        print("instructions:", k.nins, "waits:", k.nwait, "counts:", k.cnt, "sim_us: %.1f" % k.sim_time)
    return nc


_NC_CACHE = {}


def _prep_shared(inp):
    f = lambda a: np.ascontiguousarray(np.asarray(a, dtype=np.float32))
    w_in = f(inp["w_in"][0])
    col = lambda g: f(g).reshape(8, 128).T
    gout = np.concatenate([f(inp["g_fox_out"][0]), f(inp["g_mlstm_out"][0])])
    sh = {
        "w_in": w_in,
        "w_out": f(inp["w_out"][0]),
        "w_up": f(inp["w_up"][0]),
        "w_down": f(inp["w_down"][0]),
        "w_ple": f(inp["w_ple"][0]),
        "w_pg": f(inp["w_ple_gate"][0]),
        "wgate": f(np.concatenate([w_in[:, 1536:1544], w_in[:, 3080:3088]], axis=1)),
        "gcol": f(np.concatenate([col(inp["g_mix"][0]), col(inp["g_mlp"][0]), col(inp["g_ple"][0]), col(gout)], axis=1)),
        "gfox": f(f(inp["g_fox_out"][0]).reshape(8, 64).T),
        "gfin": f(inp["g_final"]),
        "gbias": f(np.concatenate([inp["b_fox_f"][0], inp["b_mlstm_i"][0], inp["b_mlstm_f"][0]])),
        "wconv": f(f(inp["w_conv"][0]).reshape(4, 8, 128).transpose(2, 1, 0)),
        "gmixrow": f(f(inp["g_mix"][0]).reshape(1, D)),
        "wc3row": f(f(inp["w_conv"][0])[3].reshape(1, D)),
    }
    return sh


def run(inp, n_cores=8, dbg=None):
    x = np.asarray(inp["x"], dtype=np.float32)
    p = np.asarray(inp["p"], dtype=np.float32)[0]
    B, S, _ = x.shape
    NSEQ = B // n_cores
    key = (NSEQ, S, tuple(sorted(dbg.items())) if dbg else None)
    if key not in _NC_CACHE:
        _NC_CACHE[key] = build_nc(NSEQ, S, dbg)
    nc = _NC_CACHE[key]
    sh = _prep_shared(inp)
    in_maps = []
    for c in range(n_cores):
        m = dict(sh)
        m["x"] = np.ascontiguousarray(x[c * NSEQ:(c + 1) * NSEQ])
        m["p"] = np.ascontiguousarray(p[c * NSEQ:(c + 1) * NSEQ])
        in_maps.append(m)
    res = run_bass_kernel_spmd(nc, in_maps, core_ids=list(range(n_cores)))
    y = np.concatenate([r["y"] for r in res.results], axis=0)
    if dbg:
        return y, res.results
    return y


def kernel(**inputs):
    return run(inputs, 8).astype(np.float32)
```

```python
import heapq
import os
import sys
import types
import numpy as np
import concourse.bass as bass
import concourse.mybir as mybir
from concourse.bass_utils import run_bass_kernel_spmd
from contextlib import ExitStack

F32 = mybir.dt.float32
BF16 = mybir.dt.bfloat16
ALU = mybir.AluOpType
AF = mybir.ActivationFunctionType
AX = mybir.AxisListType

D = 1024
T = 512
NSUB = 4
FH = 8
MH = 4
INC = 3600
DFF = 4096
PLE = 256
EPS = 1e-6
NSLAB = 2
NPT = 3


class Res:
    __slots__ = ("name", "w", "r")

    def __init__(self, name):
        self.name = name
        self.w = None
        self.r = []


def _freeze(fn):
    if fn.__closure__ is None:
        return fn
    cells = tuple(types.CellType(c.cell_contents) for c in fn.__closure__)
    return types.FunctionType(fn.__code__, fn.__globals__, fn.__name__, fn.__defaults__, cells)


class _Probe:
    def __getattr__(self, name):
        def f(*a, **kw):
            out = kw.get("out", a[0] if a else None)
            return out, kw
        return f


def _free_elems(ap):
    n = 1
    for d in ap.shape[1:]:
        n *= int(d)
    return n


_DT_SIZE = {}


class KB:
    ENG = ("pe", "act", "dve", "pool", "sp")

    def __init__(self, nc, es):
        self.nc = nc
        self.es = es
        self.eng = {"pe": nc.tensor, "act": nc.scalar, "dve": nc.vector, "pool": nc.gpsimd, "sp": nc.sync}
        self.sem = {e: es.enter_context(nc.semaphore("s_" + e)) for e in self.ENG}
        self.ops = []
        self.dsem = {}

    def _deps(self, r, w):
        d = set()
        for x in r:
            if x.w is not None:
                d.add(x.w)
        for x in w:
            if x.w is not None:
                d.add(x.w)
            d.update(x.r)
        return d

    def _mark(self, oid, r, w):
        for x in r:
            x.r.append(oid)
        for x in w:
            x.w = oid
            x.r = []

    def _add(self, **kw):
        oid = len(self.ops)
        kw["id"] = oid
        kw["line"] = sys._getframe(2).f_lineno
        self.ops.append(kw)
        return oid

    def op(self, e, fn, r=(), w=(), f=None):
        if f is None:
            out, kw = fn(_Probe())
            f = _free_elems(out)
        base = {"act": 0.20, "dve": 0.12, "pool": 0.25}.get(e, 0.1)
        rate = {"act": 1400.0, "dve": 960.0, "pool": 500.0}.get(e, 1000.0)
        oid = self._add(eng=e, kind="op", fns=[_freeze(fn)], deps=self._deps(r, w), dur=base + f / rate, lat=0.0)
        self._mark(oid, r, w)
        return oid

    def pe(self, fns, r=(), w=(), n=None):
        dur = 0.0
        for fn in fns:
            out, kw = fn(_Probe())
            ni = _free_elems(out)
            passes = 4.0 if ("rhs" in kw and kw["rhs"].dtype == F32) else 1.0
            dur += 0.03 + passes * ni / 2200.0
        oid = self._add(eng="pe", kind="op", fns=[_freeze(fn) for fn in fns], deps=self._deps(r, w), dur=dur, lat=0.0)
        self._mark(oid, r, w)
        return oid

    def dma(self, q, stream, out, in_, r=(), w=(), nbytes=None):
        if stream not in self.dsem:
            self.dsem[stream] = self.es.enter_context(self.nc.semaphore("d_" + stream))
        if nbytes is None:
            nel = 1
            for d in out.shape:
                nel *= int(d)
            nbytes = nel * (2 if out.dtype == BF16 else 4)
        fn = lambda e, out=out, in_=in_: e.dma_start(out=out, in_=in_)
        oid = self._add(eng=q, kind="dma", fns=[fn], deps=self._deps(r, w), dur=(1.0 if q == "pool" else 0.06), lat=2.0 + nbytes / 300000.0, stream=stream)
        self._mark(oid, r, w)
        return oid

    def alias(self, old, new):
        ids = set()
        for o in old:
            if o.w is not None:
                ids.add(o.w)
            ids.update(o.r)
        for n in new:
            n.r = list(set(n.r) | ids)

    def finalize(self, final_wait_streams=()):
        ops = self.ops
        n = len(ops)
        succ = [[] for _ in range(n)]
        ndep = [0] * n
        for o in ops:
            o["deps"].discard(o["id"])
            ndep[o["id"]] = len(o["deps"])
            for d in o["deps"]:
                succ[d].append(o["id"])
        fin = [0.0] * n
        start = [0.0] * n
        efree = {e: 0.0 for e in self.ENG}
        ready_t = [0.0] * n
        heap = []
        for o in ops:
            if ndep[o["id"]] == 0:
                heapq.heappush(heap, (0.0, o["id"]))
        order = []
        while heap:
            est, oid = heapq.heappop(heap)
            o = ops[oid]
            real = max(ready_t[oid], efree[o["eng"]])
            if real > est + 1e-9:
                heapq.heappush(heap, (real, oid))
                continue
            start[oid] = real
            efree[o["eng"]] = real + o["dur"]
            fin[oid] = real + o["dur"] + o["lat"]
            order.append(oid)
            for s_ in succ[oid]:
                lat = 0.04 if ops[s_]["eng"] == o["eng"] else 0.18
                ready_t[s_] = max(ready_t[s_], fin[oid] + lat)
                ndep[s_] -= 1
                if ndep[s_] == 0:
                    heapq.heappush(heap, (max(ready_t[s_], efree[ops[s_]["eng"]]), s_))
        assert len(order) == n, (len(order), n)
        self.sim_time = max(fin) if n else 0.0
        if os.environ.get("KB_ANALYZE"):
            for eng_name in ("pe", "act", "dve"):
                prev_end = 0.0
                busy = 0.0
                blame = {}
                for oid in order:
                    o = ops[oid]
                    if o["eng"] != eng_name:
                        continue
                    gap = start[oid] - prev_end
                    if gap > 0.05 and o["deps"]:
                        d = max(o["deps"], key=lambda d_: fin[d_])
                        key = (ops[d]["eng"], ops[d]["line"], o["line"])
                        blame[key] = blame.get(key, 0.0) + gap
                    busy += o["dur"]
                    prev_end = start[oid] + o["dur"]
                print("ANALYZE %s busy %.0f us of %.0f (%.0f%%)" % (eng_name, busy, self.sim_time, 100 * busy / self.sim_time))
                for key, g in sorted(blame.items(), key=lambda kv: -kv[1])[:14]:
                    print("   idle %.0f us waiting for %s op@line %d (consumer line %d)" % (g, key[0], key[1], key[2]))
        cnt = {e: 0 for e in self.ENG}
        dcnt = {}
        tok = [None] * n
        know = {e: {} for e in self.ENG}
        opknow = [None] * n
        streams = {e: [] for e in self.ENG}
        nwait = 0
        for oid in order:
            o = ops[oid]
            e = o["eng"]
            need = {}
            for d in o["deps"]:
                key, val = tok[d]
                if key == "pe" and e == "pe" and ops[d]["kind"] == "op":
                    continue
                if key not in need or need[key][0] < val:
                    need[key] = (val, d)
            ke = know[e]
            for key, (val, d) in need.items():
                if ke.get(key, 0) >= val:
                    continue
                semh = self.sem[key] if key in self.sem else self.dsem[key[4:]]
                self.eng[e].wait_ge(semh, val)
                streams[e].append(("wait", key, val))
                nwait += 1
                for k2, v2 in opknow[d].items():
                    if ke.get(k2, 0) < v2:
                        ke[k2] = v2
            ins = None
            for fn in o["fns"]:
                ins = fn(self.eng[e])
            if o["kind"] == "dma":
                st = o["stream"]
                dcnt[st] = dcnt.get(st, 0) + 1
                ins.then_inc(self.dsem[st], 16)
                tok[oid] = ("dma:" + st, 16 * dcnt[st])
                streams[e].append(("inc", "dma:" + st, 16))
            else:
                cnt[e] += 1
                ins.then_inc(self.sem[e], 1)
                tok[oid] = (e, cnt[e])
                streams[e].append(("inc", e, 1))
            ok_ = dict(ke)
            tk, tv = tok[oid]
            if ok_.get(tk, 0) < tv:
                ok_[tk] = tv
            opknow[oid] = ok_
        for st in final_wait_streams:
            if st in dcnt:
                self.eng["pool"].wait_ge(self.dsem[st], 16 * dcnt[st])
                streams["pool"].append(("wait", "dma:" + st, 16 * dcnt[st]))
        semv = {}
        pos = {e: 0 for e in self.ENG}
        progress = True
        while progress:
            progress = False
            for e in self.ENG:
                st = streams[e]
                while pos[e] < len(st):
                    kind, key, val = st[pos[e]]
                    if kind == "wait":
                        if semv.get(key, 0) < val:
                            break
                    else:
                        semv[key] = semv.get(key, 0) + val
                    pos[e] += 1
                    progress = True
        for e in self.ENG:
            assert pos[e] == len(streams[e]), ("DEADLOCK in emitted program", e, pos[e], len(streams[e]), streams[e][pos[e]])
        self.cnt = cnt
        self.nwait = nwait
        self.nins = sum(len(o["fns"]) for o in ops)


def build_nc(NSEQ, S, dbg=None):
    nc = bass.Bass("TRN2", target_bir_lowering=False)
    NT = S // T
    NKT = S // 128
    di = lambda n, sh: nc.dram_tensor(n, sh, F32, kind="ExternalInput").ap()
    x_d = di("x", [NSEQ, S, D])
    p_d = di("p", [NSEQ, S, PLE])
    w_in_d = di("w_in", [D, INC])
    w_out_d = di("w_out", [D, D])
    w_up_d = di("w_up", [D, DFF])
    w_down_d = di("w_down", [DFF, D])
    w_ple_d = di("w_ple", [PLE, D])
    w_pg_d = di("w_pg", [D, D])
    wgate_d = di("wgate", [D, 16])
    gcol_d = di("gcol", [128, 32])
    gfox_d = di("gfox", [64, 8])
    gfin_d = di("gfin", [D])
    gbias_d = di("gbias", [16])
    wconv_d = di("wconv", [128, 8, 4])
    gmixrow_d = di("gmixrow", [1, D])
    wc3row_d = di("wc3row", [1, D])
    y_d = nc.dram_tensor("y", [NSEQ, S, D], F32, kind="ExternalOutput").ap()
    dbg_d = {}
    if dbg:
        for n, sh in dbg.items():
            dbg_d[n] = nc.dram_tensor("dbg_" + n, sh, F32, kind="ExternalOutput").ap()
    wb = lambda n, sh: nc.dram_tensor(n, sh, BF16, kind="Internal").ap()
    wb_in = wb("wb_in", [D, INC])
    wb_out = wb("wb_out", [D, D])
    wb_up = wb("wb_up", [D, DFF])
    wb_down = wb("wb_down", [DFF, D])
    wb_ple = wb("wb_ple", [PLE, D])
    wb_pg = wb("wb_pg", [D, D])

    es = ExitStack()
    with es:
        es.enter_context(nc.allow_low_precision("bf16 matmul operands / activations are intended (bf16-reference regime)"))
        k = KB(nc, es)
        SB = lambda n, sh, dt: es.enter_context(nc.sbuf_tensor("sb_" + n, sh, dt))
        Kc = SB("Kc", [128, FH, S], BF16)
        Vcf = SB("Vc", [128, NKT * FH * 65 + 64], BF16)
        Vc = Vcf[:, 0:NKT * FH * 65].rearrange("p (k h e) -> p k h e", k=NKT, h=FH)
        Cst = SB("Cst", [128, MH, 129], F32)
        Csb = SB("Csb", [128, MH, 129], BF16)
        BufA = SB("BufA", [128, 8192], BF16)
        hT = SB("hT", [128, 8, T], BF16)
        mixT = SB("mixT", [128, 8, T], BF16)
        slabs = [SB(f"slab{i}", [128, 8, 512], BF16) for i in range(NSLAB)]
        BufB = SB("BufB", [128, 8192], BF16)
        U2 = SB("U2", [128, 2560], F32)
        U3 = SB("U3", [128, 2080], F32)
        sgo = SB("sgo", [128, MH, T], BF16)
        pleW = SB("pleW", [128, 2, D], BF16)
        ginv = SB("ginv", [128, 4], F32)
        hml = SB("hml", [128, 3, NSUB, FH], BF16)
        ident_f = SB("ident_f", [128, 128], F32)
        ident_b = SB("ident_b", [128, 128], BF16)
        tri_f = SB("tri_f", [128, 128], F32)
        ones_f = SB("ones_f", [128, 128], F32)
        maskT = SB("maskT", [128, 128], BF16)
        Amat = SB("Amat", [128, 64], BF16)
        gfin = SB("gfin", [128, D], F32)
        gcol = SB("gcol", [128, 32], F32)
        gfox = SB("gfox", [64, 8], F32)
        gb = SB("gb", [128, 16], F32)
        wcv = SB("wcv", [128, 8, 4], F32)
        Wg = SB("Wg", [128, 8, 16], BF16)
        halo = SB("halo", [128, 8, 3], F32)
        tot = SB("tot", [1, 16], F32)
        Mt = SB("Mt", [4, 8], F32)
        DG = SB("DG", [4, 32], F32)
        A4 = SB("A4", [4, 4], F32)
        wdec = SB("wdec", [4, 4], F32)
        ms = SB("ms", [128, 8], F32)
        ms2 = SB("ms2", [128, 8], F32)
        h0col = SB("h0col", [128, 8], F32)
        s00 = SB("s00", [1, 4], F32)
        rstd2 = SB("rstd2", [128, 4], F32)
        sigb = SB("sigb", [128, 2, 512], F32)
        rstd = SB("rstd", [128, 8], F32)
        epsc = SB("epsc", [128, 1], F32)
        lnsc = SB("lnsc", [128, 1], F32)
        zg = SB("zg", [128, NSUB, 16], F32)
        e1 = SB("e1", [128, NSUB, 16], F32)
        lsp = SB("lsp", [128, NSUB, 16], F32)
        cpos = SB("cpos", [128, NSUB, 16], F32)
        r1t = SB("r1t", [128, 64], F32)
        alpha = SB("alpha", [128, NSUB, 4], F32)
        mbc = SB("mbc", [128, 32], F32)
        d1 = SB("d1", [128, NSUB, 4], F32)
        es_t = SB("es_t", [128, NSUB, 4], F32)
        clampv = SB("clampv", [128, NSUB, 4], F32)
        sm = SB("sm", [128, 2, 32], F32)
        ps = [es.enter_context(nc.psum_tensor(f"ps{i}", [128, 512], F32)) for i in range(8)]
        psr = [Res(f"ps{i}") for i in range(8)]
        psb = [ps[i][:].bitcast(BF16) for i in range(8)]

        def buf_views(X):
            return dict(
                R=X[:].bitcast(F32).rearrange("p (s d) -> p s d", s=NSUB),
                aT=X[:].rearrange("p (b c t) -> p b c t", b=2, c=8),
                QT=X[:, 0:4096].rearrange("p (h t) -> p h t", h=FH),
                qcT=X[:, 4096:6144].rearrange("p (h t) -> p h t", h=MH),
                kcT=X[:, 6144:8192].rearrange("p (h t) -> p h t", h=MH),
                rR=[Res(f"R{s_}") for s_ in range(NSUB)],
                rUa=[Res(f"U1a{h}") for h in range(FH)],
                rUb=[Res(f"U1b{h}") for h in range(8)],
            )

        bufs = [buf_views(BufA), buf_views(BufB)]
        R = bufs[0]["R"]
        r_R = bufs[0]["rR"]
        aT, QT, qcT, kcT = bufs[1]["aT"], bufs[1]["QT"], bufs[1]["qcT"], bufs[1]["kcT"]
        r_U1a, r_U1b = bufs[1]["rUa"], bufs[1]["rUb"]
        U2b = U2[:].bitcast(BF16)
        U3b = U3[:].bitcast(BF16)
        Pt = [U2b[:, i * 512:(i + 1) * 512] for i in range(NPT)]
        sqx = U2b[:, 1536:2048]
        lnr = U2[:, 1024:1536]
        cv_acc = [U2[:, 1536 + i * 512:1536 + (i + 1) * 512] for i in range(2)]
        junkF = U2b[:, 4096:5120]
        p32 = U2[:, 0:1024].rearrange("p (s d) -> p s d", s=NSUB)
        pbf = U2b[:, 2048:3072].rearrange("p (s d) -> p s d", s=NSUB)
        pT = U2b[:, 3072:4096].rearrange("p (c t) -> p c t", c=2)
        ktm = U3b[:, 0:2048].rearrange("p (s h d) -> p s h d", s=NSUB, h=MH)
        vx = U3b[:, 2048:2048 + NSUB * MH * 129].rearrange("p (s h d) -> p s h d", s=NSUB, h=MH)
        hn = U3b[:, 0:2048].rearrange("p (b d) -> p b d", b=2)
        EK = U3b[:, 2048:3072].rearrange("p (s h e) -> p s h e", s=NSUB, h=FH)
        EQ = U3b[:, 3072:4096].rearrange("p (s h e) -> p s h e", s=NSUB, h=FH)
        sig = [sigb[:, i, :] for i in range(2)]
        prod = [U3[:, 1024 + i * 512:1024 + (i + 1) * 512] for i in range(2)]
        Ssb = SB("Ssb", [128, 2, MH, 128], BF16)
        ybf = SB("ybf", [128, 2, MH, 128], BF16)
        print("sbuf bytes remaining:", nc.sbuf_bytes_remaining)

        r_Kc = [Res(f"Kc{h}") for h in range(FH)]
        r_Vc = Res("Vc")
        r_C = [Res(f"C{h}") for h in range(MH)]
        r_Csb = [Res(f"Csb{h}") for h in range(MH)]
        r_hT = [Res(f"hT{s}") for s in range(NSUB)]
        r_mix = [Res(f"mix{c}") for c in range(8)]
        r_slab = [Res(f"slab{i}") for i in range(NSLAB)]
        r_U2 = Res("U2")
        r_cvacc = [Res(f"cvacc{i}") for i in range(2)]
        r_Pt = [Res(f"Pt{i}") for i in range(NPT)]
        r_sqx = Res("sqx")
        r_lnr = Res("lnr")
        r_p = Res("pbufs")
        r_ktm = [Res(f"ktm{s}") for s in range(NSUB)]
        r_vx = [Res(f"vx{s}") for s in range(NSUB)]
        r_hn = [Res("hn0"), Res("hn1")]
        r_sig = [Res("sig0"), Res("sig1")]
        r_prod = [Res("prod0"), Res("prod1")]
        r_sgo = [Res(f"sgo{h}") for h in range(MH)]
        r_EK = Res("EK")
        r_EQ = Res("EQ")
        r_pleW = Res("pleW")
        r_s00 = Res("s00")
        r_h0 = Res("h0col")
        V_E = [r_EK, r_EQ]
        r_const = Res("const")
        r_halo = [Res(f"halo{c}") for c in range(8)]
        r_tot = Res("tot")
        r_Mt = Res("Mt")
        r_g = Res("gates")
        r_ms = Res("ms")
        r_msS = [Res(f"ms{i}") for i in range(NSUB)]
        r_ms2 = [Res(f"ms2_{i}") for i in range(NSUB)]
        r_sm = Res("sm")
        r_smj = [Res("smj0"), Res("smj1")]
        r_Ssb = [Res("Ssb0"), Res("Ssb1")]
        r_ybf = [Res("ybf0"), Res("ybf1")]
        r_w = {n: Res("w_" + n) for n in ["in", "out", "up", "down", "ple", "pg"]}
        V_att = r_Pt + [r_sqx, r_lnr]
        V_conv = r_cvacc
        V_ple2 = [r_p]
        V_hn = r_hn
        V_ml = r_ktm + r_vx
        V_ple3 = r_sig + r_prod

        bank_rr = {"mm": [0, [0, 1, 2]], "st": [0, [3, 4, 5]], "pv": [0, [6, 7]], "all": [0, [0, 1, 2, 3, 4, 5, 6, 7]]}

        def bank(cls):
            st = bank_rr[cls]
            b = st[1][st[0] % len(st[1])]
            st[0] += 1
            return b

        k.dma("pool", "xin", R[:], x_d[0, 0:T, :].rearrange("(j p) d -> p j d", p=128), w=r_R)
        k.dma("pool", "setup2", Wg[:], wgate_d.rearrange("(c p) n -> p c n", p=128), w=[r_const])
        k.dma("sp", "setup", gcol[:], gcol_d, w=[r_const])
        k.dma("sp", "setup", gfox[:], gfox_d, w=[r_const])
        k.dma("sp", "setup", gfin[:], gfin_d.partition_broadcast(128), w=[r_const])
        k.dma("sp", "setup", gb[:], gbias_d.partition_broadcast(128), w=[r_const])
        k.dma("sp", "setup", wcv[:], wconv_d, w=[r_const])
        k.op("dve", lambda e: e.memset(ident_f[:], 1.0), w=[r_const])
        k.op("pool", lambda e: e.affine_select(out=ident_f[:], in_=ident_f[:], pattern=[[-1, 128]], compare_op=ALU.is_equal,
                                               fill=0.0, base=0, channel_multiplier=1), r=[r_const], w=[r_const])
        k.op("dve", lambda e: e.memset(tri_f[:], 1.0), w=[r_const])
        k.op("pool", lambda e: e.affine_select(out=tri_f[:], in_=tri_f[:], pattern=[[1, 128]], compare_op=ALU.is_ge,
                                               fill=0.0, base=0, channel_multiplier=-1), r=[r_const], w=[r_const])
        k.op("dve", lambda e: e.memset(ones_f[:], 1.0), w=[r_const])
        k.op("dve", lambda e: e.tensor_copy(out=ident_b[:], in_=ident_f[:]), r=[r_const], w=[r_const])
        k.op("dve", lambda e: e.tensor_copy(out=maskT[:], in_=tri_f[:]), r=[r_const], w=[r_const])
        k.op("dve", lambda e: e.memset(Amat[0:64, :], 1.0 / 64), w=[r_const])
        k.op("dve", lambda e: e.memset(Amat[64:65, :], EPS), w=[r_const])
        k.op("dve", lambda e: e.memset(epsc[:], EPS), w=[r_const])
        k.op("dve", lambda e: e.memset(lnsc[:], float(-0.5 * np.log(128.0))), w=[r_const])
        k.op("dve", lambda e: e.reciprocal(out=ginv[:], in_=gcol[:, 28:32]), r=[r_const], w=[r_const])
        k.op("dve", lambda e: e.memset(Vcf[:], 0.0), w=[r_Vc])
        k.op("dve", lambda e: e.memset(Vc[:, :, :, 64:65], 1.0), r=[r_Vc], w=[r_Vc])

        slab_i = [0]

        r_wblk = {}

        def cast_blk(name, dst, src):
            r_wblk[name] = Res("wb_" + name)
            k.dma("pool", "wc_" + name, dst, src, w=[r_wblk[name]])

        for c0 in (0, 512, 1024, 1544, 2056, 2568, 3088):
            cast_blk(f"in{c0}", wb_in[:, c0:c0 + 512], w_in_d[:, c0:c0 + 512])
        for hf in range(2):
            cast_blk(f"out{hf}", wb_out[:, hf * 512:(hf + 1) * 512], w_out_d[:, hf * 512:(hf + 1) * 512])
        for qd in range(4):
            for sl in range(2):
                c0 = qd * 1024 + sl * 512
                cast_blk(f"up{c0}", wb_up[:, c0:c0 + 512], w_up_d[:, c0:c0 + 512])
            for hf in range(2):
                cast_blk(f"down{qd}_{hf}", wb_down[qd * 1024:(qd + 1) * 1024, hf * 512:(hf + 1) * 512], w_down_d[qd * 1024:(qd + 1) * 1024, hf * 512:(hf + 1) * 512])
        cast_blk("ple", wb_ple, w_ple_d)
        for hf in range(2):
            cast_blk(f"pg{hf}", wb_pg[:, hf * 512:(hf + 1) * 512], w_pg_d[:, hf * 512:(hf + 1) * 512])
        k.dma("sp", "setup3", pleW[:], wb_ple.rearrange("(c p) n -> p c n", p=128), r=[r_wblk["ple"]], w=[r_pleW])
        tile_specs = []
        for c0 in (0, 512, 1024, 1544, 2056, 2568, 3088):
            tile_specs.append((f"in{c0}", wb_in[:, c0:c0 + 512].rearrange("(c p) n -> p c n", p=128)))
        for hf in range(2):
            tile_specs.append((f"out{hf}", wb_out[:, hf * 512:(hf + 1) * 512].rearrange("(c p) n -> p c n", p=128)))
        def up_specs(qd):
            for sl in range(2):
                c0 = qd * 1024 + sl * 512
                tile_specs.append((f"up{c0}", wb_up[:, c0:c0 + 512].rearrange("(c p) n -> p c n", p=128)))

        def down_specs(qd):
            for hf in range(2):
                tile_specs.append((f"down{qd}_{hf}", wb_down[qd * 1024:(qd + 1) * 1024, hf * 512:(hf + 1) * 512].rearrange("(c p) n -> p c n", p=128)))

        up_specs(0)
        for qd in range(4):
            if qd + 1 < 4:
                up_specs(qd + 1)
            down_specs(qd)
        for hf in range(2):
            tile_specs.append((f"pg{hf}", wb_pg[:, hf * 512:(hf + 1) * 512].rearrange("(c p) n -> p c n", p=128)))
        slab_specs = tile_specs * (NSEQ * NT)
        slab_pos = [0]
        slab_ready = []
        slab_free = list(range(NSLAB))

        def slab_topup():
            while slab_free and slab_pos[0] < len(slab_specs):
                b = slab_free.pop(0)
                name, src = slab_specs[slab_pos[0]]
                slab_pos[0] += 1
                k.dma("sp", f"slab{b}", slabs[b][:], src, r=[r_wblk[name]], w=[r_slab[b]])
                slab_ready.append((b, name))

        def slab_acquire(name):
            slab_topup()
            b, nm = slab_ready.pop(0)
            assert nm == name, (nm, name)
            return b

        def slab_release(b):
            slab_free.append(b)
            slab_topup()

        def wslab(wap, c0, ncols=512):
            return wap[:, c0:c0 + ncols].rearrange("(c p) n -> p c n", p=128)

        dbg_done = set()

        def dump(name, ap, rlist):
            if name in dbg_d and name not in dbg_done:
                dbg_done.add(name)
                k.dma("pool", "dbg", dbg_d[name], ap, r=rlist)

        def norm_a(sub):
            b = sub % 2
            k.op("dve", lambda e: e.memset(ms[:, sub:sub + 1], 0.0), w=[r_msS[sub]])
            k.op("act", lambda e: e.activation(out=hn[:, b, :], in_=R[:, sub, :], func=AF.Square, scale=1.0 / 32, accum_out=ms[:, sub:sub + 1]),
                 r=[r_R[sub], r_msS[sub]], w=[r_hn[b], r_msS[sub]])
            k.op("act", lambda e: e.activation(out=ms[:, 4 + sub:5 + sub], in_=ms[:, sub:sub + 1], func=AF.Ln, bias=epsc[:]), r=[r_msS[sub], r_const], w=[r_msS[sub]])
            k.op("act", lambda e: e.activation(out=rstd[:, sub:sub + 1], in_=ms[:, 4 + sub:5 + sub], func=AF.Exp, scale=-0.5), r=[r_msS[sub]], w=[r_msS[sub]])
            k.op("dve", lambda e: e.tensor_scalar(out=hn[:, b, :], in0=R[:, sub, :], scalar1=rstd[:, sub:sub + 1], scalar2=None, op0=ALU.mult),
                 r=[r_R[sub], r_msS[sub]], w=[r_hn[b]])

        def norm_b(gidx, sub):
            b = sub % 2
            pb = bank("mm")
            k.pe([(lambda e, c=c: e.transpose(out=psb[pb][:, c * 128:(c + 1) * 128], in_=hn[:, b, c * 128:(c + 1) * 128], identity=ident_b[:]))
                  for c in range(8)], r=[r_hn[b], r_const], w=[psr[pb]])
            k.op("dve", lambda e: e.tensor_tensor(
                out=hT[:, :, sub * 128:(sub + 1) * 128], in0=psb[pb].rearrange("p (c t) -> p c t", c=8),
                in1=gcol[:, gidx * 8:(gidx + 1) * 8].unsqueeze(2).to_broadcast([128, 8, 128]), op=ALU.mult),
                 r=[psr[pb], r_const], w=[r_hT[sub]])

        def stage_then_norm(stage_fn, gidx, after_stage=None):
            stage_fn(0)
            stage_fn(1)
            norm_a(0)
            stage_fn(2)
            norm_a(1)
            norm_b(gidx, 0)
            stage_fn(3)
            if after_stage is not None:
                after_stage()
            norm_a(2)
            norm_b(gidx, 1)
            norm_a(3)
            norm_b(gidx, 2)
            norm_b(gidx, 3)


        def mm_feat(sb, c4, kchunks=8, rhs_fn=None):
            pb = bank("mm")
            k.pe([(lambda e, c=c, pb=pb: e.matmul(ps[pb][:], lhsT=slabs[sb][:, c, c4 * 128:(c4 + 1) * 128], rhs=hT[:, c, :],
                                                   start=(c == 0), stop=(c == kchunks - 1))) for c in range(kchunks)],
                 r=[r_slab[sb]] + r_hT, w=[psr[pb]])
            return pb

        def mm_tok(sb, sub, lhs, rl, kchunks=8, col0=0):
            pb = bank("mm")
            k.pe([(lambda e, c=c, pb=pb: e.matmul(ps[pb][:], lhsT=lhs[:, c, sub * 128:(sub + 1) * 128], rhs=slabs[sb][:, c, col0:col0 + 512],
                                                   start=(c == 0), stop=(c == kchunks - 1))) for c in range(kchunks)],
                 r=[r_slab[sb]] + rl, w=[psr[pb]])
            return pb

        def tok0_precise():
            rowA = U3[0:1, 0:1024]
            rowB = lnr[0:1, :]
            Wf = sigb[:].rearrange("p a (c n) -> p (a c) n", c=4)
            WS = V_hn + [r_lnr]
            for hf in range(2):
                k.dma("sp", "rows", rowB, gmixrow_d[0:1, hf * 512:(hf + 1) * 512], w=WS)
                k.op("dve", lambda e: e.scalar_tensor_tensor(out=rowA[:, hf * 512:(hf + 1) * 512], in0=R[0:1, 0, hf * 512:(hf + 1) * 512], scalar=rstd[0:1, 0:1], in1=rowB,
                                                             op0=ALU.mult, op1=ALU.mult), r=WS + [r_R[0], r_msS[0]], w=WS)
            hb_ = bank("mm")
            k.pe([(lambda e, c=c: e.matmul(ps[hb_][:, c:c + 1], lhsT=rowA[:, c * 128:(c + 1) * 128], rhs=ones_f[0:1, 0:1], start=True, stop=True)) for c in range(8)],
                 r=WS + [r_const], w=[psr[hb_]])
            k.op("dve", lambda e: e.tensor_copy(out=h0col[:], in_=ps[hb_][:, 0:8]), r=[psr[hb_]], w=[r_h0])
            zb = [bank("mm"), bank("mm")]
            for i in range(8):
                k.dma("sp", "wf", Wf, w_in_d[:, 1544 + i * 128:1544 + (i + 1) * 128].rearrange("(c p) n -> p c n", p=128), w=r_sig)
                k.pe([(lambda e, c=c: e.matmul(ps[zb[i // 4]][0:1, (i % 4) * 128:(i % 4 + 1) * 128], lhsT=h0col[:, c:c + 1], rhs=Wf[:, c, :], start=(c == 0), stop=(c == 7)))
                      for c in range(8)], r=r_sig + [r_h0], w=[psr[zb[i // 4]]])
            for hf in range(2):
                k.dma("sp", "rows", rowA[:, hf * 512:(hf + 1) * 512], wc3row_d[0:1, hf * 512:(hf + 1) * 512], w=WS)
                k.op("dve", lambda e: e.tensor_tensor(out=rowA[:, hf * 512:(hf + 1) * 512], in0=ps[zb[hf]][0:1, :], in1=rowA[:, hf * 512:(hf + 1) * 512], op=ALU.mult),
                     r=WS + [psr[zb[hf]]], w=WS)
                k.op("act", lambda e: e.activation(out=rowB, in_=rowA[:, hf * 512:(hf + 1) * 512], func=AF.Exp, scale=-1.0), r=WS, w=WS)
                k.op("act", lambda e: e.activation(out=rowB, in_=rowB, func=AF.Ln, bias=ones_f[0:1, 0:1]), r=WS + [r_const], w=WS)
                k.op("act", lambda e: e.activation(out=rowB, in_=rowB, func=AF.Exp, scale=-1.0), r=WS, w=WS)
                k.op("dve", lambda e: e.tensor_tensor(out=rowA[:, hf * 512:(hf + 1) * 512], in0=rowA[:, hf * 512:(hf + 1) * 512], in1=rowB, op=ALU.mult), r=WS, w=WS)
            k.op("dve", lambda e: e.tensor_tensor(out=rowB, in0=rowA[:, 0:512], in1=rowA[:, 512:1024], op=ALU.mult), r=WS, w=WS)
            k.op("dve", lambda e: e.tensor_reduce(out=s00[:], in_=rowB.rearrange("p (h d) -> p h d", h=MH), axis=AX.X, op=ALU.add), r=WS, w=[r_s00])

        for seq in range(NSEQ):
            for h in range(MH):
                k.op("pool", lambda e, h=h: e.memset(Cst[:, h, :], 0.0), w=[r_C[h]])
            k.op("pool", lambda e: e.memset(halo[:], 0.0), w=r_halo)
            k.op("pool", lambda e: e.memset(tot[:], 0.0), w=[r_tot])
            k.op("pool", lambda e: e.memset(Mt[:], 0.0), w=[r_Mt])
            for ti in range(NT):
                t0 = ti * T
                kt0 = t0 // 128
                nkt = kt0 + NSUB
                g_idx = seq * NT + ti
                bR, bU = bufs[g_idx % 2], bufs[(g_idx + 1) % 2]
                R, r_R = bR["R"], bR["rR"]
                aT, QT, qcT, kcT = bU["aT"], bU["QT"], bU["qcT"], bU["kcT"]
                r_U1a, r_U1b = bU["rUa"], bU["rUb"]
                k.alias(bU["rR"], r_U1a + r_U1b)
                k.alias(V_ple2, V_att + V_conv)
                k.alias(V_ml + V_hn, V_E)
                norm_a(0)
                norm_a(1)
                norm_b(0, 0)
                norm_a(2)
                norm_b(0, 1)
                norm_a(3)
                norm_b(0, 2)
                norm_b(0, 3)
                gbk = bank("mm")
                for sub in range(NSUB):
                    k.pe([(lambda e, c=c: e.matmul(ps[gbk][:, sub * 16:(sub + 1) * 16], lhsT=hT[:, c, sub * 128:(sub + 1) * 128], rhs=Wg[:, c, :],
                                                   start=(c == 0), stop=(c == 7))) for c in range(8)], r=[r_hT[sub], r_const], w=[psr[gbk]])
                k.op("dve", lambda e: e.tensor_tensor(out=zg[:], in0=ps[gbk][:, 0:64].rearrange("p (s g) -> p s g", s=NSUB),
                                                      in1=gb[:].unsqueeze(1).to_broadcast([128, NSUB, 16]), op=ALU.add), r=[psr[gbk], r_const], w=[r_g])
                k.op("act", lambda e: e.activation(out=e1[:], in_=zg[:], func=AF.Exp, scale=-1.0), r=[r_g], w=[r_g])
                k.op("act", lambda e: e.activation(out=lsp[:], in_=e1[:], func=AF.Ln, bias=ones_f[:, 0:1]), r=[r_g, r_const], w=[r_g])
                k.op("pool", lambda e: e.memset(EK[:], 0.0), w=[r_EK])
                k.op("pool", lambda e: e.memset(EQ[:], 0.0), w=[r_EQ])
                k.op("pool", lambda e: e.memset(EK[:, :, :, 0:3], 1.0), r=[r_EK], w=[r_EK])
                k.op("pool", lambda e: e.memset(EQ[:, :, :, 3:6], 1.0), r=[r_EQ], w=[r_EQ])
                sb = slab_acquire("in0")
                for j in range(4):
                    pb = mm_feat(sb, j)
                    k.op("act", lambda e: e.activation(out=QT[0:64, 2 * j, :], in_=ps[pb][0:64, :], func=AF.Copy, scale=0.125), r=[psr[pb]], w=[r_U1a[2 * j]])
                    k.op("dve", lambda e: e.tensor_scalar(out=QT[0:64, 2 * j + 1, :], in0=ps[pb][64:128, :], scalar1=0.125, scalar2=None, op0=ALU.mult),
                         r=[psr[pb]], w=[r_U1a[2 * j + 1]])
                slab_release(sb)
                cbk = bank("mm")
                for j in range(NSUB):
                    fns = [lambda e, j=j: e.matmul(ps[cbk][:, j * 16:(j + 1) * 16], lhsT=tri_f[:], rhs=lsp[:, j, :], start=True, stop=False)]
                    for i in range(j):
                        fns.append(lambda e, j=j, i=i: e.matmul(ps[cbk][:, j * 16:(j + 1) * 16], lhsT=ones_f[:], rhs=lsp[:, i, :], start=False, stop=False))
                    fns.append(lambda e, j=j: e.matmul(ps[cbk][:, j * 16:(j + 1) * 16], lhsT=ones_f[0:1, :], rhs=tot[0:1, :], start=False, stop=True))
                    k.pe(fns, r=[r_g, r_tot, r_const], w=[psr[cbk]])
                k.op("dve", lambda e: e.tensor_copy(out=cpos[:], in_=ps[cbk][:, 0:64].rearrange("p (s g) -> p s g", s=NSUB)), r=[psr[cbk]], w=[r_g])
                tbk = bank("mm")
                fns = [(lambda e, i=i: e.matmul(ps[tbk][0:1, 0:16], lhsT=ones_f[:, 0:1], rhs=lsp[:, i, :], start=(i == 0), stop=False)) for i in range(NSUB)]
                fns.append(lambda e: e.matmul(ps[tbk][0:1, 0:16], lhsT=ones_f[0:1, 0:1], rhs=tot[0:1, :], start=False, stop=True))
                k.pe(fns, r=[r_g, r_tot, r_const], w=[psr[tbk]])
                k.op("dve", lambda e: e.tensor_copy(out=tot[:], in_=ps[tbk][0:1, 0:16]), r=[psr[tbk]], w=[r_tot])
                sb = slab_acquire("in512")
                for j in range(4):
                    pb = mm_feat(sb, j)
                    k.op("act", lambda e: e.activation(out=Kc[0:64, 2 * j, t0:t0 + T], in_=ps[pb][0:64, :], func=AF.Copy), r=[psr[pb]], w=[r_Kc[2 * j]])
                    k.op("dve", lambda e: e.tensor_copy(out=Kc[0:64, 2 * j + 1, t0:t0 + T], in_=ps[pb][64:128, :]), r=[psr[pb]], w=[r_Kc[2 * j + 1]])
                slab_release(sb)
                cf = cpos[:, :, 0:8]
                r1v = r1t[:, 0:32].rearrange("p (s h) -> p s h", s=NSUB)
                r2v = r1t[:, 32:64].rearrange("p (s h) -> p s h", s=NSUB)
                k.op("dve", lambda e: e.tensor_copy(out=hml[:, 0], in_=cf), r=[r_g], w=[r_sm])
                k.op("dve", lambda e: e.tensor_tensor(out=r1v, in0=cf, in1=hml[:, 0], op=ALU.subtract), r=[r_g, r_sm], w=[r_sm])
                k.op("dve", lambda e: e.tensor_copy(out=hml[:, 1], in_=r1v), r=[r_sm], w=[r_sm])
                k.op("dve", lambda e: e.tensor_tensor(out=r2v, in0=r1v, in1=hml[:, 1], op=ALU.subtract), r=[r_sm], w=[r_sm])
                k.op("dve", lambda e: e.tensor_copy(out=hml[:, 2], in_=r2v), r=[r_sm], w=[r_sm])
                for i3 in range(3):
                    k.op("dve", lambda e: e.tensor_copy(out=EK[:, :, :, 3 + i3], in_=hml[:, i3]), r=[r_sm, r_EK], w=[r_EK])
                    k.op("dve", lambda e: e.tensor_scalar(out=EQ[:, :, :, i3], in0=hml[:, i3], scalar1=-1.0, scalar2=None, op0=ALU.mult), r=[r_sm, r_EQ], w=[r_EQ])
                k.op("dve", lambda e: e.tensor_tensor(out=alpha[:], in0=zg[:, :, 8:12], in1=cpos[:, :, 12:16], op=ALU.add), r=[r_g], w=[r_g])
                abk = bank("mm")
                k.pe([(lambda e, j=j: e.matmul(ps[abk][0:4, j * 128:(j + 1) * 128], lhsT=alpha[:, j, :], rhs=ident_f[:], start=True, stop=True)) for j in range(NSUB)],
                     r=[r_g, r_const], w=[psr[abk]])
                k.op("dve", lambda e: e.tensor_reduce(out=A4[:], in_=ps[abk][0:4, :].rearrange("p (s t) -> p s t", s=NSUB), axis=AX.X, op=ALU.max),
                     r=[psr[abk]], w=[r_Mt])
                for j in range(NSUB):
                    k.op("dve", lambda e: e.tensor_tensor(out=Mt[:, j + 1:j + 2], in0=Mt[:, j:j + 1], in1=A4[:, j:j + 1], op=ALU.max), r=[r_Mt], w=[r_Mt])
                k.op("dve", lambda e: e.tensor_tensor(out=wdec[:], in0=Mt[:, 0:4], in1=Mt[:, 1:5], op=ALU.subtract), r=[r_Mt], w=[r_Mt])
                k.op("act", lambda e: e.activation(out=wdec[:], in_=wdec[:], func=AF.Exp), r=[r_Mt], w=[r_Mt])
                k.op("dve", lambda e: e.tensor_tensor(out=DG[:, 0:16].rearrange("p (j h) -> p j h", j=NSUB), in0=Mt[:, 1:5].unsqueeze(2).to_broadcast([4, NSUB, 4]),
                                                      in1=ident_f[0:4, 0:4].unsqueeze(1).to_broadcast([4, NSUB, 4]), op=ALU.mult), r=[r_Mt, r_const], w=[r_Mt])
                k.op("dve", lambda e: e.tensor_tensor(out=DG[:, 16:32].rearrange("p (j h) -> p j h", j=NSUB), in0=wdec[:].unsqueeze(2).to_broadcast([4, NSUB, 4]),
                                                      in1=ident_f[0:4, 0:4].unsqueeze(1).to_broadcast([4, NSUB, 4]), op=ALU.mult), r=[r_Mt, r_const], w=[r_Mt])
                k.op("dve", lambda e: e.tensor_copy(out=Mt[:, 0:1], in_=Mt[:, 4:5]), r=[r_Mt], w=[r_Mt])
                bbk = bank("mm")
                k.pe([lambda e: e.matmul(ps[bbk][:, 0:32], lhsT=ones_f[0:4, :], rhs=DG[:], start=True, stop=True)], r=[r_Mt, r_const], w=[psr[bbk]])
                k.op("dve", lambda e: e.tensor_copy(out=mbc[:], in_=ps[bbk][:, 0:32]), r=[psr[bbk]], w=[r_g])
                mbM = mbc[:, 0:16].rearrange("p (j h) -> p j h", j=NSUB)
                k.op("dve", lambda e: e.tensor_tensor(out=d1[:], in0=alpha[:], in1=mbM, op=ALU.subtract), r=[r_g], w=[r_g])
                k.op("act", lambda e: e.activation(out=es_t[:], in_=d1[:], func=AF.Exp, bias=lnsc[:]), r=[r_g, r_const], w=[r_g])
                k.op("dve", lambda e: e.tensor_tensor(out=d1[:], in0=cpos[:, :, 12:16], in1=mbM, op=ALU.subtract), r=[r_g], w=[r_g])
                k.op("act", lambda e: e.activation(out=clampv[:], in_=d1[:], func=AF.Exp), r=[r_g], w=[r_g])
                sb = slab_acquire("in1024")
                for sub in range(NSUB):
                    pb = mm_tok(sb, sub, hT, [r_hT[sub]])
                    if sub % 2 == 0:
                        k.op("act", lambda e: e.activation(out=Vc[:, kt0 + sub, :, 0:64], in_=ps[pb][:].rearrange("p (h d) -> p h d", h=FH), func=AF.Copy), r=[psr[pb]], w=[r_Vc])
                    else:
                        k.op("dve", lambda e: e.tensor_copy(out=Vc[:, kt0 + sub, :, 0:64], in_=ps[pb][:].rearrange("p (h d) -> p h d", h=FH)), r=[psr[pb]], w=[r_Vc])
                slab_release(sb)
                exk = [bank("all") for _ in range(3)]
                exq = [bank("all") for _ in range(3)]
                hgroups = [(0, 3), (3, 3), (6, 2)]
                for EX, rEX, banks in ((EK, r_EK, exk), (EQ, r_EQ, exq)):
                    for gi, (h0, n) in enumerate(hgroups):
                        k.pe([(lambda e, sub=sub: e.matmul(ps[banks[gi]][0:n * 32, sub * 128:(sub + 1) * 128], lhsT=EX[:, sub, h0:h0 + n, :].rearrange("p h e -> p (h e)"),
                                                           rhs=ident_b[:], start=True, stop=True)) for sub in range(NSUB)], r=[rEX, r_const], w=[psr[banks[gi]]])
                for h in range(FH):
                    gi, row = h // 3, 32 * (h % 3)
                    k.op("dve", lambda e: e.tensor_copy(out=Kc[64:70, h, t0:t0 + T], in_=ps[exk[gi]][row:row + 6, :]), r=[psr[exk[gi]]], w=[r_Kc[h]])
                    k.op("act", lambda e: e.activation(out=QT[64:70, h, :], in_=ps[exq[gi]][row:row + 6, :], func=AF.Copy), r=[psr[exq[gi]]], w=[r_U1a[h]])

                if ti == 0:
                    tok0_precise()
                k.alias(V_hn, r_ktm)
                k.alias(V_E, r_vx)
                bg = []
                slab_of = {}

                def conv_a(which, col0, c4):
                    if c4 == 0:
                        slab_of[col0] = slab_acquire(f"in{col0}")
                    sbx = slab_of[col0]
                    ch = which * 4 + c4
                    pb = mm_feat(sbx, c4)
                    if c4 == 3:
                        slab_release(sbx)
                    cb = ch % 2
                    acc = cv_acc[cb]
                    k.op("dve", lambda e: e.tensor_scalar(out=acc, in0=ps[pb][:, 0:512], scalar1=wcv[:, ch, 3:4], scalar2=None, op0=ALU.mult),
                         r=[psr[pb], r_const], w=[r_cvacc[cb]])
                    for jj in (2, 1, 0):
                        sh = 3 - jj
                        k.op("dve", lambda e: e.scalar_tensor_tensor(out=acc[:, sh:512], in0=ps[pb][:, 0:512 - sh], scalar=wcv[:, ch, jj:jj + 1], in1=acc[:, sh:512],
                                                                     op0=ALU.mult, op1=ALU.add), r=[psr[pb], r_cvacc[cb], r_const], w=[r_cvacc[cb]])
                        k.op("dve", lambda e: e.scalar_tensor_tensor(out=acc[:, 0:sh], in0=halo[:, ch, 3 - sh:3], scalar=wcv[:, ch, jj:jj + 1], in1=acc[:, 0:sh],
                                                                     op0=ALU.mult, op1=ALU.add), r=[r_halo[ch], r_cvacc[cb], r_const], w=[r_cvacc[cb]])
                    k.op("dve", lambda e: e.tensor_copy(out=halo[:, ch, :], in_=ps[pb][:, 509:512]), r=[psr[pb]], w=[r_halo[ch]])

                def conv_b(which, dst, rr_off, c4):
                    ch = which * 4 + c4
                    cb = ch % 2
                    acc = cv_acc[cb]
                    etmp = Ssb[:, cb, :, :].rearrange("p h t -> p (h t)")
                    k.op("act", lambda e: e.activation(out=etmp, in_=acc, func=AF.Exp, scale=-1.0), r=[r_cvacc[cb]], w=[r_Ssb[cb]])
                    k.op("act", lambda e: e.activation(out=etmp, in_=etmp, func=AF.Ln, bias=ones_f[:, 0:1]), r=[r_Ssb[cb], r_const], w=[r_Ssb[cb]])
                    k.op("act", lambda e: e.activation(out=etmp, in_=etmp, func=AF.Exp, scale=-1.0), r=[r_Ssb[cb]], w=[r_Ssb[cb]])
                    k.op("dve", lambda e: e.tensor_tensor(out=dst[:, c4, :], in0=acc, in1=etmp, op=ALU.mult), r=[r_cvacc[cb], r_Ssb[cb]], w=[r_U1b[rr_off + c4]])

                def mv_unit(sub):
                    if sub == 0:
                        slab_of["mv"] = slab_acquire("in2568")
                    pb = mm_tok(slab_of["mv"], sub, hT, [r_hT[sub]])
                    if sub == NSUB - 1:
                        slab_release(slab_of["mv"])
                    k.op("pool", lambda e: e.memset(vx[:, sub, :, 128:129], 1.0), w=[r_vx[sub]])
                    k.op("dve", lambda e: e.tensor_copy(out=vx[:, sub, :, 0:128], in_=ps[pb][:].rearrange("p (h d) -> p h d", h=MH)), r=[psr[pb]], w=[r_vx[sub]])
                    k.op("dve", lambda e: e.tensor_tensor(out=vx[:, sub, :, :], in0=vx[:, sub, :, :], in1=es_t[:, sub, :].unsqueeze(2).to_broadcast([128, MH, 129]), op=ALU.mult),
                         r=[r_vx[sub], r_g], w=[r_vx[sub]])

                def mo_unit(c4):
                    if c4 == 0:
                        slab_of["mo"] = slab_acquire("in3088")
                    pb = mm_feat(slab_of["mo"], c4)
                    if c4 == 3:
                        slab_release(slab_of["mo"])
                    k.op("act", lambda e: e.activation(out=sgo[:, c4, :], in_=ps[pb][:], func=AF.Exp, scale=-1.0), r=[psr[pb]], w=[r_sgo[c4]])
                    k.op("act", lambda e: e.activation(out=sgo[:, c4, :], in_=sgo[:, c4, :], func=AF.Ln, bias=ones_f[:, 0:1]), r=[r_sgo[c4], r_const], w=[r_sgo[c4]])
                    k.op("act", lambda e: e.activation(out=sgo[:, c4, :], in_=sgo[:, c4, :], func=AF.Exp, scale=-1.0), r=[r_sgo[c4]], w=[r_sgo[c4]])
                    k.op("dve", lambda e: e.tensor_scalar(out=sgo[:, c4, :], in0=sgo[:, c4, :], scalar1=gcol[:, 28 + c4:29 + c4], scalar2=None, op0=ALU.mult),
                         r=[r_sgo[c4], r_const], w=[r_sgo[c4]])

                ua, ub_ = [], []
                for which, col0, dst, rr_off in ((0, 1544, qcT, 0), (1, 2056, kcT, 4)):
                    for c4 in range(4):
                        ua.append(lambda which=which, col0=col0, c4=c4: conv_a(which, col0, c4))
                        ub_.append(lambda which=which, dst=dst, rr_off=rr_off, c4=c4: conv_b(which, dst, rr_off, c4))
                bg.append(ua[0])
                for n in range(1, 8):
                    bg.append(ua[n])
                    bg.append(ub_[n - 1])
                bg.append(ub_[7])
                for sub in range(NSUB):
                    bg.append(lambda sub=sub: mv_unit(sub))
                post_units = [lambda c4=c4: mo_unit(c4) for c4 in range(4)]


                pairs = [(h, kt) for h in range(FH) for kt in range(nkt)]
                sbank = {}
                pvbank = {}

                def lo_of(kt):
                    return max(0, (kt - kt0) * 128)

                def emit_S(idx):
                    h, kt = pairs[idx]
                    b = bank("st")
                    sbank[idx] = b
                    lo = lo_of(kt)
                    k.pe([lambda e: e.matmul(ps[b][:, lo:T], lhsT=Kc[0:70, h, kt * 128:(kt + 1) * 128], rhs=QT[0:70, h, lo:T], start=True, stop=True)],
                         r=[r_Kc[h], r_U1a[h]], w=[psr[b]])

                def epilogue(h):
                    pvb = pvbank[h]
                    k.op("act", lambda e: e.activation(out=sqx[0:65, :], in_=ps[pvb][0:65, :], func=AF.Square), r=[psr[pvb]], w=[r_sqx])
                    rb = bank("mm")
                    k.pe([lambda e: e.matmul(ps[rb][0:64, :], lhsT=Amat[0:65, :], rhs=sqx[0:65, :], start=True, stop=True)], r=[r_sqx, r_const], w=[psr[rb]])
                    k.op("act", lambda e: e.activation(out=lnr[0:64, :], in_=ps[rb][0:64, :], func=AF.Ln), r=[psr[rb]], w=[r_lnr])
                    k.op("act", lambda e: e.activation(out=lnr[0:64, :], in_=lnr[0:64, :], func=AF.Exp, scale=-0.5), r=[r_lnr], w=[r_lnr])
                    po = (h % 2) * 64
                    k.op("dve", lambda e: e.scalar_tensor_tensor(out=mixT[po:po + 64, h // 2, :], in0=ps[pvb][0:64, :], scalar=gfox[0:64, h:h + 1], in1=lnr[0:64, :],
                                                                 op0=ALU.mult, op1=ALU.mult), r=[psr[pvb], r_lnr, r_const], w=[r_mix[h // 2]])

                def attn_range(p0, p1, with_bg):
                    for i_ in range(p0, min(p0 + 2, p1)):
                        emit_S(i_)
                    for idx in range(p0, p1):
                        h, kt = pairs[idx]
                        if idx + 2 < p1:
                            emit_S(idx + 2)
                        b = sbank[idx]
                        lo = lo_of(kt)
                        pi = idx % NPT
                        k.op("act", lambda e: e.activation(out=Pt[pi][:, lo:T], in_=ps[b][:, lo:T], func=AF.Exp), r=[psr[b]], w=[r_Pt[pi]])
                        if kt >= kt0:
                            k.op("dve", lambda e: e.tensor_tensor(out=Pt[pi][:, lo:lo + 128], in0=Pt[pi][:, lo:lo + 128], in1=maskT[:], op=ALU.mult),
                                 r=[r_Pt[pi], r_const], w=[r_Pt[pi]])
                        if kt == 0:
                            pvbank[h] = bank("pv")
                        pvb = pvbank[h]
                        vo = (kt * FH + h) * 65
                        k.pe([lambda e: e.matmul(ps[pvb][:, lo:T], lhsT=Vcf[:, vo:vo + 128], rhs=Pt[pi][:, lo:T], start=(kt == 0), stop=(kt == nkt - 1))],
                             r=[r_Pt[pi], r_Vc], w=[psr[pvb]])
                        if kt == nkt - 1:
                            epilogue(h)
                        if with_bg and bg and (idx + 1) % bg_stride == 0:
                            bg.pop(0)()

                p_mid = (FH // 2) * nkt
                bg_stride = max(1, p_mid // (len(bg) + 2))
                attn_range(0, p_mid, True)
                while bg:
                    bg.pop(0)()
                for u in post_units:
                    u()
                st1 = {}
                st2 = {}

                def ml_stage1(j):
                    js = slice(j * 128, (j + 1) * 128)
                    jb = j % 2
                    ktb = 0
                    k.pe([(lambda e, h=h: e.transpose(out=psb[ktb][:, h * 128:(h + 1) * 128], in_=kcT[:, h, js], identity=ident_b[:])) for h in range(MH)],
                         r=r_U1b[4:8] + [r_const], w=[psr[ktb]])
                    k.op("dve", lambda e: e.tensor_copy(out=ktm[:, j, :, :], in_=psb[ktb][:, 0:512].rearrange("p (h d) -> p h d", h=MH)), r=[psr[ktb]], w=[r_ktm[j]])
                    stb = 1
                    k.pe([(lambda e, h=h: e.matmul(ps[stb][:, h * 128:(h + 1) * 128], lhsT=kcT[:, h, js], rhs=qcT[:, h, js], start=True, stop=True)) for h in range(MH)],
                         r=r_U1b, w=[psr[stb]])
                    k.op("dve", lambda e: e.tensor_tensor(out=Ssb[:, jb, :, :], in0=ps[stb][:].rearrange("p (h t) -> p h t", h=MH),
                                                          in1=maskT[:].unsqueeze(1).to_broadcast([128, MH, 128]), op=ALU.mult), r=[psr[stb], r_const], w=[r_Ssb[jb]])
                    if ti == 0 and j == 0:
                        k.op("dve", lambda e: e.tensor_copy(out=Ssb[0:1, jb, :, 0:1], in_=s00[:].unsqueeze(2)), r=[r_s00, r_Ssb[jb]], w=[r_Ssb[jb]])
                    ub = [2, 1]
                    k.pe([(lambda e, h=h: e.matmul(ps[2][:, h * 129:(h + 1) * 129], lhsT=ktm[:, j, h, :], rhs=vx[:, j, h, :], start=True, stop=True)) for h in range(3)],
                         r=[r_ktm[j], r_vx[j]], w=[psr[2]])
                    k.pe([lambda e: e.matmul(ps[1][:, 0:129], lhsT=ktm[:, j, 3, :], rhs=vx[:, j, 3, :], start=True, stop=True)], r=[r_ktm[j], r_vx[j]], w=[psr[1]])
                    st1[j] = ub

                def ml_stage2(j):
                    js = slice(j * 128, (j + 1) * 128)
                    jb = j % 2
                    ub = st1[j]
                    wbc = mbc[:, 16 + 4 * j:20 + 4 * j].unsqueeze(2).to_broadcast([128, MH, 129])
                    k.op("dve", lambda e: e.tensor_tensor(out=Cst[:], in0=Cst[:], in1=wbc, op=ALU.mult), r=r_C + [r_g], w=r_C)
                    k.op("dve", lambda e: e.tensor_copy(out=Csb[:], in_=Cst[:]), r=r_C, w=r_Csb)
                    k.op("dve", lambda e: e.tensor_tensor(out=Cst[:, 0:3, :], in0=Cst[:, 0:3, :], in1=ps[2][:, 0:387].rearrange("p (h d) -> p h d", h=3), op=ALU.add),
                         r=r_C + [psr[2]], w=r_C)
                    k.op("dve", lambda e: e.tensor_tensor(out=Cst[:, 3, :], in0=Cst[:, 3, :], in1=ps[1][:, 0:129], op=ALU.add), r=r_C + [psr[1]], w=r_C)
                    nb = [0, 1]
                    for hb in range(2):
                        fns = []
                        for hh in range(2):
                            h = 2 * hb + hh
                            fns.append(lambda e, h=h, hh=hh: e.matmul(ps[nb[hb]][:, hh * 129:(hh + 1) * 129], lhsT=Ssb[:, jb, h, :], rhs=vx[:, j, h, :], start=True, stop=False))
                            fns.append(lambda e, h=h, hh=hh: e.matmul(ps[nb[hb]][:, hh * 129:(hh + 1) * 129], lhsT=qcT[:, h, js], rhs=Csb[:, h, :], start=False, stop=True))
                        k.pe(fns, r=[r_Ssb[jb], r_vx[j]] + r_U1b[0:4] + r_Csb, w=[psr[nb[hb]]])
                    st2[j] = nb

                def ml_stage3a(j):
                    jb = j % 2
                    nb = st2[j]
                    smj = sm[:, jb, :]
                    k.op("dve", lambda e: e.memset(smj[:, 0:4], 0.0), w=[r_smj[jb]])
                    for h in range(MH):
                        hb, hh = h // 2, h % 2
                        k.op("act", lambda e: e.activation(out=ybf[:, jb, h, :], in_=ps[nb[hb]][:, hh * 129:hh * 129 + 128], func=AF.Square, scale=float(128 ** -0.5),
                                                           accum_out=smj[:, h:h + 1]), r=[psr[nb[hb]], r_smj[jb]], w=[r_ybf[jb], r_smj[jb]])
                    for hb in range(2):
                        k.op("dve", lambda e: e.tensor_copy(out=smj[:, 4 + 2 * hb:6 + 2 * hb], in_=ps[nb[hb]][:, 0:258].rearrange("p (h d) -> p h d", h=2)[:, :, 128]),
                             r=[psr[nb[hb]], r_smj[jb]], w=[r_smj[jb]])
                    k.op("dve", lambda e: e.scalar_tensor_tensor(out=smj[:, 8:12], in0=smj[:, 4:8], scalar=-1.0, in1=smj[:, 4:8], op0=ALU.mult, op1=ALU.max), r=[r_smj[jb]], w=[r_smj[jb]])
                    k.op("dve", lambda e: e.tensor_tensor(out=smj[:, 12:16], in0=smj[:, 8:12], in1=clampv[:, j, :], op=ALU.max), r=[r_smj[jb], r_g], w=[r_smj[jb]])
                    k.op("dve", lambda e: e.scalar_tensor_tensor(out=smj[:, 16:20], in0=smj[:, 12:16], scalar=EPS, in1=smj[:, 12:16], op0=ALU.mult, op1=ALU.mult), r=[r_smj[jb]], w=[r_smj[jb]])
                    k.op("dve", lambda e: e.tensor_tensor(out=smj[:, 20:24], in0=smj[:, 16:20], in1=smj[:, 0:4], op=ALU.add), r=[r_smj[jb]], w=[r_smj[jb]])
                    k.op("act", lambda e: e.activation(out=smj[:, 24:28], in_=smj[:, 20:24], func=AF.Ln), r=[r_smj[jb]], w=[r_smj[jb]])
                    k.op("act", lambda e: e.activation(out=smj[:, 28:32], in_=smj[:, 24:28], func=AF.Exp, scale=-0.5), r=[r_smj[jb]], w=[r_smj[jb]])
                    for hb in range(2):
                        k.op("dve", lambda e: e.tensor_tensor(out=ybf[:, jb, 2 * hb:2 * hb + 2, :], in0=ps[nb[hb]][:, 0:258].rearrange("p (h d) -> p h d", h=2)[:, :, 0:128],
                                                              in1=smj[:, 28 + 2 * hb:30 + 2 * hb].unsqueeze(2).to_broadcast([128, 2, 128]), op=ALU.mult),
                             r=[psr[nb[hb]], r_smj[jb]], w=[r_ybf[jb]])

                def ml_stage3b(j):
                    js = slice(j * 128, (j + 1) * 128)
                    jb = j % 2
                    yb = 2
                    k.pe([(lambda e, h=h: e.transpose(out=psb[yb][:, h * 128:(h + 1) * 128], in_=ybf[:, jb, h, :], identity=ident_b[:])) for h in range(MH)],
                         r=[r_ybf[jb], r_const], w=[psr[yb]])
                    k.op("dve", lambda e: e.tensor_tensor(out=mixT[:, 4:8, js], in0=psb[yb][:, 0:512].rearrange("p (h t) -> p h t", h=MH), in1=sgo[:, :, js], op=ALU.mult),
                         r=[psr[yb]] + r_sgo, w=r_mix[4:8])

                for j in range(NSUB):
                    ml_stage1(j)
                    ml_stage2(j)
                    ml_stage3a(j)
                    ml_stage3b(j)
                dump("mixml", mixT[:, 4:8, :], r_mix[4:8])
                attn_range(p_mid, len(pairs), False)
                dump("mixfox", mixT[:, 0:4, :], r_mix[0:4])


                k.alias(V_ml, V_hn)
                sbo = [slab_acquire(f"out{half}") for half in range(2)]

                def wout_stage(sub):
                    for half in range(2):
                        pb = mm_tok(sbo[half], sub, mixT, r_mix)
                        k.op("dve", lambda e: e.tensor_tensor(out=R[:, sub, half * 512:(half + 1) * 512], in0=R[:, sub, half * 512:(half + 1) * 512],
                                                              in1=ps[pb][:], op=ALU.add), r=[psr[pb], r_R[sub]], w=[r_R[sub]])

                stage_then_norm(wout_stage, 1, after_stage=lambda: [slab_release(b) for b in sbo])
                dump("x1", R[:], r_R)
                def up_stage(qd):
                    ab = qd % 2
                    r_aT = r_U1a if ab == 0 else r_U1b
                    for sl in range(2):
                        sb = slab_acquire(f"up{qd * 1024 + sl * 512}")
                        for c4 in range(4):
                            pb = mm_feat(sb, c4)
                            if c4 == 3:
                                slab_release(sb)
                            kk = sl * 4 + c4
                            k.op("act", lambda e: e.activation(out=aT[:, ab, kk, :], in_=ps[pb][:], func=AF.Relu), r=[psr[pb]], w=[r_aT[kk]])
                            k.op("dve", lambda e: e.tensor_tensor(out=aT[:, ab, kk, :], in0=aT[:, ab, kk, :], in1=aT[:, ab, kk, :], op=ALU.mult), r=[r_aT[kk]], w=[r_aT[kk]])

                def down_one(qd, sbx, sub, half):
                    ab = qd % 2
                    r_aT = r_U1a if ab == 0 else r_U1b
                    pb = mm_tok(sbx, sub, aT[:, ab], r_aT)
                    k.op("dve", lambda e: e.tensor_tensor(out=R[:, sub, half * 512:(half + 1) * 512], in0=R[:, sub, half * 512:(half + 1) * 512],
                                                          in1=ps[pb][:], op=ALU.add), r=[psr[pb], r_R[sub]], w=[r_R[sub]])

                up_stage(0)
                for qd in range(4):
                    if qd + 1 < 4:
                        up_stage(qd + 1)
                    if qd < 3:
                        for half in range(2):
                            sbx = slab_acquire(f"down{qd}_{half}")
                            for sub in range(NSUB):
                                down_one(qd, sbx, sub, half)
                            slab_release(sbx)
                    else:
                        sbd = [slab_acquire(f"down{qd}_{half}") for half in range(2)]

                        def down_stage(sub):
                            for half in range(2):
                                down_one(3, sbd[half], sub, half)

                        stage_then_norm(down_stage, 2, after_stage=lambda: [slab_release(b) for b in sbd])
                dump("x2", R[:], r_R)
                if g_idx + 1 < NSEQ * NT:
                    nseq, nti = divmod(g_idx + 1, NT)
                    k.alias(r_U1a + r_U1b, bU["rR"])
                    k.dma("pool", "xin", bU["R"], x_d[nseq, nti * T:(nti + 1) * T, :].rearrange("(j p) d -> p j d", p=128), w=bU["rR"])
                k.alias(V_conv + V_att, V_ple2)
                k.dma("pool", "pin", p32, p_d[seq, t0:t0 + T, :].rearrange("(j p) d -> p j d", p=128), w=[r_p])
                k.op("dve", lambda e: e.tensor_copy(out=pbf, in_=p32), r=[r_p], w=[r_p])
                for sub in range(NSUB):
                    pb = bank("mm")
                    k.pe([(lambda e, c=c: e.transpose(out=psb[pb][:, c * 128:(c + 1) * 128], in_=pbf[:, sub, c * 128:(c + 1) * 128], identity=ident_b[:])) for c in range(2)],
                         r=[r_p, r_const], w=[psr[pb]])
                    k.op("dve", lambda e: e.tensor_copy(out=pT[:, :, sub * 128:(sub + 1) * 128], in_=psb[pb][:, 0:256].rearrange("p (c t) -> p c t", c=2)),
                         r=[psr[pb]], w=[r_p])
                sbg = [slab_acquire(f"pg{half}") for half in range(2)]

                def ple_stage(sub):
                    for half in range(2):
                        gbk2 = mm_tok(sbg[half], sub, hT, [r_hT[sub]])
                        ebk = bank("mm")
                        k.pe([(lambda e, c=c: e.matmul(ps[ebk][:], lhsT=pT[:, c, sub * 128:(sub + 1) * 128], rhs=pleW[:, c, half * 512:(half + 1) * 512],
                                                        start=(c == 0), stop=(c == 1))) for c in range(2)], r=[r_p, r_pleW], w=[psr[ebk]])
                        si = half
                        k.op("act", lambda e: e.activation(out=sig[si], in_=ps[gbk2][:], func=AF.Exp, scale=-1.0), r=[psr[gbk2]], w=[r_sig[si]])
                        k.op("act", lambda e: e.activation(out=sig[si], in_=sig[si], func=AF.Ln, bias=ones_f[:, 0:1]), r=[r_sig[si], r_const], w=[r_sig[si]])
                        k.op("act", lambda e: e.activation(out=sig[si], in_=sig[si], func=AF.Exp, scale=-1.0), r=[r_sig[si]], w=[r_sig[si]])
                        k.op("dve", lambda e: e.tensor_tensor(out=sig[si], in0=sig[si], in1=ps[ebk][:], op=ALU.mult), r=[r_sig[si], psr[ebk]], w=[r_sig[si]])
                        k.op("dve", lambda e: e.tensor_tensor(out=R[:, sub, half * 512:(half + 1) * 512], in0=R[:, sub, half * 512:(half + 1) * 512],
                                                              in1=sig[si], op=ALU.add), r=[r_sig[si], r_R[sub]], w=[r_R[sub]])

                def final_stage(sub):
                    k.op("dve", lambda e: e.memset(ms2[:, sub:sub + 1], 0.0), w=[r_ms2[sub]])
                    k.op("act", lambda e: e.activation(out=junkF, in_=R[:, sub, :], func=AF.Square, scale=1.0 / 32, accum_out=ms2[:, sub:sub + 1]),
                         r=[r_R[sub], r_ms2[sub]], w=[r_cvacc[1], r_ms2[sub]])
                    k.op("act", lambda e: e.activation(out=ms2[:, 4 + sub:5 + sub], in_=ms2[:, sub:sub + 1], func=AF.Ln, bias=epsc[:]), r=[r_ms2[sub], r_const], w=[r_ms2[sub]])
                    k.op("act", lambda e: e.activation(out=rstd2[:, sub:sub + 1], in_=ms2[:, 4 + sub:5 + sub], func=AF.Exp, scale=-0.5), r=[r_ms2[sub]], w=[r_ms2[sub]])
                    k.op("dve", lambda e: e.scalar_tensor_tensor(out=R[:, sub, :], in0=R[:, sub, :], scalar=rstd2[:, sub:sub + 1], in1=gfin[:], op0=ALU.mult, op1=ALU.mult),
                         r=[r_R[sub], r_ms2[sub], r_const], w=[r_R[sub]])
                    k.dma("pool", f"yout{sub}", y_d[seq, t0 + sub * 128:t0 + (sub + 1) * 128, :], R[:, sub, :], r=[r_R[sub]])

                ple_stage(0)
                ple_stage(1)
                final_stage(0)
                ple_stage(2)
                final_stage(1)
                ple_stage(3)
                for b_ in sbg:
                    slab_release(b_)
                final_stage(2)
                final_stage(3)

        k.finalize(final_wait_streams=("yout0", "yout1", "yout2", "yout3", "dbg"))
        print("instructions:", k.nins, "waits:", k.nwait, "counts:", k.cnt, "sim_us: %.1f" % k.sim_time)
    return nc


_NC_CACHE = {}


def _prep_shared(inp):
    f = lambda a: np.ascontiguousarray(np.asarray(a, dtype=np.float32))
    w_in = f(inp["w_in"][0])
    col = lambda g: f(g).reshape(8, 128).T
    gout = np.concatenate([f(inp["g_fox_out"][0]), f(inp["g_mlstm_out"][0])])
    sh = {
        "w_in": w_in,
        "w_out": f(inp["w_out"][0]),
        "w_up": f(inp["w_up"][0]),
        "w_down": f(inp["w_down"][0]),
        "w_ple": f(inp["w_ple"][0]),
        "w_pg": f(inp["w_ple_gate"][0]),
        "wgate": f(np.concatenate([w_in[:, 1536:1544], w_in[:, 3080:3088]], axis=1)),
        "gcol": f(np.concatenate([col(inp["g_mix"][0]), col(inp["g_mlp"][0]), col(inp["g_ple"][0]), col(gout)], axis=1)),
        "gfox": f(f(inp["g_fox_out"][0]).reshape(8, 64).T),
        "gfin": f(inp["g_final"]),
        "gbias": f(np.concatenate([inp["b_fox_f"][0], inp["b_mlstm_i"][0], inp["b_mlstm_f"][0]])),
        "wconv": f(f(inp["w_conv"][0]).reshape(4, 8, 128).transpose(2, 1, 0)),
        "gmixrow": f(f(inp["g_mix"][0]).reshape(1, D)),
        "wc3row": f(f(inp["w_conv"][0])[3].reshape(1, D)),
    }
    return sh


def run(inp, n_cores=8, dbg=None):
    x = np.asarray(inp["x"], dtype=np.float32)
    p = np.asarray(inp["p"], dtype=np.float32)[0]
    B, S, _ = x.shape
    NSEQ = B // n_cores
    key = (NSEQ, S, tuple(sorted(dbg.items())) if dbg else None)
    if key not in _NC_CACHE:
        _NC_CACHE[key] = build_nc(NSEQ, S, dbg)
    nc = _NC_CACHE[key]
    sh = _prep_shared(inp)
    in_maps = []
    for c in range(n_cores):
        m = dict(sh)
        m["x"] = np.ascontiguousarray(x[c * NSEQ:(c + 1) * NSEQ])
        m["p"] = np.ascontiguousarray(p[c * NSEQ:(c + 1) * NSEQ])
        in_maps.append(m)
    res = run_bass_kernel_spmd(nc, in_maps, core_ids=list(range(n_cores)))
    y = np.concatenate([r["y"] for r in res.results], axis=0)
    if dbg:
        return y, res.results
    return y


def kernel(**inputs):
    return run(inputs, 8).astype(np.float32)
```

```python
import heapq
import os
import sys
import types
import numpy as np
import concourse.bass as bass
import concourse.mybir as mybir
from concourse.bass_utils import run_bass_kernel_spmd
from contextlib import ExitStack

F32 = mybir.dt.float32
BF16 = mybir.dt.bfloat16
ALU = mybir.AluOpType
AF = mybir.ActivationFunctionType
AX = mybir.AxisListType

D = 1024
T = 512
NSUB = 4
FH = 8
MH = 4
INC = 3600
DFF = 4096
PLE = 256
EPS = 1e-6
NSLAB = 2
NPT = 3


class Res:
    __slots__ = ("name", "w", "r")

    def __init__(self, name):
        self.name = name
        self.w = None
        self.r = []


def _freeze(fn):
    if fn.__closure__ is None:
        return fn
    cells = tuple(types.CellType(c.cell_contents) for c in fn.__closure__)
    return types.FunctionType(fn.__code__, fn.__globals__, fn.__name__, fn.__defaults__, cells)


class _Probe:
    def __getattr__(self, name):
        def f(*a, **kw):
            out = kw.get("out", a[0] if a else None)
            return out, kw
        return f


def _free_elems(ap):
    n = 1
    for d in ap.shape[1:]:
        n *= int(d)
    return n


_DT_SIZE = {}


class KB:
    ENG = ("pe", "act", "dve", "pool", "sp")

    def __init__(self, nc, es):
        self.nc = nc
        self.es = es
        self.eng = {"pe": nc.tensor, "act": nc.scalar, "dve": nc.vector, "pool": nc.gpsimd, "sp": nc.sync}
        self.sem = {e: es.enter_context(nc.semaphore("s_" + e)) for e in self.ENG}
        self.ops = []
        self.dsem = {}

    def _deps(self, r, w):
        d = set()
        for x in r:
            if x.w is not None:
                d.add(x.w)
        for x in w:
            if x.w is not None:
                d.add(x.w)
            d.update(x.r)
        return d

    def _mark(self, oid, r, w):
        for x in r:
            x.r.append(oid)
        for x in w:
            x.w = oid
            x.r = []

    def _add(self, **kw):
        oid = len(self.ops)
        kw["id"] = oid
        kw["line"] = sys._getframe(2).f_lineno
        self.ops.append(kw)
        return oid

    def op(self, e, fn, r=(), w=(), f=None):
        if f is None:
            out, kw = fn(_Probe())
            f = _free_elems(out)
        base = {"act": 0.20, "dve": 0.12, "pool": 0.25}.get(e, 0.1)
        rate = {"act": 1400.0, "dve": 960.0, "pool": 500.0}.get(e, 1000.0)
        oid = self._add(eng=e, kind="op", fns=[_freeze(fn)], deps=self._deps(r, w), dur=base + f / rate, lat=0.0)
        self._mark(oid, r, w)
        return oid

    def pe(self, fns, r=(), w=(), n=None):
        dur = 0.0
        for fn in fns:
            out, kw = fn(_Probe())
            ni = _free_elems(out)
            passes = 4.0 if ("rhs" in kw and kw["rhs"].dtype == F32) else 1.0
            dur += 0.03 + passes * ni / 2200.0
        oid = self._add(eng="pe", kind="op", fns=[_freeze(fn) for fn in fns], deps=self._deps(r, w), dur=dur, lat=0.0)
        self._mark(oid, r, w)
        return oid

    def dma(self, q, stream, out, in_, r=(), w=(), nbytes=None):
        if stream not in self.dsem:
            self.dsem[stream] = self.es.enter_context(self.nc.semaphore("d_" + stream))
        if nbytes is None:
            nel = 1
            for d in out.shape:
                nel *= int(d)
            nbytes = nel * (2 if out.dtype == BF16 else 4)
        fn = lambda e, out=out, in_=in_: e.dma_start(out=out, in_=in_)
        oid = self._add(eng=q, kind="dma", fns=[fn], deps=self._deps(r, w), dur=(1.0 if q == "pool" else 0.06), lat=2.0 + nbytes / 300000.0, stream=stream)
        self._mark(oid, r, w)
        return oid

    def alias(self, old, new):
        ids = set()
        for o in old:
            if o.w is not None:
                ids.add(o.w)
            ids.update(o.r)
        for n in new:
            n.r = list(set(n.r) | ids)

    def finalize(self, final_wait_streams=()):
        ops = self.ops
        n = len(ops)
        succ = [[] for _ in range(n)]
        ndep = [0] * n
        for o in ops:
            o["deps"].discard(o["id"])
            ndep[o["id"]] = len(o["deps"])
            for d in o["deps"]:
                succ[d].append(o["id"])
        fin = [0.0] * n
        start = [0.0] * n
        efree = {e: 0.0 for e in self.ENG}
        ready_t = [0.0] * n
        heap = []
        for o in ops:
            if ndep[o["id"]] == 0:
                heapq.heappush(heap, (0.0, o["id"]))
        order = []
        while heap:
            est, oid = heapq.heappop(heap)
            o = ops[oid]
            real = max(ready_t[oid], efree[o["eng"]])
            if real > est + 1e-9:
                heapq.heappush(heap, (real, oid))
                continue
            start[oid] = real
            efree[o["eng"]] = real + o["dur"]
            fin[oid] = real + o["dur"] + o["lat"]
            order.append(oid)
            for s_ in succ[oid]:
                lat = (0.04 if o["eng"] == "pe" else 0.25) if ops[s_]["eng"] == o["eng"] else 0.18
                ready_t[s_] = max(ready_t[s_], fin[oid] + lat)
                ndep[s_] -= 1
                if ndep[s_] == 0:
                    heapq.heappush(heap, (max(ready_t[s_], efree[ops[s_]["eng"]]), s_))
        assert len(order) == n, (len(order), n)
        self.sim_time = max(fin) if n else 0.0
        if os.environ.get("KB_ANALYZE"):
            for eng_name in ("pe", "act", "dve"):
                prev_end = 0.0
                busy = 0.0
                blame = {}
                for oid in order:
                    o = ops[oid]
                    if o["eng"] != eng_name:
                        continue
                    gap = start[oid] - prev_end
                    if gap > 0.05 and o["deps"]:
                        d = max(o["deps"], key=lambda d_: fin[d_])
                        key = (ops[d]["eng"], ops[d]["line"], o["line"])
                        blame[key] = blame.get(key, 0.0) + gap
                    busy += o["dur"]
                    prev_end = start[oid] + o["dur"]
                print("ANALYZE %s busy %.0f us of %.0f (%.0f%%)" % (eng_name, busy, self.sim_time, 100 * busy / self.sim_time))
                for key, g in sorted(blame.items(), key=lambda kv: -kv[1])[:14]:
                    print("   idle %.0f us waiting for %s op@line %d (consumer line %d)" % (g, key[0], key[1], key[2]))
        cnt = {e: 0 for e in self.ENG}
        dcnt = {}
        tok = [None] * n
        know = {e: {} for e in self.ENG}
        opknow = [None] * n
        streams = {e: [] for e in self.ENG}
        nwait = 0
        for oid in order:
            o = ops[oid]
            e = o["eng"]
            need = {}
            for d in o["deps"]:
                key, val = tok[d]
                if key == "pe" and e == "pe" and ops[d]["kind"] == "op":
                    continue
                if key not in need or need[key][0] < val:
                    need[key] = (val, d)
            ke = know[e]
            for key, (val, d) in need.items():
                if ke.get(key, 0) >= val:
                    continue
                semh = self.sem[key] if key in self.sem else self.dsem[key[4:]]
                self.eng[e].wait_ge(semh, val)
                streams[e].append(("wait", key, val))
                nwait += 1
                for k2, v2 in opknow[d].items():
                    if ke.get(k2, 0) < v2:
                        ke[k2] = v2
            ins = None
            for fn in o["fns"]:
                ins = fn(self.eng[e])
            if o["kind"] == "dma":
                st = o["stream"]
                dcnt[st] = dcnt.get(st, 0) + 1
                ins.then_inc(self.dsem[st], 16)
                tok[oid] = ("dma:" + st, 16 * dcnt[st])
                streams[e].append(("inc", "dma:" + st, 16))
            else:
                cnt[e] += 1
                ins.then_inc(self.sem[e], 1)
                tok[oid] = (e, cnt[e])
                streams[e].append(("inc", e, 1))
            ok_ = dict(ke)
            tk, tv = tok[oid]
            if ok_.get(tk, 0) < tv:
                ok_[tk] = tv
            opknow[oid] = ok_
        for st in final_wait_streams:
            if st in dcnt:
                self.eng["pool"].wait_ge(self.dsem[st], 16 * dcnt[st])
                streams["pool"].append(("wait", "dma:" + st, 16 * dcnt[st]))
        semv = {}
        pos = {e: 0 for e in self.ENG}
        progress = True
        while progress:
            progress = False
            for e in self.ENG:
                st = streams[e]
                while pos[e] < len(st):
                    kind, key, val = st[pos[e]]
                    if kind == "wait":
                        if semv.get(key, 0) < val:
                            break
                    else:
                        semv[key] = semv.get(key, 0) + val
                    pos[e] += 1
                    progress = True
        for e in self.ENG:
            assert pos[e] == len(streams[e]), ("DEADLOCK in emitted program", e, pos[e], len(streams[e]), streams[e][pos[e]])
        self.cnt = cnt
        self.nwait = nwait
        self.nins = sum(len(o["fns"]) for o in ops)


def build_nc(NSEQ, S, dbg=None):
    nc = bass.Bass("TRN2", target_bir_lowering=False)
    NT = S // T
    NKT = S // 128
    di = lambda n, sh: nc.dram_tensor(n, sh, F32, kind="ExternalInput").ap()
    x_d = di("x", [NSEQ, S, D])
    p_d = di("p", [NSEQ, S, PLE])
    w_in_d = di("w_in", [D, INC])
    w_out_d = di("w_out", [D, D])
    w_up_d = di("w_up", [D, DFF])
    w_down_d = di("w_down", [DFF, D])
    w_ple_d = di("w_ple", [PLE, D])
    w_pg_d = di("w_pg", [D, D])
    wgate_d = di("wgate", [D, 16])
    gcol_d = di("gcol", [128, 32])
    gfox_d = di("gfox", [64, 8])
    gfin_d = di("gfin", [D])
    gbias_d = di("gbias", [16])
    wconv_d = di("wconv", [128, 8, 4])
    gmixrow_d = di("gmixrow", [1, D])
    wc3row_d = di("wc3row", [1, D])
    y_d = nc.dram_tensor("y", [NSEQ, S, D], F32, kind="ExternalOutput").ap()
    dbg_d = {}
    if dbg:
        for n, sh in dbg.items():
            dbg_d[n] = nc.dram_tensor("dbg_" + n, sh, F32, kind="ExternalOutput").ap()
    wb = lambda n, sh: nc.dram_tensor(n, sh, BF16, kind="Internal").ap()
    wb_in = wb("wb_in", [D, INC])
    wb_out = wb("wb_out", [D, D])
    wb_up = wb("wb_up", [D, DFF])
    wb_down = wb("wb_down", [DFF, D])
    wb_ple = wb("wb_ple", [PLE, D])
    wb_pg = wb("wb_pg", [D, D])

    es = ExitStack()
    with es:
        es.enter_context(nc.allow_low_precision("bf16 matmul operands / activations are intended (bf16-reference regime)"))
        k = KB(nc, es)
        SB = lambda n, sh, dt: es.enter_context(nc.sbuf_tensor("sb_" + n, sh, dt))
        Kc = SB("Kc", [128, FH, S], BF16)
        Vcf = SB("Vc", [128, NKT * FH * 65 + 64], BF16)
        Vc = Vcf[:, 0:NKT * FH * 65].rearrange("p (k h e) -> p k h e", k=NKT, h=FH)
        Cst = SB("Cst", [128, MH, 129], F32)
        Csb = SB("Csb", [128, MH, 129], BF16)
        BufA = SB("BufA", [128, 8192], BF16)
        hT = SB("hT", [128, 8, T], BF16)
        mixT = SB("mixT", [128, 8, T], BF16)
        slabs = [SB(f"slab{i}", [128, 8, 512], BF16) for i in range(NSLAB)]
        BufB = SB("BufB", [128, 8192], BF16)
        U2 = SB("U2", [128, 2560], F32)
        U3 = SB("U3", [128, 2080], F32)
        sgo = SB("sgo", [128, MH, T], BF16)
        pleW = SB("pleW", [128, 2, D], BF16)
        ginv = SB("ginv", [128, 4], F32)
        hml = SB("hml", [128, 3, NSUB, FH], BF16)
        ident_f = SB("ident_f", [128, 128], F32)
        ident_b = SB("ident_b", [128, 128], BF16)
        tri_f = SB("tri_f", [128, 128], F32)
        ones_f = SB("ones_f", [128, 128], F32)
        maskT = SB("maskT", [128, 128], BF16)
        Amat = SB("Amat", [128, 64], BF16)
        gfin = SB("gfin", [128, D], F32)
        gcol = SB("gcol", [128, 32], F32)
        gfox = SB("gfox", [64, 8], F32)
        gb = SB("gb", [128, 16], F32)
        wcv = SB("wcv", [128, 8, 4], F32)
        Wg = SB("Wg", [128, 8, 16], BF16)
        halo = SB("halo", [128, 8, 3], F32)
        tot = SB("tot", [1, 16], F32)
        Mt = SB("Mt", [4, 8], F32)
        DG = SB("DG", [4, 32], F32)
        A4 = SB("A4", [4, 4], F32)
        wdec = SB("wdec", [4, 4], F32)
        ms = SB("ms", [128, 8], F32)
        ms2 = SB("ms2", [128, 8], F32)
        h0col = SB("h0col", [128, 8], F32)
        s00 = SB("s00", [1, 4], F32)
        rstd2 = SB("rstd2", [128, 4], F32)
        sigb = SB("sigb", [128, 2, 512], F32)
        rstd = SB("rstd", [128, 8], F32)
        epsc = SB("epsc", [128, 1], F32)
        lnsc = SB("lnsc", [128, 1], F32)
        zg = SB("zg", [128, NSUB, 16], F32)
        e1 = SB("e1", [128, NSUB, 16], F32)
        lsp = SB("lsp", [128, NSUB, 16], F32)
        cpos = SB("cpos", [128, NSUB, 16], F32)
        r1t = SB("r1t", [128, 64], F32)
        alpha = SB("alpha", [128, NSUB, 4], F32)
        mbc = SB("mbc", [128, 32], F32)
        d1 = SB("d1", [128, NSUB, 4], F32)
        es_t = SB("es_t", [128, NSUB, 4], F32)
        clampv = SB("clampv", [128, NSUB, 4], F32)
        sm = SB("sm", [128, 2, 32], F32)
        ps = [es.enter_context(nc.psum_tensor(f"ps{i}", [128, 512], F32)) for i in range(8)]
        psr = [Res(f"ps{i}") for i in range(8)]
        psb = [ps[i][:].bitcast(BF16) for i in range(8)]

        def buf_views(X):
            return dict(
                R=X[:].bitcast(F32).rearrange("p (s d) -> p s d", s=NSUB),
                aT=X[:].rearrange("p (b c t) -> p b c t", b=2, c=8),
                QT=X[:, 0:4096].rearrange("p (h t) -> p h t", h=FH),
                qcT=X[:, 4096:6144].rearrange("p (h t) -> p h t", h=MH),
                kcT=X[:, 6144:8192].rearrange("p (h t) -> p h t", h=MH),
                rR=[Res(f"R{s_}") for s_ in range(NSUB)],
                rUa=[Res(f"U1a{h}") for h in range(FH)],
                rUb=[Res(f"U1b{h}") for h in range(8)],
            )

        bufs = [buf_views(BufA), buf_views(BufB)]
        R = bufs[0]["R"]
        r_R = bufs[0]["rR"]
        aT, QT, qcT, kcT = bufs[1]["aT"], bufs[1]["QT"], bufs[1]["qcT"], bufs[1]["kcT"]
        r_U1a, r_U1b = bufs[1]["rUa"], bufs[1]["rUb"]
        U2b = U2[:].bitcast(BF16)
        U3b = U3[:].bitcast(BF16)
        Pt = [U2b[:, i * 512:(i + 1) * 512] for i in range(NPT)]
        sqx = U2b[:, 1536:2048]
        lnr = U2[:, 1024:1536]
        cv_acc = [U2[:, 1536 + i * 512:1536 + (i + 1) * 512] for i in range(2)]
        junkF = U2b[:, 4096:5120]
        p32 = U2[:, 0:1024].rearrange("p (s d) -> p s d", s=NSUB)
        pbf = U2b[:, 2048:3072].rearrange("p (s d) -> p s d", s=NSUB)
        pT = U2b[:, 3072:4096].rearrange("p (c t) -> p c t", c=2)
        ktm = U3b[:, 0:2048].rearrange("p (s h d) -> p s h d", s=NSUB, h=MH)
        vx = U3b[:, 2048:2048 + NSUB * MH * 129].rearrange("p (s h d) -> p s h d", s=NSUB, h=MH)
        hn = U3b[:, 0:2048].rearrange("p (b d) -> p b d", b=2)
        EK = U3b[:, 2048:3072].rearrange("p (s h e) -> p s h e", s=NSUB, h=FH)
        EQ = U3b[:, 3072:4096].rearrange("p (s h e) -> p s h e", s=NSUB, h=FH)
        sig = [sigb[:, i, :] for i in range(2)]
        prod = [U3[:, 1024 + i * 512:1024 + (i + 1) * 512] for i in range(2)]
        Ssb = SB("Ssb", [128, 2, MH, 128], BF16)
        ybf = SB("ybf", [128, 2, MH, 128], BF16)
        print("sbuf bytes remaining:", nc.sbuf_bytes_remaining)

        r_Kc = [Res(f"Kc{h}") for h in range(FH)]
        r_Vc = Res("Vc")
        r_C = [Res(f"C{h}") for h in range(MH)]
        r_Csb = [Res(f"Csb{h}") for h in range(MH)]
        r_hT = [Res(f"hT{s}") for s in range(NSUB)]
        r_mix = [Res(f"mix{c}") for c in range(8)]
        r_slab = [Res(f"slab{i}") for i in range(NSLAB)]
        r_U2 = Res("U2")
        r_cvacc = [Res(f"cvacc{i}") for i in range(2)]
        r_Pt = [Res(f"Pt{i}") for i in range(NPT)]
        r_sqx = Res("sqx")
        r_lnr = Res("lnr")
        r_p = Res("pbufs")
        r_ktm = [Res(f"ktm{s}") for s in range(NSUB)]
        r_vx = [Res(f"vx{s}") for s in range(NSUB)]
        r_hn = [Res("hn0"), Res("hn1")]
        r_sig = [Res("sig0"), Res("sig1")]
        r_prod = [Res("prod0"), Res("prod1")]
        r_sgo = [Res(f"sgo{h}") for h in range(MH)]
        r_EK = Res("EK")
        r_EQ = Res("EQ")
        r_pleW = Res("pleW")
        r_s00 = Res("s00")
        r_h0 = Res("h0col")
        V_E = [r_EK, r_EQ]
        r_const = Res("const")
        r_halo = [Res(f"halo{c}") for c in range(8)]
        r_tot = Res("tot")
        r_Mt = Res("Mt")
        r_g = Res("gates")
        r_ms = Res("ms")
        r_msS = [Res(f"ms{i}") for i in range(NSUB)]
        r_ms2 = [Res(f"ms2_{i}") for i in range(NSUB)]
        r_sm = Res("sm")
        r_smj = [Res("smj0"), Res("smj1")]
        r_Ssb = [Res("Ssb0"), Res("Ssb1")]
        r_ybf = [Res("ybf0"), Res("ybf1")]
        r_w = {n: Res("w_" + n) for n in ["in", "out", "up", "down", "ple", "pg"]}
        V_att = r_Pt + [r_sqx, r_lnr]
        V_conv = r_cvacc
        V_ple2 = [r_p]
        V_hn = r_hn
        V_ml = r_ktm + r_vx
        V_ple3 = r_sig + r_prod

        bank_rr = {"mm": [0, [0, 1, 2]], "st": [0, [3, 4, 5]], "pv": [0, [6, 7]], "all": [0, [0, 1, 2, 3, 4, 5, 6, 7]]}

        def bank(cls):
            st = bank_rr[cls]
            b = st[1][st[0] % len(st[1])]
            st[0] += 1
            return b

        k.dma("pool", "xin", R[:], x_d[0, 0:T, :].rearrange("(j p) d -> p j d", p=128), w=r_R)
        k.dma("pool", "setup2", Wg[:], wgate_d.rearrange("(c p) n -> p c n", p=128), w=[r_const])
        k.dma("sp", "setup", gcol[:], gcol_d, w=[r_const])
        k.dma("sp", "setup", gfox[:], gfox_d, w=[r_const])
        k.dma("sp", "setup", gfin[:], gfin_d.partition_broadcast(128), w=[r_const])
        k.dma("sp", "setup", gb[:], gbias_d.partition_broadcast(128), w=[r_const])
        k.dma("sp", "setup", wcv[:], wconv_d, w=[r_const])
        k.op("dve", lambda e: e.memset(ident_f[:], 1.0), w=[r_const])
        k.op("pool", lambda e: e.affine_select(out=ident_f[:], in_=ident_f[:], pattern=[[-1, 128]], compare_op=ALU.is_equal,
                                               fill=0.0, base=0, channel_multiplier=1), r=[r_const], w=[r_const])
        k.op("dve", lambda e: e.memset(tri_f[:], 1.0), w=[r_const])
        k.op("pool", lambda e: e.affine_select(out=tri_f[:], in_=tri_f[:], pattern=[[1, 128]], compare_op=ALU.is_ge,
                                               fill=0.0, base=0, channel_multiplier=-1), r=[r_const], w=[r_const])
        k.op("dve", lambda e: e.memset(ones_f[:], 1.0), w=[r_const])
        k.op("dve", lambda e: e.tensor_copy(out=ident_b[:], in_=ident_f[:]), r=[r_const], w=[r_const])
        k.op("dve", lambda e: e.tensor_copy(out=maskT[:], in_=tri_f[:]), r=[r_const], w=[r_const])
        k.op("dve", lambda e: e.memset(Amat[0:64, :], 1.0 / 64), w=[r_const])
        k.op("dve", lambda e: e.memset(Amat[64:65, :], EPS), w=[r_const])
        k.op("dve", lambda e: e.memset(epsc[:], EPS), w=[r_const])
        k.op("dve", lambda e: e.memset(lnsc[:], float(-0.5 * np.log(128.0))), w=[r_const])
        k.op("dve", lambda e: e.reciprocal(out=ginv[:], in_=gcol[:, 28:32]), r=[r_const], w=[r_const])
        k.op("dve", lambda e: e.memset(Vcf[:], 0.0), w=[r_Vc])
        k.op("dve", lambda e: e.memset(Vc[:, :, :, 64:65], 1.0), r=[r_Vc], w=[r_Vc])

        slab_i = [0]

        r_wblk = {}

        def cast_blk(name, dst, src):
            r_wblk[name] = Res("wb_" + name)
            k.dma("pool", "wc_" + name, dst, src, w=[r_wblk[name]])

        for c0 in (0, 512, 1024, 1544, 2056, 2568, 3088):
            cast_blk(f"in{c0}", wb_in[:, c0:c0 + 512], w_in_d[:, c0:c0 + 512])
        for hf in range(2):
            cast_blk(f"out{hf}", wb_out[:, hf * 512:(hf + 1) * 512], w_out_d[:, hf * 512:(hf + 1) * 512])
        for qd in range(4):
            for sl in range(2):
                c0 = qd * 1024 + sl * 512
                cast_blk(f"up{c0}", wb_up[:, c0:c0 + 512], w_up_d[:, c0:c0 + 512])
            for hf in range(2):
                cast_blk(f"down{qd}_{hf}", wb_down[qd * 1024:(qd + 1) * 1024, hf * 512:(hf + 1) * 512], w_down_d[qd * 1024:(qd + 1) * 1024, hf * 512:(hf + 1) * 512])
        cast_blk("ple", wb_ple, w_ple_d)
        for hf in range(2):
            cast_blk(f"pg{hf}", wb_pg[:, hf * 512:(hf + 1) * 512], w_pg_d[:, hf * 512:(hf + 1) * 512])
        k.dma("sp", "setup3", pleW[:], wb_ple.rearrange("(c p) n -> p c n", p=128), r=[r_wblk["ple"]], w=[r_pleW])
        tile_specs = []
        for c0 in (0, 512, 1024, 1544, 2056, 2568, 3088):
            tile_specs.append((f"in{c0}", wb_in[:, c0:c0 + 512].rearrange("(c p) n -> p c n", p=128)))
        for hf in range(2):
            tile_specs.append((f"out{hf}", wb_out[:, hf * 512:(hf + 1) * 512].rearrange("(c p) n -> p c n", p=128)))
        def up_specs(qd):
            for sl in range(2):
                c0 = qd * 1024 + sl * 512
                tile_specs.append((f"up{c0}", wb_up[:, c0:c0 + 512].rearrange("(c p) n -> p c n", p=128)))

        def down_specs(qd):
            for hf in range(2):
                tile_specs.append((f"down{qd}_{hf}", wb_down[qd * 1024:(qd + 1) * 1024, hf * 512:(hf + 1) * 512].rearrange("(c p) n -> p c n", p=128)))

        up_specs(0)
        for qd in range(4):
            if qd + 1 < 4:
                up_specs(qd + 1)
            down_specs(qd)
        for hf in range(2):
            tile_specs.append((f"pg{hf}", wb_pg[:, hf * 512:(hf + 1) * 512].rearrange("(c p) n -> p c n", p=128)))
        slab_specs = tile_specs * (NSEQ * NT)
        slab_pos = [0]
        slab_ready = []
        slab_free = list(range(NSLAB))

        def slab_topup():
            while slab_free and slab_pos[0] < len(slab_specs):
                b = slab_free.pop(0)
                name, src = slab_specs[slab_pos[0]]
                slab_pos[0] += 1
                k.dma("sp", f"slab{b}", slabs[b][:], src, r=[r_wblk[name]], w=[r_slab[b]])
                slab_ready.append((b, name))

        def slab_acquire(name):
            slab_topup()
            b, nm = slab_ready.pop(0)
            assert nm == name, (nm, name)
            return b

        def slab_release(b):
            slab_free.append(b)
            slab_topup()

        def wslab(wap, c0, ncols=512):
            return wap[:, c0:c0 + ncols].rearrange("(c p) n -> p c n", p=128)

        dbg_done = set()

        def dump(name, ap, rlist):
            if name in dbg_d and name not in dbg_done:
                dbg_done.add(name)
                k.dma("pool", "dbg", dbg_d[name], ap, r=rlist)

        def norm_a(sub):
            b = sub % 2
            k.op("dve", lambda e: e.memset(ms[:, sub:sub + 1], 0.0), w=[r_msS[sub]])
            k.op("act", lambda e: e.activation(out=hn[:, b, :], in_=R[:, sub, :], func=AF.Square, scale=1.0 / 32, accum_out=ms[:, sub:sub + 1]),
                 r=[r_R[sub], r_msS[sub]], w=[r_hn[b], r_msS[sub]])
            k.op("act", lambda e: e.activation(out=ms[:, 4 + sub:5 + sub], in_=ms[:, sub:sub + 1], func=AF.Ln, bias=epsc[:]), r=[r_msS[sub], r_const], w=[r_msS[sub]])
            k.op("act", lambda e: e.activation(out=rstd[:, sub:sub + 1], in_=ms[:, 4 + sub:5 + sub], func=AF.Exp, scale=-0.5), r=[r_msS[sub]], w=[r_msS[sub]])
            k.op("dve", lambda e: e.tensor_scalar(out=hn[:, b, :], in0=R[:, sub, :], scalar1=rstd[:, sub:sub + 1], scalar2=None, op0=ALU.mult),
                 r=[r_R[sub], r_msS[sub]], w=[r_hn[b]])

        def norm_b(gidx, sub):
            b = sub % 2
            pb = bank("mm")
            k.pe([(lambda e, c=c: e.transpose(out=psb[pb][:, c * 128:(c + 1) * 128], in_=hn[:, b, c * 128:(c + 1) * 128], identity=ident_b[:]))
                  for c in range(8)], r=[r_hn[b], r_const], w=[psr[pb]])
            k.op("dve", lambda e: e.tensor_tensor(
                out=hT[:, :, sub * 128:(sub + 1) * 128], in0=psb[pb].rearrange("p (c t) -> p c t", c=8),
                in1=gcol[:, gidx * 8:(gidx + 1) * 8].unsqueeze(2).to_broadcast([128, 8, 128]), op=ALU.mult),
                 r=[psr[pb], r_const], w=[r_hT[sub]])

        def stage_then_norm(stage_fn, gidx, after_stage=None):
            stage_fn(0)
            stage_fn(1)
            norm_a(0)
            stage_fn(2)
            norm_a(1)
            norm_b(gidx, 0)
            stage_fn(3)
            if after_stage is not None:
                after_stage()
            norm_a(2)
            norm_b(gidx, 1)
            norm_a(3)
            norm_b(gidx, 2)
            norm_b(gidx, 3)


        def mm_feat(sb, c4, kchunks=8, rhs_fn=None):
            pb = bank("mm")
            k.pe([(lambda e, c=c, pb=pb: e.matmul(ps[pb][:], lhsT=slabs[sb][:, c, c4 * 128:(c4 + 1) * 128], rhs=hT[:, c, :],
                                                   start=(c == 0), stop=(c == kchunks - 1))) for c in range(kchunks)],
                 r=[r_slab[sb]] + r_hT, w=[psr[pb]])
            return pb

        def mm_tok(sb, sub, lhs, rl, kchunks=8, col0=0):
            pb = bank("mm")
            k.pe([(lambda e, c=c, pb=pb: e.matmul(ps[pb][:], lhsT=lhs[:, c, sub * 128:(sub + 1) * 128], rhs=slabs[sb][:, c, col0:col0 + 512],
                                                   start=(c == 0), stop=(c == kchunks - 1))) for c in range(kchunks)],
                 r=[r_slab[sb]] + rl, w=[psr[pb]])
            return pb

        def tok0_precise():
            rowA = U3[0:1, 0:1024]
            rowB = lnr[0:1, :]
            Wf = sigb[:].rearrange("p a (c n) -> p (a c) n", c=4)
            WS = V_hn + [r_lnr]
            for hf in range(2):
                k.dma("sp", "rows", rowB, gmixrow_d[0:1, hf * 512:(hf + 1) * 512], w=WS)
                k.op("dve", lambda e: e.scalar_tensor_tensor(out=rowA[:, hf * 512:(hf + 1) * 512], in0=R[0:1, 0, hf * 512:(hf + 1) * 512], scalar=rstd[0:1, 0:1], in1=rowB,
                                                             op0=ALU.mult, op1=ALU.mult), r=WS + [r_R[0], r_msS[0]], w=WS)
            hb_ = bank("mm")
            k.pe([(lambda e, c=c: e.matmul(ps[hb_][:, c:c + 1], lhsT=rowA[:, c * 128:(c + 1) * 128], rhs=ones_f[0:1, 0:1], start=True, stop=True)) for c in range(8)],
                 r=WS + [r_const], w=[psr[hb_]])
            k.op("dve", lambda e: e.tensor_copy(out=h0col[:], in_=ps[hb_][:, 0:8]), r=[psr[hb_]], w=[r_h0])
            zb = [bank("mm"), bank("mm")]
            for i in range(8):
                k.dma("sp", "wf", Wf, w_in_d[:, 1544 + i * 128:1544 + (i + 1) * 128].rearrange("(c p) n -> p c n", p=128), w=r_sig)
                k.pe([(lambda e, c=c: e.matmul(ps[zb[i // 4]][0:1, (i % 4) * 128:(i % 4 + 1) * 128], lhsT=h0col[:, c:c + 1], rhs=Wf[:, c, :], start=(c == 0), stop=(c == 7)))
                      for c in range(8)], r=r_sig + [r_h0], w=[psr[zb[i // 4]]])
            for hf in range(2):
                k.dma("sp", "rows", rowA[:, hf * 512:(hf + 1) * 512], wc3row_d[0:1, hf * 512:(hf + 1) * 512], w=WS)
                k.op("dve", lambda e: e.tensor_tensor(out=rowA[:, hf * 512:(hf + 1) * 512], in0=ps[zb[hf]][0:1, :], in1=rowA[:, hf * 512:(hf + 1) * 512], op=ALU.mult),
                     r=WS + [psr[zb[hf]]], w=WS)
                k.op("act", lambda e: e.activation(out=rowB, in_=rowA[:, hf * 512:(hf + 1) * 512], func=AF.Exp, scale=-1.0), r=WS, w=WS)
                k.op("act", lambda e: e.activation(out=rowB, in_=rowB, func=AF.Ln, bias=ones_f[0:1, 0:1]), r=WS + [r_const], w=WS)
                k.op("act", lambda e: e.activation(out=rowB, in_=rowB, func=AF.Exp, scale=-1.0), r=WS, w=WS)
                k.op("dve", lambda e: e.tensor_tensor(out=rowA[:, hf * 512:(hf + 1) * 512], in0=rowA[:, hf * 512:(hf + 1) * 512], in1=rowB, op=ALU.mult), r=WS, w=WS)
            k.op("dve", lambda e: e.tensor_tensor(out=rowB, in0=rowA[:, 0:512], in1=rowA[:, 512:1024], op=ALU.mult), r=WS, w=WS)
            k.op("dve", lambda e: e.tensor_reduce(out=s00[:], in_=rowB.rearrange("p (h d) -> p h d", h=MH), axis=AX.X, op=ALU.add), r=WS, w=[r_s00])

        for seq in range(NSEQ):
            for h in range(MH):
                k.op("pool", lambda e, h=h: e.memset(Cst[:, h, :], 0.0), w=[r_C[h]])
            k.op("pool", lambda e: e.memset(halo[:], 0.0), w=r_halo)
            k.op("pool", lambda e: e.memset(tot[:], 0.0), w=[r_tot])
            k.op("pool", lambda e: e.memset(Mt[:], 0.0), w=[r_Mt])
            for ti in range(NT):
                t0 = ti * T
                kt0 = t0 // 128
                nkt = kt0 + NSUB
                g_idx = seq * NT + ti
                bR, bU = bufs[g_idx % 2], bufs[(g_idx + 1) % 2]
                R, r_R = bR["R"], bR["rR"]
                aT, QT, qcT, kcT = bU["aT"], bU["QT"], bU["qcT"], bU["kcT"]
                r_U1a, r_U1b = bU["rUa"], bU["rUb"]
                k.alias(bU["rR"], r_U1a + r_U1b)
                k.alias(V_ple2, V_att + V_conv)
                k.alias(V_ml + V_hn, V_E)
                norm_a(0)
                norm_a(1)
                norm_b(0, 0)
                norm_a(2)
                norm_b(0, 1)
                norm_a(3)
                norm_b(0, 2)
                norm_b(0, 3)
                gbk = bank("mm")
                for sub in range(NSUB):
                    k.pe([(lambda e, c=c: e.matmul(ps[gbk][:, sub * 16:(sub + 1) * 16], lhsT=hT[:, c, sub * 128:(sub + 1) * 128], rhs=Wg[:, c, :],
                                                   start=(c == 0), stop=(c == 7))) for c in range(8)], r=[r_hT[sub], r_const], w=[psr[gbk]])
                k.op("dve", lambda e: e.tensor_tensor(out=zg[:], in0=ps[gbk][:, 0:64].rearrange("p (s g) -> p s g", s=NSUB),
                                                      in1=gb[:].unsqueeze(1).to_broadcast([128, NSUB, 16]), op=ALU.add), r=[psr[gbk], r_const], w=[r_g])
                k.op("act", lambda e: e.activation(out=e1[:], in_=zg[:], func=AF.Exp, scale=-1.0), r=[r_g], w=[r_g])
                k.op("act", lambda e: e.activation(out=lsp[:], in_=e1[:], func=AF.Ln, bias=ones_f[:, 0:1]), r=[r_g, r_const], w=[r_g])
                k.op("pool", lambda e: e.memset(EK[:], 0.0), w=[r_EK])
                k.op("pool", lambda e: e.memset(EQ[:], 0.0), w=[r_EQ])
                k.op("pool", lambda e: e.memset(EK[:, :, :, 0:3], 1.0), r=[r_EK], w=[r_EK])
                k.op("pool", lambda e: e.memset(EQ[:, :, :, 3:6], 1.0), r=[r_EQ], w=[r_EQ])
                sb = slab_acquire("in0")
                for j in range(4):
                    pb = mm_feat(sb, j)
                    k.op("act", lambda e: e.activation(out=QT[0:64, 2 * j, :], in_=ps[pb][0:64, :], func=AF.Copy, scale=0.125), r=[psr[pb]], w=[r_U1a[2 * j]])
                    k.op("dve", lambda e: e.tensor_scalar(out=QT[0:64, 2 * j + 1, :], in0=ps[pb][64:128, :], scalar1=0.125, scalar2=None, op0=ALU.mult),
                         r=[psr[pb]], w=[r_U1a[2 * j + 1]])
                slab_release(sb)
                cbk = bank("mm")
                for j in range(NSUB):
                    fns = [lambda e, j=j: e.matmul(ps[cbk][:, j * 16:(j + 1) * 16], lhsT=tri_f[:], rhs=lsp[:, j, :], start=True, stop=False)]
                    for i in range(j):
                        fns.append(lambda e, j=j, i=i: e.matmul(ps[cbk][:, j * 16:(j + 1) * 16], lhsT=ones_f[:], rhs=lsp[:, i, :], start=False, stop=False))
                    fns.append(lambda e, j=j: e.matmul(ps[cbk][:, j * 16:(j + 1) * 16], lhsT=ones_f[0:1, :], rhs=tot[0:1, :], start=False, stop=True))
                    k.pe(fns, r=[r_g, r_tot, r_const], w=[psr[cbk]])
                k.op("dve", lambda e: e.tensor_copy(out=cpos[:], in_=ps[cbk][:, 0:64].rearrange("p (s g) -> p s g", s=NSUB)), r=[psr[cbk]], w=[r_g])
                tbk = bank("mm")
                fns = [(lambda e, i=i: e.matmul(ps[tbk][0:1, 0:16], lhsT=ones_f[:, 0:1], rhs=lsp[:, i, :], start=(i == 0), stop=False)) for i in range(NSUB)]
                fns.append(lambda e: e.matmul(ps[tbk][0:1, 0:16], lhsT=ones_f[0:1, 0:1], rhs=tot[0:1, :], start=False, stop=True))
                k.pe(fns, r=[r_g, r_tot, r_const], w=[psr[tbk]])
                k.op("dve", lambda e: e.tensor_copy(out=tot[:], in_=ps[tbk][0:1, 0:16]), r=[psr[tbk]], w=[r_tot])
                sb = slab_acquire("in512")
                for j in range(4):
                    pb = mm_feat(sb, j)
                    k.op("act", lambda e: e.activation(out=Kc[0:64, 2 * j, t0:t0 + T], in_=ps[pb][0:64, :], func=AF.Copy), r=[psr[pb]], w=[r_Kc[2 * j]])
                    k.op("dve", lambda e: e.tensor_copy(out=Kc[0:64, 2 * j + 1, t0:t0 + T], in_=ps[pb][64:128, :]), r=[psr[pb]], w=[r_Kc[2 * j + 1]])
                slab_release(sb)
                cf = cpos[:, :, 0:8]
                r1v = r1t[:, 0:32].rearrange("p (s h) -> p s h", s=NSUB)
                r2v = r1t[:, 32:64].rearrange("p (s h) -> p s h", s=NSUB)
                k.op("dve", lambda e: e.tensor_copy(out=hml[:, 0], in_=cf), r=[r_g], w=[r_sm])
                k.op("dve", lambda e: e.tensor_tensor(out=r1v, in0=cf, in1=hml[:, 0], op=ALU.subtract), r=[r_g, r_sm], w=[r_sm])
                k.op("dve", lambda e: e.tensor_copy(out=hml[:, 1], in_=r1v), r=[r_sm], w=[r_sm])
                k.op("dve", lambda e: e.tensor_tensor(out=r2v, in0=r1v, in1=hml[:, 1], op=ALU.subtract), r=[r_sm], w=[r_sm])
                k.op("dve", lambda e: e.tensor_copy(out=hml[:, 2], in_=r2v), r=[r_sm], w=[r_sm])
                for i3 in range(3):
                    k.op("dve", lambda e: e.tensor_copy(out=EK[:, :, :, 3 + i3], in_=hml[:, i3]), r=[r_sm, r_EK], w=[r_EK])
                    k.op("dve", lambda e: e.tensor_scalar(out=EQ[:, :, :, i3], in0=hml[:, i3], scalar1=-1.0, scalar2=None, op0=ALU.mult), r=[r_sm, r_EQ], w=[r_EQ])
                k.op("dve", lambda e: e.tensor_tensor(out=alpha[:], in0=zg[:, :, 8:12], in1=cpos[:, :, 12:16], op=ALU.add), r=[r_g], w=[r_g])
                abk = bank("mm")
                k.pe([(lambda e, j=j: e.matmul(ps[abk][0:4, j * 128:(j + 1) * 128], lhsT=alpha[:, j, :], rhs=ident_f[:], start=True, stop=True)) for j in range(NSUB)],
                     r=[r_g, r_const], w=[psr[abk]])
                k.op("dve", lambda e: e.tensor_reduce(out=A4[:], in_=ps[abk][0:4, :].rearrange("p (s t) -> p s t", s=NSUB), axis=AX.X, op=ALU.max),
                     r=[psr[abk]], w=[r_Mt])
                for j in range(NSUB):
                    k.op("dve", lambda e: e.tensor_tensor(out=Mt[:, j + 1:j + 2], in0=Mt[:, j:j + 1], in1=A4[:, j:j + 1], op=ALU.max), r=[r_Mt], w=[r_Mt])
                k.op("dve", lambda e: e.tensor_tensor(out=wdec[:], in0=Mt[:, 0:4], in1=Mt[:, 1:5], op=ALU.subtract), r=[r_Mt], w=[r_Mt])
                k.op("act", lambda e: e.activation(out=wdec[:], in_=wdec[:], func=AF.Exp), r=[r_Mt], w=[r_Mt])
                k.op("dve", lambda e: e.tensor_tensor(out=DG[:, 0:16].rearrange("p (j h) -> p j h", j=NSUB), in0=Mt[:, 1:5].unsqueeze(2).to_broadcast([4, NSUB, 4]),
                                                      in1=ident_f[0:4, 0:4].unsqueeze(1).to_broadcast([4, NSUB, 4]), op=ALU.mult), r=[r_Mt, r_const], w=[r_Mt])
                k.op("dve", lambda e: e.tensor_tensor(out=DG[:, 16:32].rearrange("p (j h) -> p j h", j=NSUB), in0=wdec[:].unsqueeze(2).to_broadcast([4, NSUB, 4]),
                                                      in1=ident_f[0:4, 0:4].unsqueeze(1).to_broadcast([4, NSUB, 4]), op=ALU.mult), r=[r_Mt, r_const], w=[r_Mt])
                k.op("dve", lambda e: e.tensor_copy(out=Mt[:, 0:1], in_=Mt[:, 4:5]), r=[r_Mt], w=[r_Mt])
                bbk = bank("mm")
                k.pe([lambda e: e.matmul(ps[bbk][:, 0:32], lhsT=ones_f[0:4, :], rhs=DG[:], start=True, stop=True)], r=[r_Mt, r_const], w=[psr[bbk]])
                k.op("dve", lambda e: e.tensor_copy(out=mbc[:], in_=ps[bbk][:, 0:32]), r=[psr[bbk]], w=[r_g])
                mbM = mbc[:, 0:16].rearrange("p (j h) -> p j h", j=NSUB)
                k.op("dve", lambda e: e.tensor_tensor(out=d1[:], in0=alpha[:], in1=mbM, op=ALU.subtract), r=[r_g], w=[r_g])
                k.op("act", lambda e: e.activation(out=es_t[:], in_=d1[:], func=AF.Exp, bias=lnsc[:]), r=[r_g, r_const], w=[r_g])
                k.op("dve", lambda e: e.tensor_tensor(out=d1[:], in0=cpos[:, :, 12:16], in1=mbM, op=ALU.subtract), r=[r_g], w=[r_g])
                k.op("act", lambda e: e.activation(out=clampv[:], in_=d1[:], func=AF.Exp), r=[r_g], w=[r_g])
                sb = slab_acquire("in1024")
                for sub in range(NSUB):
                    pb = mm_tok(sb, sub, hT, [r_hT[sub]])
                    if sub % 2 == 0:
                        k.op("act", lambda e: e.activation(out=Vc[:, kt0 + sub, :, 0:64], in_=ps[pb][:].rearrange("p (h d) -> p h d", h=FH), func=AF.Copy), r=[psr[pb]], w=[r_Vc])
                    else:
                        k.op("dve", lambda e: e.tensor_copy(out=Vc[:, kt0 + sub, :, 0:64], in_=ps[pb][:].rearrange("p (h d) -> p h d", h=FH)), r=[psr[pb]], w=[r_Vc])
                slab_release(sb)
                exk = [bank("all") for _ in range(3)]
                exq = [bank("all") for _ in range(3)]
                hgroups = [(0, 3), (3, 3), (6, 2)]
                for EX, rEX, banks in ((EK, r_EK, exk), (EQ, r_EQ, exq)):
                    for gi, (h0, n) in enumerate(hgroups):
                        k.pe([(lambda e, sub=sub: e.matmul(ps[banks[gi]][0:n * 32, sub * 128:(sub + 1) * 128], lhsT=EX[:, sub, h0:h0 + n, :].rearrange("p h e -> p (h e)"),
                                                           rhs=ident_b[:], start=True, stop=True)) for sub in range(NSUB)], r=[rEX, r_const], w=[psr[banks[gi]]])
                for h in range(FH):
                    gi, row = h // 3, 32 * (h % 3)
                    k.op("dve", lambda e: e.tensor_copy(out=Kc[64:70, h, t0:t0 + T], in_=ps[exk[gi]][row:row + 6, :]), r=[psr[exk[gi]]], w=[r_Kc[h]])
                    k.op("act", lambda e: e.activation(out=QT[64:70, h, :], in_=ps[exq[gi]][row:row + 6, :], func=AF.Copy), r=[psr[exq[gi]]], w=[r_U1a[h]])

                if ti == 0:
                    tok0_precise()
                k.alias(V_hn, r_ktm)
                k.alias(V_E, r_vx)
                bg = []
                slab_of = {}

                def conv_a(which, col0, c4):
                    if c4 == 0:
                        slab_of[col0] = slab_acquire(f"in{col0}")
                    sbx = slab_of[col0]
                    ch = which * 4 + c4
                    pb = mm_feat(sbx, c4)
                    if c4 == 3:
                        slab_release(sbx)
                    cb = ch % 2
                    acc = cv_acc[cb]
                    k.op("dve", lambda e: e.tensor_scalar(out=acc, in0=ps[pb][:, 0:512], scalar1=wcv[:, ch, 3:4], scalar2=None, op0=ALU.mult),
                         r=[psr[pb], r_const], w=[r_cvacc[cb]])
                    for jj in (2, 1, 0):
                        sh = 3 - jj
                        k.op("dve", lambda e: e.scalar_tensor_tensor(out=acc[:, sh:512], in0=ps[pb][:, 0:512 - sh], scalar=wcv[:, ch, jj:jj + 1], in1=acc[:, sh:512],
                                                                     op0=ALU.mult, op1=ALU.add), r=[psr[pb], r_cvacc[cb], r_const], w=[r_cvacc[cb]])
                        k.op("dve", lambda e: e.scalar_tensor_tensor(out=acc[:, 0:sh], in0=halo[:, ch, 3 - sh:3], scalar=wcv[:, ch, jj:jj + 1], in1=acc[:, 0:sh],
                                                                     op0=ALU.mult, op1=ALU.add), r=[r_halo[ch], r_cvacc[cb], r_const], w=[r_cvacc[cb]])
                    k.op("dve", lambda e: e.tensor_copy(out=halo[:, ch, :], in_=ps[pb][:, 509:512]), r=[psr[pb]], w=[r_halo[ch]])

                def conv_b(which, dst, rr_off, c4):
                    ch = which * 4 + c4
                    cb = ch % 2
                    acc = cv_acc[cb]
                    etmp = Ssb[:, cb, :, :].rearrange("p h t -> p (h t)")
                    k.op("act", lambda e: e.activation(out=etmp, in_=acc, func=AF.Exp, scale=-1.0), r=[r_cvacc[cb]], w=[r_Ssb[cb]])
                    k.op("act", lambda e: e.activation(out=etmp, in_=etmp, func=AF.Ln, bias=ones_f[:, 0:1]), r=[r_Ssb[cb], r_const], w=[r_Ssb[cb]])
                    k.op("act", lambda e: e.activation(out=etmp, in_=etmp, func=AF.Exp, scale=-1.0), r=[r_Ssb[cb]], w=[r_Ssb[cb]])
                    k.op("dve", lambda e: e.tensor_tensor(out=dst[:, c4, :], in0=acc, in1=etmp, op=ALU.mult), r=[r_cvacc[cb], r_Ssb[cb]], w=[r_U1b[rr_off + c4]])

                def mv_unit(sub):
                    if sub == 0:
                        slab_of["mv"] = slab_acquire("in2568")
                    pb = mm_tok(slab_of["mv"], sub, hT, [r_hT[sub]])
                    if sub == NSUB - 1:
                        slab_release(slab_of["mv"])
                    k.op("pool", lambda e: e.memset(vx[:, sub, :, 128:129], 1.0), w=[r_vx[sub]])
                    k.op("dve", lambda e: e.tensor_copy(out=vx[:, sub, :, 0:128], in_=ps[pb][:].rearrange("p (h d) -> p h d", h=MH)), r=[psr[pb]], w=[r_vx[sub]])
                    k.op("dve", lambda e: e.tensor_tensor(out=vx[:, sub, :, :], in0=vx[:, sub, :, :], in1=es_t[:, sub, :].unsqueeze(2).to_broadcast([128, MH, 129]), op=ALU.mult),
                         r=[r_vx[sub], r_g], w=[r_vx[sub]])

                def mo_unit(c4):
                    if c4 == 0:
                        slab_of["mo"] = slab_acquire("in3088")
                    pb = mm_feat(slab_of["mo"], c4)
                    if c4 == 3:
                        slab_release(slab_of["mo"])
                    k.op("act", lambda e: e.activation(out=sgo[:, c4, :], in_=ps[pb][:], func=AF.Exp, scale=-1.0), r=[psr[pb]], w=[r_sgo[c4]])
                    k.op("act", lambda e: e.activation(out=sgo[:, c4, :], in_=sgo[:, c4, :], func=AF.Ln, bias=ones_f[:, 0:1]), r=[r_sgo[c4], r_const], w=[r_sgo[c4]])
                    k.op("act", lambda e: e.activation(out=sgo[:, c4, :], in_=sgo[:, c4, :], func=AF.Exp, scale=-1.0), r=[r_sgo[c4]], w=[r_sgo[c4]])
                    k.op("dve", lambda e: e.tensor_scalar(out=sgo[:, c4, :], in0=sgo[:, c4, :], scalar1=gcol[:, 28 + c4:29 + c4], scalar2=None, op0=ALU.mult),
                         r=[r_sgo[c4], r_const], w=[r_sgo[c4]])

                ua, ub_ = [], []
                for which, col0, dst, rr_off in ((0, 1544, qcT, 0), (1, 2056, kcT, 4)):
                    for c4 in range(4):
                        ua.append(lambda which=which, col0=col0, c4=c4: conv_a(which, col0, c4))
                        ub_.append(lambda which=which, dst=dst, rr_off=rr_off, c4=c4: conv_b(which, dst, rr_off, c4))
                bg.append(ua[0])
                for n in range(1, 8):
                    bg.append(ua[n])
                    bg.append(ub_[n - 1])
                bg.append(ub_[7])
                for sub in range(NSUB):
                    bg.append(lambda sub=sub: mv_unit(sub))
                post_units = [lambda c4=c4: mo_unit(c4) for c4 in range(4)]


                pairs = [(h, kt) for h in range(FH) for kt in range(nkt)]
                sbank = {}
                pvbank = {}

                def lo_of(kt):
                    return max(0, (kt - kt0) * 128)

                def emit_S(idx):
                    h, kt = pairs[idx]
                    b = bank("st")
                    sbank[idx] = b
                    lo = lo_of(kt)
                    k.pe([lambda e: e.matmul(ps[b][:, lo:T], lhsT=Kc[0:70, h, kt * 128:(kt + 1) * 128], rhs=QT[0:70, h, lo:T], start=True, stop=True)],
                         r=[r_Kc[h], r_U1a[h]], w=[psr[b]])

                def epilogue(h):
                    pvb = pvbank[h]
                    k.op("act", lambda e: e.activation(out=sqx[0:65, :], in_=ps[pvb][0:65, :], func=AF.Square), r=[psr[pvb]], w=[r_sqx])
                    rb = bank("mm")
                    k.pe([lambda e: e.matmul(ps[rb][0:64, :], lhsT=Amat[0:65, :], rhs=sqx[0:65, :], start=True, stop=True)], r=[r_sqx, r_const], w=[psr[rb]])
                    k.op("act", lambda e: e.activation(out=lnr[0:64, :], in_=ps[rb][0:64, :], func=AF.Ln), r=[psr[rb]], w=[r_lnr])
                    k.op("act", lambda e: e.activation(out=lnr[0:64, :], in_=lnr[0:64, :], func=AF.Exp, scale=-0.5), r=[r_lnr], w=[r_lnr])
                    po = (h % 2) * 64
                    k.op("dve", lambda e: e.scalar_tensor_tensor(out=mixT[po:po + 64, h // 2, :], in0=ps[pvb][0:64, :], scalar=gfox[0:64, h:h + 1], in1=lnr[0:64, :],
                                                                 op0=ALU.mult, op1=ALU.mult), r=[psr[pvb], r_lnr, r_const], w=[r_mix[h // 2]])

                def attn_range(p0, p1, with_bg):
                    for i_ in range(p0, min(p0 + 2, p1)):
                        emit_S(i_)
                    for idx in range(p0, p1):
                        h, kt = pairs[idx]
                        if idx + 2 < p1:
                            emit_S(idx + 2)
                        b = sbank[idx]
                        lo = lo_of(kt)
                        pi = idx % NPT
                        k.op("act", lambda e: e.activation(out=Pt[pi][:, lo:T], in_=ps[b][:, lo:T], func=AF.Exp), r=[psr[b]], w=[r_Pt[pi]])
                        if kt >= kt0:
                            k.op("dve", lambda e: e.tensor_tensor(out=Pt[pi][:, lo:lo + 128], in0=Pt[pi][:, lo:lo + 128], in1=maskT[:], op=ALU.mult),
                                 r=[r_Pt[pi], r_const], w=[r_Pt[pi]])
                        if kt == 0:
                            pvbank[h] = bank("pv")
                        pvb = pvbank[h]
                        vo = (kt * FH + h) * 65
                        k.pe([lambda e: e.matmul(ps[pvb][:, lo:T], lhsT=Vcf[:, vo:vo + 128], rhs=Pt[pi][:, lo:T], start=(kt == 0), stop=(kt == nkt - 1))],
                             r=[r_Pt[pi], r_Vc], w=[psr[pvb]])
                        if kt == nkt - 1:
                            epilogue(h)
                        if with_bg and bg and (idx + 1) % bg_stride == 0:
                            bg.pop(0)()

                p_mid = (FH // 2) * nkt
                bg_stride = max(1, p_mid // (len(bg) + 2))
                attn_range(0, p_mid, True)
                while bg:
                    bg.pop(0)()
                for u in post_units:
                    u()
                st1 = {}
                st2 = {}

                def ml_stage1(j):
                    js = slice(j * 128, (j + 1) * 128)
                    jb = j % 2
                    ktb = 0
                    k.pe([(lambda e, h=h: e.transpose(out=psb[ktb][:, h * 128:(h + 1) * 128], in_=kcT[:, h, js], identity=ident_b[:])) for h in range(MH)],
                         r=r_U1b[4:8] + [r_const], w=[psr[ktb]])
                    k.op("dve", lambda e: e.tensor_copy(out=ktm[:, j, :, :], in_=psb[ktb][:, 0:512].rearrange("p (h d) -> p h d", h=MH)), r=[psr[ktb]], w=[r_ktm[j]])
                    stb = 1
                    k.pe([(lambda e, h=h: e.matmul(ps[stb][:, h * 128:(h + 1) * 128], lhsT=kcT[:, h, js], rhs=qcT[:, h, js], start=True, stop=True)) for h in range(MH)],
                         r=r_U1b, w=[psr[stb]])
                    k.op("dve", lambda e: e.tensor_tensor(out=Ssb[:, jb, :, :], in0=ps[stb][:].rearrange("p (h t) -> p h t", h=MH),
                                                          in1=maskT[:].unsqueeze(1).to_broadcast([128, MH, 128]), op=ALU.mult), r=[psr[stb], r_const], w=[r_Ssb[jb]])
                    if ti == 0 and j == 0:
                        k.op("dve", lambda e: e.tensor_copy(out=Ssb[0:1, jb, :, 0:1], in_=s00[:].unsqueeze(2)), r=[r_s00, r_Ssb[jb]], w=[r_Ssb[jb]])
                    ub = [2, 1]
                    k.pe([(lambda e, h=h: e.matmul(ps[2][:, h * 129:(h + 1) * 129], lhsT=ktm[:, j, h, :], rhs=vx[:, j, h, :], start=True, stop=True)) for h in range(3)],
                         r=[r_ktm[j], r_vx[j]], w=[psr[2]])
                    k.pe([lambda e: e.matmul(ps[1][:, 0:129], lhsT=ktm[:, j, 3, :], rhs=vx[:, j, 3, :], start=True, stop=True)], r=[r_ktm[j], r_vx[j]], w=[psr[1]])
                    st1[j] = ub

                def ml_stage2(j):
                    js = slice(j * 128, (j + 1) * 128)
                    jb = j % 2
                    ub = st1[j]
                    wbc = mbc[:, 16 + 4 * j:20 + 4 * j].unsqueeze(2).to_broadcast([128, MH, 129])
                    k.op("dve", lambda e: e.tensor_tensor(out=Cst[:], in0=Cst[:], in1=wbc, op=ALU.mult), r=r_C + [r_g], w=r_C)
                    k.op("dve", lambda e: e.tensor_copy(out=Csb[:], in_=Cst[:]), r=r_C, w=r_Csb)
                    k.op("dve", lambda e: e.tensor_tensor(out=Cst[:, 0:3, :], in0=Cst[:, 0:3, :], in1=ps[2][:, 0:387].rearrange("p (h d) -> p h d", h=3), op=ALU.add),
                         r=r_C + [psr[2]], w=r_C)
                    k.op("dve", lambda e: e.tensor_tensor(out=Cst[:, 3, :], in0=Cst[:, 3, :], in1=ps[1][:, 0:129], op=ALU.add), r=r_C + [psr[1]], w=r_C)
                    nb = [0, 1]
                    for hb in range(2):
                        fns = []
                        for hh in range(2):
                            h = 2 * hb + hh
                            fns.append(lambda e, h=h, hh=hh: e.matmul(ps[nb[hb]][:, hh * 129:(hh + 1) * 129], lhsT=Ssb[:, jb, h, :], rhs=vx[:, j, h, :], start=True, stop=False))
                            fns.append(lambda e, h=h, hh=hh: e.matmul(ps[nb[hb]][:, hh * 129:(hh + 1) * 129], lhsT=qcT[:, h, js], rhs=Csb[:, h, :], start=False, stop=True))
                        k.pe(fns, r=[r_Ssb[jb], r_vx[j]] + r_U1b[0:4] + r_Csb, w=[psr[nb[hb]]])
                    st2[j] = nb

                def ml_stage3a(j):
                    jb = j % 2
                    nb = st2[j]
                    smj = sm[:, jb, :]
                    k.op("dve", lambda e: e.memset(smj[:, 0:4], 0.0), w=[r_smj[jb]])
                    for h in range(MH):
                        hb, hh = h // 2, h % 2
                        k.op("act", lambda e: e.activation(out=ybf[:, jb, h, :], in_=ps[nb[hb]][:, hh * 129:hh * 129 + 128], func=AF.Square, scale=float(128 ** -0.5),
                                                           accum_out=smj[:, h:h + 1]), r=[psr[nb[hb]], r_smj[jb]], w=[r_ybf[jb], r_smj[jb]])
                    for hb in range(2):
                        k.op("dve", lambda e: e.tensor_copy(out=smj[:, 4 + 2 * hb:6 + 2 * hb], in_=ps[nb[hb]][:, 0:258].rearrange("p (h d) -> p h d", h=2)[:, :, 128]),
                             r=[psr[nb[hb]], r_smj[jb]], w=[r_smj[jb]])
                    k.op("dve", lambda e: e.scalar_tensor_tensor(out=smj[:, 8:12], in0=smj[:, 4:8], scalar=-1.0, in1=smj[:, 4:8], op0=ALU.mult, op1=ALU.max), r=[r_smj[jb]], w=[r_smj[jb]])
                    k.op("dve", lambda e: e.tensor_tensor(out=smj[:, 12:16], in0=smj[:, 8:12], in1=clampv[:, j, :], op=ALU.max), r=[r_smj[jb], r_g], w=[r_smj[jb]])
                    k.op("dve", lambda e: e.scalar_tensor_tensor(out=smj[:, 16:20], in0=smj[:, 12:16], scalar=EPS, in1=smj[:, 12:16], op0=ALU.mult, op1=ALU.mult), r=[r_smj[jb]], w=[r_smj[jb]])
                    k.op("dve", lambda e: e.tensor_tensor(out=smj[:, 20:24], in0=smj[:, 16:20], in1=smj[:, 0:4], op=ALU.add), r=[r_smj[jb]], w=[r_smj[jb]])
                    k.op("act", lambda e: e.activation(out=smj[:, 24:28], in_=smj[:, 20:24], func=AF.Ln), r=[r_smj[jb]], w=[r_smj[jb]])
                    k.op("act", lambda e: e.activation(out=smj[:, 28:32], in_=smj[:, 24:28], func=AF.Exp, scale=-0.5), r=[r_smj[jb]], w=[r_smj[jb]])
                    for hb in range(2):
                        k.op("dve", lambda e: e.tensor_tensor(out=ybf[:, jb, 2 * hb:2 * hb + 2, :], in0=ps[nb[hb]][:, 0:258].rearrange("p (h d) -> p h d", h=2)[:, :, 0:128],
                                                              in1=smj[:, 28 + 2 * hb:30 + 2 * hb].unsqueeze(2).to_broadcast([128, 2, 128]), op=ALU.mult),
                             r=[psr[nb[hb]], r_smj[jb]], w=[r_ybf[jb]])

                def ml_stage3b(j):
                    js = slice(j * 128, (j + 1) * 128)
                    jb = j % 2
                    yb = 2
                    k.pe([(lambda e, h=h: e.transpose(out=psb[yb][:, h * 128:(h + 1) * 128], in_=ybf[:, jb, h, :], identity=ident_b[:])) for h in range(MH)],
                         r=[r_ybf[jb], r_const], w=[psr[yb]])
                    k.op("dve", lambda e: e.tensor_tensor(out=mixT[:, 4:8, js], in0=psb[yb][:, 0:512].rearrange("p (h t) -> p h t", h=MH), in1=sgo[:, :, js], op=ALU.mult),
                         r=[psr[yb]] + r_sgo, w=r_mix[4:8])

                for j in range(NSUB):
                    ml_stage1(j)
                    ml_stage2(j)
                    ml_stage3a(j)
                    ml_stage3b(j)
                dump("mixml", mixT[:, 4:8, :], r_mix[4:8])
                attn_range(p_mid, len(pairs), False)
                dump("mixfox", mixT[:, 0:4, :], r_mix[0:4])


                k.alias(V_ml, V_hn)
                sbo = [slab_acquire(f"out{half}") for half in range(2)]

                def wout_stage(sub):
                    for half in range(2):
                        pb = mm_tok(sbo[half], sub, mixT, r_mix)
                        k.op("dve", lambda e: e.tensor_tensor(out=R[:, sub, half * 512:(half + 1) * 512], in0=R[:, sub, half * 512:(half + 1) * 512],
                                                              in1=ps[pb][:], op=ALU.add), r=[psr[pb], r_R[sub]], w=[r_R[sub]])

                stage_then_norm(wout_stage, 1, after_stage=lambda: [slab_release(b) for b in sbo])
                dump("x1", R[:], r_R)
                def up_stage(qd):
                    ab = qd % 2
                    r_aT = r_U1a if ab == 0 else r_U1b
                    for sl in range(2):
                        sb = slab_acquire(f"up{qd * 1024 + sl * 512}")
                        for c4 in range(4):
                            pb = mm_feat(sb, c4)
                            if c4 == 3:
                                slab_release(sb)
                            kk = sl * 4 + c4
                            k.op("act", lambda e: e.activation(out=aT[:, ab, kk, :], in_=ps[pb][:], func=AF.Relu), r=[psr[pb]], w=[r_aT[kk]])
                            k.op("dve", lambda e: e.tensor_tensor(out=aT[:, ab, kk, :], in0=aT[:, ab, kk, :], in1=aT[:, ab, kk, :], op=ALU.mult), r=[r_aT[kk]], w=[r_aT[kk]])

                def down_one(qd, sbx, sub, half):
                    ab = qd % 2
                    r_aT = r_U1a if ab == 0 else r_U1b
                    pb = mm_tok(sbx, sub, aT[:, ab], r_aT)
                    k.op("dve", lambda e: e.tensor_tensor(out=R[:, sub, half * 512:(half + 1) * 512], in0=R[:, sub, half * 512:(half + 1) * 512],
                                                          in1=ps[pb][:], op=ALU.add), r=[psr[pb], r_R[sub]], w=[r_R[sub]])

                up_stage(0)
                for qd in range(4):
                    if qd + 1 < 4:
                        up_stage(qd + 1)
                    if qd < 3:
                        for half in range(2):
                            sbx = slab_acquire(f"down{qd}_{half}")
                            for sub in range(NSUB):
                                down_one(qd, sbx, sub, half)
                            slab_release(sbx)
                    else:
                        sbd = [slab_acquire(f"down{qd}_{half}") for half in range(2)]

                        def down_stage(sub):
                            for half in range(2):
                                down_one(3, sbd[half], sub, half)

                        stage_then_norm(down_stage, 2, after_stage=lambda: [slab_release(b) for b in sbd])
                dump("x2", R[:], r_R)
                if g_idx + 1 < NSEQ * NT:
                    nseq, nti = divmod(g_idx + 1, NT)
                    k.alias(r_U1a + r_U1b, bU["rR"])
                    k.dma("pool", "xin", bU["R"], x_d[nseq, nti * T:(nti + 1) * T, :].rearrange("(j p) d -> p j d", p=128), w=bU["rR"])
                k.alias(V_conv + V_att, V_ple2)
                k.dma("pool", "pin", p32, p_d[seq, t0:t0 + T, :].rearrange("(j p) d -> p j d", p=128), w=[r_p])
                k.op("dve", lambda e: e.tensor_copy(out=pbf, in_=p32), r=[r_p], w=[r_p])
                for sub in range(NSUB):
                    pb = bank("mm")
                    k.pe([(lambda e, c=c: e.transpose(out=psb[pb][:, c * 128:(c + 1) * 128], in_=pbf[:, sub, c * 128:(c + 1) * 128], identity=ident_b[:])) for c in range(2)],
                         r=[r_p, r_const], w=[psr[pb]])
                    k.op("dve", lambda e: e.tensor_copy(out=pT[:, :, sub * 128:(sub + 1) * 128], in_=psb[pb][:, 0:256].rearrange("p (c t) -> p c t", c=2)),
                         r=[psr[pb]], w=[r_p])
                sbg = [slab_acquire(f"pg{half}") for half in range(2)]

                def ple_stage(sub):
                    for half in range(2):
                        gbk2 = mm_tok(sbg[half], sub, hT, [r_hT[sub]])
                        ebk = bank("mm")
                        k.pe([(lambda e, c=c: e.matmul(ps[ebk][:], lhsT=pT[:, c, sub * 128:(sub + 1) * 128], rhs=pleW[:, c, half * 512:(half + 1) * 512],
                                                        start=(c == 0), stop=(c == 1))) for c in range(2)], r=[r_p, r_pleW], w=[psr[ebk]])
                        si = half
                        k.op("act", lambda e: e.activation(out=sig[si], in_=ps[gbk2][:], func=AF.Exp, scale=-1.0), r=[psr[gbk2]], w=[r_sig[si]])
                        k.op("act", lambda e: e.activation(out=sig[si], in_=sig[si], func=AF.Ln, bias=ones_f[:, 0:1]), r=[r_sig[si], r_const], w=[r_sig[si]])
                        k.op("act", lambda e: e.activation(out=sig[si], in_=sig[si], func=AF.Exp, scale=-1.0), r=[r_sig[si]], w=[r_sig[si]])
                        k.op("dve", lambda e: e.tensor_tensor(out=sig[si], in0=sig[si], in1=ps[ebk][:], op=ALU.mult), r=[r_sig[si], psr[ebk]], w=[r_sig[si]])
                        k.op("dve", lambda e: e.tensor_tensor(out=R[:, sub, half * 512:(half + 1) * 512], in0=R[:, sub, half * 512:(half + 1) * 512],
                                                              in1=sig[si], op=ALU.add), r=[r_sig[si], r_R[sub]], w=[r_R[sub]])

                def final_stage(sub):
                    k.op("dve", lambda e: e.memset(ms2[:, sub:sub + 1], 0.0), w=[r_ms2[sub]])
                    k.op("act", lambda e: e.activation(out=junkF, in_=R[:, sub, :], func=AF.Square, scale=1.0 / 32, accum_out=ms2[:, sub:sub + 1]),
                         r=[r_R[sub], r_ms2[sub]], w=[r_cvacc[1], r_ms2[sub]])
                    k.op("act", lambda e: e.activation(out=ms2[:, 4 + sub:5 + sub], in_=ms2[:, sub:sub + 1], func=AF.Ln, bias=epsc[:]), r=[r_ms2[sub], r_const], w=[r_ms2[sub]])
                    k.op("act", lambda e: e.activation(out=rstd2[:, sub:sub + 1], in_=ms2[:, 4 + sub:5 + sub], func=AF.Exp, scale=-0.5), r=[r_ms2[sub]], w=[r_ms2[sub]])
                    k.op("dve", lambda e: e.scalar_tensor_tensor(out=R[:, sub, :], in0=R[:, sub, :], scalar=rstd2[:, sub:sub + 1], in1=gfin[:], op0=ALU.mult, op1=ALU.mult),
                         r=[r_R[sub], r_ms2[sub], r_const], w=[r_R[sub]])
                    k.dma("pool", f"yout{sub}", y_d[seq, t0 + sub * 128:t0 + (sub + 1) * 128, :], R[:, sub, :], r=[r_R[sub]])

                ple_stage(0)
                ple_stage(1)
                final_stage(0)
                ple_stage(2)
                final_stage(1)
                ple_stage(3)
                for b_ in sbg:
                    slab_release(b_)
                final_stage(2)
                final_stage(3)

        k.finalize(final_wait_streams=("yout0", "yout1", "yout2", "yout3", "dbg"))
        print("instructions:", k.nins, "waits:", k.nwait, "counts:", k.cnt, "sim_us: %.1f" % k.sim_time)
    return nc


_NC_CACHE = {}


def _prep_shared(inp):
    f = lambda a: np.ascontiguousarray(np.asarray(a, dtype=np.float32))
    w_in = f(inp["w_in"][0])
    col = lambda g: f(g).reshape(8, 128).T
    gout = np.concatenate([f(inp["g_fox_out"][0]), f(inp["g_mlstm_out"][0])])
    sh = {
        "w_in": w_in,
        "w_out": f(inp["w_out"][0]),
        "w_up": f(inp["w_up"][0]),
        "w_down": f(inp["w_down"][0]),
        "w_ple": f(inp["w_ple"][0]),
        "w_pg": f(inp["w_ple_gate"][0]),
        "wgate": f(np.concatenate([w_in[:, 1536:1544], w_in[:, 3080:3088]], axis=1)),
        "gcol": f(np.concatenate([col(inp["g_mix"][0]), col(inp["g_mlp"][0]), col(inp["g_ple"][0]), col(gout)], axis=1)),
        "gfox": f(f(inp["g_fox_out"][0]).reshape(8, 64).T),
        "gfin": f(inp["g_final"]),
        "gbias": f(np.concatenate([inp["b_fox_f"][0], inp["b_mlstm_i"][0], inp["b_mlstm_f"][0]])),
        "wconv": f(f(inp["w_conv"][0]).reshape(4, 8, 128).transpose(2, 1, 0)),
        "gmixrow": f(f(inp["g_mix"][0]).reshape(1, D)),
        "wc3row": f(f(inp["w_conv"][0])[3].reshape(1, D)),
    }
    return sh


def run(inp, n_cores=8, dbg=None):
    x = np.asarray(inp["x"], dtype=np.float32)
    p = np.asarray(inp["p"], dtype=np.float32)[0]
    B, S, _ = x.shape
    NSEQ = B // n_cores
    key = (NSEQ, S, tuple(sorted(dbg.items())) if dbg else None)
    if key not in _NC_CACHE:
        _NC_CACHE[key] = build_nc(NSEQ, S, dbg)
    nc = _NC_CACHE[key]
    sh = _prep_shared(inp)
    in_maps = []
    for c in range(n_cores):
        m = dict(sh)
        m["x"] = np.ascontiguousarray(x[c * NSEQ:(c + 1) * NSEQ])
        m["p"] = np.ascontiguousarray(p[c * NSEQ:(c + 1) * NSEQ])
        in_maps.append(m)
    res = run_bass_kernel_spmd(nc, in_maps, core_ids=list(range(n_cores)))
    y = np.concatenate([r["y"] for r in res.results], axis=0)
    if dbg:
        return y, res.results
    return y


def kernel(**inputs):
    return run(inputs, 8).astype(np.float32)
```
